# Optimizing a Trainium2 kernel written in Bass

```python
import math
import jax, jax.numpy as jnp
from jax import lax
import numpy as np

D_MODEL = 1024
BATCH = 32
SEQ = 2048
DEPTH = 4
DEC_BATCH = 1
DEC_SEQ = 16384
PAST_LEN = 128

A_HEADS = 8
A_HEAD_DIM = 64
A_WIDTH = A_HEADS * A_HEAD_DIM
A_ROT_DIM = A_HEAD_DIM // 4
ATT_THETA = 500000.0
DILATED = ((128, 1), (512, 4), (2048, 16))
R_HEADS = 4
R_KEY_DIM = 128
R_VAL_DIM = 256
R_QK_WIDTH = R_HEADS * R_KEY_DIM
R_V_WIDTH = R_HEADS * R_VAL_DIM
RET_THETA = 10000.0
RET_CHUNK = 128
IN_SIZES = (A_WIDTH, A_WIDTH, A_WIDTH, A_WIDTH, R_QK_WIDTH, R_QK_WIDTH, R_V_WIDTH, R_V_WIDTH, D_MODEL, D_MODEL)
IN_WIDTH = sum(IN_SIZES)
IN_SPLIT_IDX = [int(v) for v in np.cumsum(IN_SIZES)[:-1]]
EPS = 1e-6
NEG = -1e30

kernel_name = "hybrid_dilated_attn_retention_encoder"


def rms_norm(x, g):
    xf = x.astype(jnp.float32)
    y = xf * lax.rsqrt(jnp.mean(xf * xf, axis=-1, keepdims=True) + EPS) * g.astype(jnp.float32)
    return y.astype(x.dtype)


def rope(x, pos, theta, n_rot):
    half = n_rot // 2
    freqs = jnp.exp(-math.log(theta) * jnp.arange(half, dtype=jnp.float32) / half)
    ang = pos[:, None] * freqs[None, :]
    cos = jnp.cos(ang)[:, None, :]
    sin = jnp.sin(ang)[:, None, :]
    x1 = x[..., :half].astype(jnp.float32)
    x2 = x[..., half:n_rot].astype(jnp.float32)
    rot = jnp.concatenate([x1 * cos - x2 * sin, x1 * sin + x2 * cos], axis=-1).astype(x.dtype)
    return jnp.concatenate([rot, x[..., n_rot:]], axis=-1)


def window_attn(q, k, v, dil, half):
    B, S, H, E = q.shape
    L = S // dil
    blk = half
    nb = -(-L // blk)
    Lp = nb * blk

    def strided(t):
        return t.reshape(B, L, dil, H, E).transpose(0, 2, 3, 1, 4)

    qs = jnp.pad(strided(q), ((0, 0), (0, 0), (0, 0), (0, Lp - L), (0, 0))).reshape(B, dil, H, nb, blk, E)

    def windows(t):
        tp = jnp.pad(strided(t), ((0, 0), (0, 0), (0, 0), (blk, Lp - L + blk), (0, 0))).reshape(B, dil, H, nb + 2, blk, E)
        return jnp.concatenate([tp[:, :, :, j:j + nb] for j in range(3)], axis=4)

    kw = windows(k)
    vw = windows(v)
    s = jnp.einsum('bdhnqe,bdhnke->bdhnqk', qs, kw, preferred_element_type=jnp.float32) * (E ** -0.5)
    qi = jnp.arange(blk)
    kj = jnp.arange(3 * blk)
    rel = kj[None, :] - blk - qi[:, None]
    kpos = jnp.arange(nb)[:, None] * blk + kj[None, :] - blk
    mask = (jnp.abs(rel) <= half)[None] & ((kpos >= 0) & (kpos < L))[:, None, :]
    s = jnp.where(mask, s, NEG)
    m = jnp.max(s, axis=-1, keepdims=True)
    p = jnp.exp(s - m)
    den = jnp.sum(p, axis=-1, keepdims=True)
    o = jnp.einsum('bdhnqk,bdhnke->bdhnqe', p, vw.astype(jnp.float32)) / den
    lse = (m + jnp.log(den))[..., 0]
    o = o.reshape(B, dil, H, Lp, E)[:, :, :, :L].transpose(0, 3, 1, 2, 4).reshape(B, S, H, E)
    lse = lse.reshape(B, dil, H, Lp)[..., :L].transpose(0, 3, 1, 2).reshape(B, S, H)
    return o, lse


def retention_scan(q, k, v, log_g):
    B, S, H, Dk = q.shape
    Dv = v.shape[-1]
    C = RET_CHUNK
    N = S // C
    qc = q.reshape(B, N, C, H, Dk)
    kc = k.reshape(B, N, C, H, Dk)
    vc = v.reshape(B, N, C, H, Dv)
    pos = jnp.arange(C, dtype=jnp.float32)
    rel = pos[:, None] - pos[None, :]
    intra = jnp.where(rel[None] >= 0, jnp.exp(jnp.maximum(rel, 0.0)[None] * log_g[:, None, None]), 0.0)
    scores = jnp.einsum('bnihd,bnjhd->bnhij', qc, kc) * intra
    inner = jnp.einsum('bnhij,bnjhe->bnihe', scores, vc)
    k_dec = jnp.exp((C - 1 - pos)[:, None] * log_g[None, :])
    q_dec = jnp.exp((pos + 1)[:, None] * log_g[None, :])
    delta = jnp.einsum('bnjhd,bnjhe->nbhde', kc * k_dec[:, :, None], vc)
    chunk_decay = jnp.exp(C * log_g)[None, :, None, None]

    def step(state, d):
        return state * chunk_decay + d, state

    _, prev = lax.scan(step, jnp.zeros((B, H, Dk, Dv), delta.dtype), delta)
    cross = jnp.einsum('bnihd,nbhde->bnihe', qc * q_dec[:, :, None], prev)
    return (inner + cross).reshape(B, S, H, Dv)


def mixer_layer(x, c, norm_g, w_ada, b_ada, w_in, q_norm_g, k_norm_g, decay_logit, ret_norm_g, w_proj_a, w_proj_b, w_out):
    B, S, _ = x.shape
    mod = jax.nn.silu(c) @ w_ada + b_ada
    shift, scale, gate = jnp.split(mod, 3, axis=-1)
    h = rms_norm(x, norm_g) * (1 + scale[:, None]) + shift[:, None]
    qa, ka, va, za, qr, kr, vr, zr, ga, gr = jnp.split(h @ w_in, IN_SPLIT_IDX, axis=-1)
    pos = jnp.arange(S, dtype=jnp.float32)

    qa = rope(rms_norm(qa.reshape(B, S, A_HEADS, A_HEAD_DIM), q_norm_g), pos, ATT_THETA, A_ROT_DIM)
    ka = rope(rms_norm(ka.reshape(B, S, A_HEADS, A_HEAD_DIM), k_norm_g), pos, ATT_THETA, A_ROT_DIM)
    va = va.reshape(B, S, A_HEADS, A_HEAD_DIM)
    outs = [window_attn(qa, ka, va, dil, win // (2 * dil)) for win, dil in DILATED]
    wts = jax.nn.softmax(jnp.stack([l for _, l in outs], axis=0), axis=0)
    ya = jnp.sum(wts[..., None] * jnp.stack([o for o, _ in outs], axis=0), axis=0)
    ya = (ya.reshape(B, S, A_WIDTH).astype(x.dtype) * jax.nn.silu(za)) @ w_proj_a

    qr = rope(qr.reshape(B, S, R_HEADS, R_KEY_DIM), pos, RET_THETA, R_KEY_DIM)
    kr = rope(kr.reshape(B, S, R_HEADS, R_KEY_DIM), pos, RET_THETA, R_KEY_DIM) * (R_KEY_DIM ** -0.5)
    vr = vr.reshape(B, S, R_HEADS, R_VAL_DIM)
    log_g = jax.nn.log_sigmoid(decay_logit.astype(jnp.float32))
    fwd = retention_scan(qr, kr, vr, log_g[0])
    bwd = jnp.flip(retention_scan(jnp.flip(qr, 1), jnp.flip(kr, 1), jnp.flip(vr, 1), log_g[1]), 1)
    yr = (fwd + bwd).astype(jnp.float32)
    mu = jnp.mean(yr, axis=-1, keepdims=True)
    var = jnp.mean(jnp.square(yr - mu), axis=-1, keepdims=True)
    yr = (yr - mu) * lax.rsqrt(var + EPS) * ret_norm_g.reshape(R_HEADS, R_VAL_DIM).astype(jnp.float32)
    yr = (yr.reshape(B, S, R_V_WIDTH).astype(x.dtype) * jax.nn.silu(zr)) @ w_proj_b

    merged = jax.nn.sigmoid(ga) * ya + jax.nn.sigmoid(gr) * yr
    return (x + gate[:, None] * (merged @ w_out)).astype(x.dtype)


def setup_inputs(seed: int = 0) -> dict:
    key = jax.random.key(seed)
    ks = jax.random.split(key, 16)
    f32 = jnp.float32
    base_logit = jnp.log(2.0 ** (5.0 + jnp.arange(R_HEADS, dtype=f32)) - 1.0)
    return {
        "x_prompt": jax.random.normal(ks[0], (BATCH, SEQ, D_MODEL), f32),
        "x_sample": jax.random.normal(ks[1], (DEC_BATCH, DEC_SEQ, D_MODEL), f32),
        "c_prompt": jax.random.normal(ks[2], (BATCH, D_MODEL), f32),
        "c_sample": jax.random.normal(ks[3], (DEC_BATCH, D_MODEL), f32),
        "norm_g": 1.0 + 0.02 * jax.random.normal(ks[4], (DEPTH, D_MODEL), f32),
        "w_ada": 0.5 * D_MODEL ** -0.5 * jax.random.normal(ks[5], (DEPTH, D_MODEL, 3 * D_MODEL), f32),
        "b_ada": 0.02 * jax.random.normal(ks[6], (DEPTH, 3 * D_MODEL), f32),
        "w_in": D_MODEL ** -0.5 * jax.random.normal(ks[7], (DEPTH, D_MODEL, IN_WIDTH), f32),
        "q_norm_g": 1.0 + 0.02 * jax.random.normal(ks[8], (DEPTH, A_HEAD_DIM), f32),
        "k_norm_g": 1.0 + 0.02 * jax.random.normal(ks[9], (DEPTH, A_HEAD_DIM), f32),
        "ret_decay_logit": base_logit[None, None, :] + 0.1 * jax.random.normal(ks[10], (DEPTH, 2, R_HEADS), f32),
        "ret_norm_g": 1.0 + 0.02 * jax.random.normal(ks[11], (DEPTH, R_V_WIDTH), f32),
        "w_proj_a": A_WIDTH ** -0.5 * jax.random.normal(ks[12], (DEPTH, A_WIDTH, D_MODEL), f32),
        "w_proj_b": R_V_WIDTH ** -0.5 * jax.random.normal(ks[13], (DEPTH, R_V_WIDTH, D_MODEL), f32),
        "w_out": D_MODEL ** -0.5 * jax.random.normal(ks[14], (DEPTH, D_MODEL, D_MODEL), f32),
    }


def reference(x_prompt, x_sample, c_prompt, c_sample, norm_g, w_ada, b_ada, w_in, q_norm_g, k_norm_g,
              ret_decay_logit, ret_norm_g, w_proj_a, w_proj_b, w_out):
    y_prompt = x_prompt
    y_sample = x_sample
    for l in range(DEPTH):
        params = (norm_g[l], w_ada[l], b_ada[l], w_in[l], q_norm_g[l], k_norm_g[l], ret_decay_logit[l],
                  ret_norm_g[l], w_proj_a[l], w_proj_b[l], w_out[l])
        y_prompt = mixer_layer(y_prompt, c_prompt, *params)
        y_sample = mixer_layer(y_sample, c_sample, *params)
    return (y_prompt, y_sample)
```

```python
import contextlib
import math
import numpy as np
import concourse.bass as bass
import concourse.mybir as mybir
from concourse.bass_utils import run_bass_kernel_spmd

F32 = mybir.dt.float32
BF16 = mybir.dt.bfloat16
ALU = mybir.AluOpType
AF = mybir.ActivationFunctionType

D = 1024
DEPTH = 4
SEQ = 2048
DEC_SEQ = 16384
NP = 4
BLK = 512
IN_W = 7168
EPS = 1e-6
EPOCH = 30000
NREL = 20


class Buf:
    __slots__ = ("w", "r", "dsem", "dram", "psum")

    def __init__(self, dram=False, psum=False):
        self.dram = dram
        self.psum = psum
        self.w = {}
        self.r = {}
        self.dsem = None


class Sched:
    def __init__(self, nc, stack):
        self.nc = nc
        self.stack = stack
        self.names = ["pe", "act", "dve", "pool", "sp"]
        self.ops = {e: [] for e in self.names}
        self.cnt = {e: 0 for e in self.names}
        self.seen = {e: {} for e in self.names}
        self.sems = {}
        self.nsem = 0
        self.dcount = {}

    def _sem(self, key):
        if key not in self.sems:
            self.nsem += 1
            self.sems[key] = self.stack.enter_context(self.nc.semaphore("s%d" % self.nsem))
        return self.sems[key]

    def _deps(self, e, reads, writes):
        deps = {}
        for b in reads:
            for k, v in b.w.items():
                if deps.get(k, 0) < v:
                    deps[k] = v
            if b.psum:
                for k, v in b.r.items():
                    if k[0] == "c" and k[1] != e and deps.get(k, 0) < v:
                        deps[k] = v
        for b in writes:
            for d in (b.w, b.r):
                for k, v in d.items():
                    if deps.get(k, 0) < v:
                        deps[k] = v
        waits = []
        seen = self.seen[e]
        for k, v in deps.items():
            if k[0] == "d":
                v = self.dcount[k]
            if seen.get(k, 0) >= v:
                continue
            seen[k] = v
            waits.append((self._sem(k), v))
        return waits

    def op(self, e, fn, reads=(), writes=()):
        waits = self._deps(e, reads, writes)
        self.cnt[e] += 1
        n = self.cnt[e]
        key = ("c", e, (n - 1) // EPOCH)
        val = (n - 1) % EPOCH + 1
        sem = self._sem(key)
        for b in writes:
            b.w = {key: val}
            b.r = {}
        for b in reads:
            if b not in writes:
                b.r[key] = val
        self.ops[e].append((waits, fn, sem, 1))

    def dma(self, e, fn, reads=(), writes=(), sbuf=None):
        waits = self._deps(e, reads, writes)
        tb = sbuf if sbuf is not None else (writes[0] if writes else reads[0])
        if tb.dsem is None:
            tb.dsem = ("d", id(tb))
            self.dcount[tb.dsem] = 0
        key = tb.dsem
        self.dcount[key] += 16
        val = self.dcount[key]
        sem = self._sem(key)
        for b in writes:
            if b.dram:
                b.w[key] = val
            else:
                b.w = {key: val}
            b.r = {}
        for b in reads:
            if b not in writes:
                b.r[key] = val
        self.ops[e].append((waits, fn, sem, 16))

    def finish(self):
        nc = self.nc
        final = [(h, self.dcount[k]) for k, h in self.sems.items() if k[0] == "d"]
        last = {}
        for e in self.names:
            n = self.cnt[e]
            if n:
                last[e] = (self._sem(("c", e, (n - 1) // EPOCH)), (n - 1) % EPOCH + 1)
        ops = self.ops

        def run(engine, name):
            for waits, fn, sem, inc in ops[name]:
                for (s, v) in waits:
                    engine.wait_ge(s, v)
                fn(engine).then_inc(sem, inc)
            if name == "sp":
                for (s, v) in final:
                    engine.wait_ge(s, v)
                for e2, (s, v) in last.items():
                    if e2 != "sp":
                        engine.wait_ge(s, v)

        with nc.Block() as block:
            @block.sync
            def _(eng):
                run(eng, "sp")

            @block.scalar
            def _(eng):
                run(eng, "act")

            @block.vector
            def _(eng):
                run(eng, "dve")

            @block.gpsimd
            def _(eng):
                run(eng, "pool")

            @block.tensor
            def _(eng):
                run(eng, "pe")


class Ring:
    def __init__(self, nc, st, name, shape, dtype, n):
        self.items = []
        for i in range(n):
            t = st.enter_context(nc.sbuf_tensor("rg_%s%d" % (name, i), shape, dtype))
            self.items.append((t, Buf()))
        self.i = 0

    def get(self):
        it = self.items[self.i % len(self.items)]
        self.i += 1
        return it


def weight_order(seq_lens, depth):
    order = []
    for l in range(depth):
        for S in seq_lens:
            nb = S // BLK
            for _ in range(nb):
                order += [("in", l, 512), ("in", l, 1024), ("in", l, 2560), ("in", l, 3072), ("in", l, 3584)]
            for _ in range(nb):
                order += [("in", l, 0), ("in", l, 1536), ("in", l, 2048), ("in", l, 4096), ("in", l, 4608)]
                for q in range(2):
                    order += [("in", l, 5120 + 512 * q), ("in", l, 6144 + 512 * q), ("pa", l, 512 * q), ("pb", l, 512 * q)]
                for q in range(2):
                    order += [("wo", l, 512 * q)]
    return order


def build_nc(seq_lens, depth):
    nseq = len(seq_lens)
    SMAX = max(seq_lens)
    nc = bass.Bass("TRN2", target_bir_lowering=False)

    def din(name, shape, dt=F32):
        return nc.dram_tensor(name, list(shape), dt, kind="ExternalInput").ap()

    xin = [din("x%d" % i, [D, S]) for i, S in enumerate(seq_lens)]
    yout = [nc.dram_tensor("y%d" % i, [D, S], F32, kind="ExternalOutput").ap() for i, S in enumerate(seq_lens)]
    cT = din("cT", [128, 8, nseq])
    w_ada = din("w_ada", [depth, D, 3 * D])
    b_adaT = din("b_adaT", [depth, 128, 24])
    norm_gT = din("norm_gT", [depth, 128, 8])
    w_in = din("w_in", [depth, D, IN_W])
    qkg = din("qkg", [128, 2 * depth])
    dlog = din("dlog", [128, 8 * depth])
    retgT = din("retgT", [depth, 128, 8])
    w_pa = din("w_pa", [depth, 512, D])
    w_pb = din("w_pb", [depth, D, D])
    w_o = din("w_o", [depth, D, D])
    cmat = din("cmat", [6, 128, 128])
    cmask = din("cmask", [NREL, 128, 512])
    cret = din("cret", [5, 128, 128])
    ccol = din("ccol", [128, 8])
    tabs = {}
    for S in sorted(set(seq_lens)):
        tabs[S] = din("tab%d" % S, [4, 128, S])

    def dscr(name, shape, dt):
        return nc.dram_tensor(name, list(shape), dt).ap()

    XS = [dscr("xs%d" % i, [D, S], F32) for i, S in enumerate(seq_lens)]
    HTd = dscr("HTd", [8, 128, SMAX], BF16)
    KTd = dscr("KTd", [4, 128, SMAX], BF16)
    VAd = dscr("VAd", [SMAX // 128, 128, 528], BF16)
    KRd = dscr("KRd", [4, 128, SMAX], BF16)
    VRd = dscr("VRd", [SMAX // 128, 128, 1024], BF16)
    SBd = dscr("SBd", [SMAX // 128, 128, 1024], BF16)
    B_XS = [Buf(dram=True) for _ in seq_lens]
    B_HTd, B_KTd, B_VAd, B_KRd, B_VRd, B_SBd = (Buf(dram=True) for _ in range(6))

    worder = weight_order(seq_lens, depth)

    with contextlib.ExitStack() as st:
        S_ = Sched(nc, st)
        op, dma = S_.op, S_.dma

        def I(name, *a, **kw):
            return lambda e: getattr(e, name)(*a, **kw)

        def sb(name, shape, dt):
            return st.enter_context(nc.sbuf_tensor("sb_" + name, list(shape), dt)), Buf()

        cm, B_cm = sb("cm", [128, 6, 128], BF16)
        for i in range(5):
            dma("pool", I("dma_start", out=cm[:, i, :], in_=cmat[i]), writes=[B_cm])
        ident, ones_blk, ones_full, RaT, RrT = (cm[:, i, :] for i in range(5))
        onesf, B_onesf = sb("onesf", [128, 64], F32)
        op("dve", I("memset", onesf[:], 1.0), writes=[B_onesf])
        masks, B_masks = sb("masks", [128, NREL, 512], BF16)
        for i in range(NREL):
            dma("pool", I("dma_start", out=masks[:, i, :], in_=cmask[i]), writes=[B_masks])
        cr, B_cr = sb("cr", [128, 5, 128], F32)
        dma("sp", I("dma_start", out=cr[:], in_=cret.rearrange("c p n -> p c n")), writes=[B_cr])
        cc_, B_cc = sb("ccol", [128, 8], F32)
        dma("sp", I("dma_start", out=cc_[:], in_=ccol), writes=[B_cc])
        onecol, epscol = cc_[:, 2:3], cc_[:, 3:4]
        qkg_s, B_qkg = sb("qkg", [128, 2 * depth], F32)
        dma("sp", I("dma_start", out=qkg_s[:], in_=qkg), writes=[B_qkg])
        lg, B_lg = sb("lg", [128, 8 * depth], F32)
        dma("sp", I("dma_start", out=lg[:], in_=dlog), writes=[B_lg])
        op("act", I("activation", out=lg[:], in_=lg[:], func=AF.Exp, scale=-1.0), reads=[B_lg], writes=[B_lg])
        op("act", I("activation", out=lg[:], in_=lg[:], func=AF.Ln, bias=onecol, scale=1.0), reads=[B_lg, B_cc], writes=[B_lg])
        op("dve", I("tensor_scalar", out=lg[:], in0=lg[:], scalar1=-1.0, scalar2=None, op0=ALU.mult), reads=[B_lg], writes=[B_lg])
        csil, B_csil = sb("csil", [128, 8, nseq], F32)
        dma("sp", I("dma_start", out=csil[:], in_=cT), writes=[B_csil])
        op("act", I("activation", out=csil[:], in_=csil[:], func=AF.Silu), reads=[B_csil], writes=[B_csil])

        modT, B_mod = sb("modT", [128, 24, nseq], F32)
        gmod, B_gmod = sb("gmod", [128, 8, nseq], F32)
        bada, B_bada = sb("bada", [128, 24], F32)
        ngT, B_ngT = sb("ngT", [128, 8], F32)
        rgT, B_rgT = sb("rgT", [128, 8], F32)
        MT, B_MT = sb("MT", [128, 4, 128], BF16)
        qdec, B_qdec = sb("qdec", [128, 8, 128], F32)
        kcol, B_kcol = sb("kcol", [128, 8], F32)
        gC, B_gC = sb("gC", [128, 8], F32)
        wada, B_wada = sb("wada", [128, 8, 128], F32)
        tmpd, B_tmpd = sb("tmpd", [128, 128], F32)
        tmpd2, B_tmpd2 = sb("tmpd2", [128, 128], F32)

        Sf, B_Sf = sb("Sf", [128, 4, 256], F32)
        Sfb, B_Sfb = sb("Sfb", [128, 4, 256], BF16)
        Sb, B_Sb = Sf, B_Sf

        WR = Ring(nc, st, "wr", [128, 8, 512], BF16, 5)
        wstate = {"i": 0, "slots": {}}
        PREFETCH = 1

        def wload(idx):
            kind, l, c0 = worder[idx]
            t, B = WR.get()
            if kind == "in":
                src = w_in[l].rearrange("(kc p) n -> p kc n", p=128)[:, :, c0:c0 + 512]
                dma("pool", I("dma_start", out=t[:], in_=src), writes=[B])
            elif kind == "pb":
                src = w_pb[l].rearrange("(kc p) n -> p kc n", p=128)[:, :, c0:c0 + 512]
                dma("pool", I("dma_start", out=t[:], in_=src), writes=[B])
            elif kind == "wo":
                src = w_o[l].rearrange("(kc p) n -> p kc n", p=128)[:, :, c0:c0 + 512]
                dma("pool", I("dma_start", out=t[:], in_=src), writes=[B])
            else:
                src = w_pa[l].rearrange("(h p) n -> p h n", p=64)[:, :, c0:c0 + 512]
                dma("pool", I("dma_start", out=t[0:64, :, :], in_=src), writes=[B])
            wstate["slots"][idx] = (t, B)

        def wnext(kind, l, c0):
            i = wstate["i"]
            assert worder[i] == (kind, l, c0), (worder[i], kind, l, c0)
            while wstate.get("loaded", 0) < min(len(worder), i + PREFETCH + 1):
                wload(wstate.get("loaded", 0))
                wstate["loaded"] = wstate.get("loaded", 0) + 1
            wstate["i"] = i + 1
            return wstate["slots"].pop(i)

        PS = []
        for i in range(7):
            t = st.enter_context(nc.psum_tensor("ps%d" % i, [128, 512], F32))
            PS.append((t, Buf(psum=True)))
        psT = (st.enter_context(nc.psum_tensor("psT", [128, 1024], BF16)), Buf(psum=True))
        pj = {"i": 0}

        def pproj():
            it = PS[pj["i"] % 2]
            pj["i"] += 1
            return it
        pStat, pRot, pSc0, pSc1, pO = PS[2], PS[3], PS[4], PS[5], PS[6]
        sc = {"i": 0}

        def pscore():
            it = (pSc0, pSc1)[sc["i"] % 2]
            sc["i"] += 1
            return it

        HT, B_HT = sb("HT", [128, 8, 512], BF16)
        TB, B_TB = sb("TB", [128, 4, 512], F32)
        QT, B_QT = sb("QT", [128, 4, 512], BF16)
        QR, B_QR = sb("QR", [128, 4, 512], BF16)
        KTw = Ring(nc, st, "ktw", [128, 2560], BF16, 1)
        VA, B_VA = sb("VA", [128, NREL, 132], BF16)
        YAG, B_YAG = sb("YAG", [64, 8, 512], BF16)
        YRG, B_YRG = sb("YRG", [128, 8, 512], BF16)
        VRb, B_VRb = sb("VRb", [128, 4, 1024], BF16)
        MG, B_MG = VRb[:].rearrange("p t (a n) -> p (t a) n", n=512), B_VRb
        KRb, B_KRb = sb("KRb", [128, 4, 512], BF16)
        SBlR = Ring(nc, st, "sbl", [128, 4, 256], BF16, 2)
        KD, B_KD = sb("KD", [128, 4, 512], BF16)
        rstd, B_rstd = sb("rstd", [128, 512], F32)
        R32 = Ring(nc, st, "r32", [128, 512], F32, 6)
        R16 = Ring(nc, st, "r16", [128, 512], BF16, 6)
        PR = Ring(nc, st, "pr", [128, 512], BF16, 3)
        SM = Ring(nc, st, "sm", [128, 128], BF16, 3)
        QF = Ring(nc, st, "qf", [128, 256], BF16, 3)
        SBst = Ring(nc, st, "sbst", [128, 1024], BF16, 2)
        VAst = Ring(nc, st, "vast", [128, 528], BF16, 2)
        XO = Ring(nc, st, "xo", [128, 512], F32, 2)
        XC = Ring(nc, st, "xc", [128, 512], F32, 2)
        for (t, B) in VAst.items:
            op("pool", I("memset", t[:], 1.0), writes=[B])

        def mm(out, lhsT, rhs, start, stop, reads, writes, **kw):
            op("pe", I("matmul", out, lhsT=lhsT, rhs=rhs, start=start, stop=stop, **kw), reads=reads, writes=writes)

        def rsqrt_ps(dst, B_dst, ps, B_ps, scale):
            op("act", I("activation", out=dst, in_=ps, func=AF.Sqrt, bias=epscol, scale=scale), reads=[B_ps, B_cc], writes=[B_dst])
            op("dve", I("reciprocal", out=dst, in_=dst), reads=[B_dst], writes=[B_dst])

        def proj_fm(Wt, B_W, c0, ncols, rhs_of_kc, B_rhs, nk=8, prow=128):
            pt, B_p = pproj()
            for kc in range(nk):
                mm(pt[0:ncols, :], Wt[0:prow, kc, c0:c0 + ncols], rhs_of_kc(kc), kc == 0, kc == nk - 1, [B_W, B_rhs], [B_p])
            return pt, B_p

        def rope(dst, B_dst, src, B_src, RT, tC, tS, mul=None, src_is_psum=True):
            sbf, B_sbf = R16.get()
            op("act", I("activation", out=sbf[:], in_=src, func=AF.Copy), reads=[B_src], writes=[B_sbf])
            mm(pRot[0][:], RT, sbf[:], True, True, [B_cm, B_sbf], [pRot[1]])
            t1, B_t1 = R32.get()
            t2, B_t2 = R32.get()
            if mul is None:
                op("dve", I("tensor_tensor", out=t1[:], in0=src, in1=tC, op=ALU.mult), reads=[B_src, B_TB], writes=[B_t1])
                op("dve", I("tensor_tensor", out=t2[:], in0=pRot[0][:], in1=tS, op=ALU.mult), reads=[pRot[1], B_TB], writes=[B_t2])
            else:
                op("dve", I("scalar_tensor_tensor", out=t1[:], in0=src, scalar=mul, in1=tC, op0=ALU.mult, op1=ALU.mult), reads=[B_src, B_TB, B_cc], writes=[B_t1])
                op("dve", I("scalar_tensor_tensor", out=t2[:], in0=pRot[0][:], scalar=mul, in1=tS, op0=ALU.mult, op1=ALU.mult), reads=[pRot[1], B_TB, B_cc], writes=[B_t2])
            op("pool", I("tensor_tensor", out=dst, in0=t1[:], in1=t2[:], op=ALU.add), reads=[B_t1, B_t2], writes=[B_dst])

        def qknorm_rope(dst, B_dst, pt, B_p, gcol):
            sq, B_sq = R16.get()
            op("act", I("activation", out=sq[:], in_=pt[:], func=AF.Square), reads=[B_p], writes=[B_sq])
            mm(pStat[0][:], ones_blk, sq[:], True, True, [B_cm, B_sq], [pStat[1]])
            rs, B_rs = R32.get()
            rsqrt_ps(rs[:], B_rs, pStat[0][:], pStat[1], 1.0 / 64)
            kn, B_kn = R32.get()
            op("dve", I("scalar_tensor_tensor", out=kn[:], in0=pt[:], scalar=gcol, in1=rs[:], op0=ALU.mult, op1=ALU.mult), reads=[B_p, B_rs, B_qkg], writes=[B_kn])
            rope(dst, B_dst, kn[:], B_kn, RaT, TB[:, 0, :], TB[:, 1, :])

        def ktranspose(src_of_h, B_src, n, colsel):
            for h in range(4):
                op("pe", I("transpose", psT[0][:, h * 128:(h + 1) * 128], src_of_h(h), ident), reads=[B_src, B_cm], writes=[psT[1]])
            for h in range(4):
                op("act", I("activation", out=KD[:, n, h * 128:(h + 1) * 128], in_=psT[0][:, h * 128:(h + 1) * 128], func=AF.Copy, scale=kcol[:, colsel + h:colsel + h + 1]), reads=[psT[1], B_kcol], writes=[B_KD])

        import os
        KSTOP = int(os.environ.get("KSTOP", "0"))

        class _Stop(Exception):
            pass
        try:
          for l in range(depth):
              dma("sp", I("dma_start", out=bada[:], in_=b_adaT[l]), writes=[B_bada])
              dma("sp", I("dma_start", out=ngT[:], in_=norm_gT[l]), writes=[B_ngT])
              dma("sp", I("dma_start", out=rgT[:], in_=retgT[l]), writes=[B_rgT])
              for fc in range(24):
                  if True:
                      dma("sp", I("dma_start", out=wada[:], in_=w_ada[l].rearrange("(kc p) n -> p kc n", p=128)[:, :, fc * 128:(fc + 1) * 128]), writes=[B_wada])
                      pt, B_p = pproj()
                      for kc in range(8):
                          mm(pt[:, 0:nseq], wada[:, kc, :], csil[:, kc, :], kc == 0, kc == 7, [B_wada, B_csil], [B_p])
                      op("dve", I("tensor_scalar", out=modT[:, fc, :], in0=pt[:, 0:nseq], scalar1=bada[:, fc:fc + 1], scalar2=None, op0=ALU.add), reads=[B_p, B_bada], writes=[B_mod])
              for c in range(8):
                  op("dve", I("tensor_scalar", out=gmod[:, c, :], in0=modT[:, 8 + c, :], scalar1=1.0, scalar2=ngT[:, c:c + 1], op0=ALU.add, op1=ALU.mult), reads=[B_mod, B_ngT], writes=[B_gmod])
              lo = 8 * l
              for h in range(4):
                  op("dve", I("tensor_scalar", out=tmpd[:], in0=cr[:, 0, :], scalar1=lg[:, lo + h:lo + h + 1], scalar2=None, op0=ALU.mult), reads=[B_cr, B_lg], writes=[B_tmpd])
                  op("dve", I("scalar_tensor_tensor", out=tmpd2[:], in0=cr[:, 1, :], scalar=lg[:, lo + 4 + h:lo + 5 + h], in1=tmpd[:], op0=ALU.mult, op1=ALU.add), reads=[B_cr, B_lg, B_tmpd], writes=[B_tmpd2])
                  op("act", I("activation", out=tmpd2[:], in_=tmpd2[:], func=AF.Exp), reads=[B_tmpd2], writes=[B_tmpd2])
                  op("dve", I("tensor_tensor", out=MT[:, h, :], in0=tmpd2[:], in1=cr[:, 2, :], op=ALU.mult), reads=[B_tmpd2, B_cr], writes=[B_MT])
                  op("act", I("activation", out=qdec[:, h, :], in_=cr[:, 3, :], func=AF.Exp, scale=lg[:, lo + h:lo + h + 1]), reads=[B_cr, B_lg], writes=[B_qdec])
                  op("act", I("activation", out=qdec[:, 4 + h, :], in_=cr[:, 4, :], func=AF.Exp, scale=lg[:, lo + 4 + h:lo + 5 + h]), reads=[B_cr, B_lg], writes=[B_qdec])
                  op("act", I("activation", out=kcol[:, h:h + 1], in_=cc_[:, 0:1], func=AF.Exp, scale=lg[:, lo + h:lo + h + 1]), reads=[B_cc, B_lg], writes=[B_kcol])
                  op("act", I("activation", out=kcol[:, 4 + h:5 + h], in_=cc_[:, 1:2], func=AF.Exp, scale=lg[:, lo + 4 + h:lo + 5 + h]), reads=[B_cc, B_lg], writes=[B_kcol])
              op("act", I("activation", out=gC[:], in_=lg[:, lo:lo + 8], func=AF.Exp, scale=128.0), reads=[B_lg], writes=[B_gC])
              if KSTOP == 1:
                  raise _Stop()
              qg = qkg_s[:, 2 * l:2 * l + 1]
              kg = qkg_s[:, 2 * l + 1:2 * l + 2]

              for si, S in enumerate(seq_lens):
                  nb = S // BLK
                  xsrc = xin[si] if l == 0 else XS[si]
                  B_xsrc = None if l == 0 else B_XS[si]
                  xdst = yout[si] if l == depth - 1 else XS[si]
                  B_xdst = None if l == depth - 1 else B_XS[si]
                  tab = tabs[S]
                  rd_x = [B_xsrc] if B_xsrc is not None else []

                  op("dve", I("memset", Sb[:], 0.0), writes=[B_Sb])
                  for b in reversed(range(nb)):
                      t0 = b * BLK
                      dma("sp", I("dma_start", out=TB[:], in_=tab.rearrange("k p t -> p k t")[:, :, t0:t0 + BLK]), writes=[B_TB])
                      for c in range(8):
                          sq, B_sq = R16.get()
                          xc, B_xc = XC.get()
                          dma("sp", I("dma_start", out=xc[:], in_=xsrc[c * 128:(c + 1) * 128, t0:t0 + BLK]), reads=rd_x, writes=[B_xc])
                          op("act", I("activation", out=sq[:], in_=xc[:], func=AF.Square), reads=[B_xc], writes=[B_sq])
                          mm(pStat[0][:], ones_full, sq[:], c == 0, c == 7, [B_cm, B_sq], [pStat[1]])
                      rsqrt_ps(rstd[:], B_rstd, pStat[0][:], pStat[1], 1.0 / D)
                      for c in range(8):
                          t1, B_t1 = R32.get()
                          xc, B_xc = XC.get()
                          dma("sp", I("dma_start", out=xc[:], in_=xsrc[c * 128:(c + 1) * 128, t0:t0 + BLK]), reads=rd_x, writes=[B_xc])
                          op("dve", I("scalar_tensor_tensor", out=t1[:], in0=xc[:], scalar=gmod[:, c, si:si + 1], in1=rstd[:], op0=ALU.mult, op1=ALU.mult), reads=[B_xc, B_gmod, B_rstd], writes=[B_t1])
                          op("act", I("activation", out=HT[:, c, :], in_=t1[:], func=AF.Identity, bias=modT[:, c, si:si + 1], scale=1.0), reads=[B_t1, B_mod], writes=[B_HT])
                      dma("sp", I("dma_start", out=HTd.rearrange("c p t -> p c t")[:, :, t0:t0 + BLK], in_=HT[:]), reads=[B_HT], writes=[B_HTd])
                      hrhs = lambda kc: HT[:, kc, :]
                      Wt, B_W = wnext("in", l, 512)
                      for cc in range(4):
                          pt, B_p = proj_fm(Wt, B_W, cc * 128, 128, hrhs, B_HT)
                          ks, B_ks = R16.get()
                          qknorm_rope(ks[:], B_ks, pt, B_p, kg)
                          dma("sp", I("dma_start", out=KTd[cc][:, t0:t0 + BLK], in_=ks[:]), reads=[B_ks], writes=[B_KTd])
                      if KSTOP == 5:
                          raise _Stop()
                      Wt, B_W = wnext("in", l, 1024)
                      for tt in range(4):
                          pt, B_p = pproj()
                          for kc in range(8):
                              mm(pt[:], HT[:, kc, tt * 128:(tt + 1) * 128], Wt[:, kc, :], kc == 0, kc == 7, [B_HT, B_W], [B_p])
                          vs, B_vs = VAst.get()
                          op("act", I("activation", out=vs[:].rearrange("p (h e) -> p h e", e=66)[:, :, 0:64], in_=pt[:].rearrange("p (h e) -> p h e", e=64), func=AF.Copy), reads=[B_p], writes=[B_vs])
                          dma("sp", I("dma_start", out=VAd[4 * b + tt], in_=vs[:]), reads=[B_vs], writes=[B_VAd])
                      if KSTOP == 7:
                          raise _Stop()
                      Wt, B_W = wnext("in", l, 2560)
                      for h in range(4):
                          pt, B_p = proj_fm(Wt, B_W, h * 128, 128, hrhs, B_HT)
                          rope(KRb[:, h, :], B_KRb, pt[:], B_p, RrT, TB[:, 2, :], TB[:, 3, :], mul=cc_[:, 4:5])
                      dma("sp", I("dma_start", out=KRd.rearrange("h p t -> p h t")[:, :, t0:t0 + BLK], in_=KRb[:]), reads=[B_KRb], writes=[B_KRd])
                      if KSTOP == 8:
                          raise _Stop()
                      for g in range(2):
                          Wt, B_W = wnext("in", l, 3072 + 512 * g)
                          for tt in range(4):
                              pt, B_p = pproj()
                              for kc in range(8):
                                  mm(pt[:], HT[:, kc, tt * 128:(tt + 1) * 128], Wt[:, kc, :], kc == 0, kc == 7, [B_HT, B_W], [B_p])
                              op("act", I("activation", out=VRb[:, tt, g * 512:(g + 1) * 512], in_=pt[:], func=AF.Copy), reads=[B_p], writes=[B_VRb])
                      dma("sp", I("dma_start", out=VRd[4 * b:4 * b + 4].rearrange("t p n -> p t n"), in_=VRb[:]), reads=[B_VRb], writes=[B_VRd])
                      if KSTOP == 6:
                          raise _Stop()
                      for n in reversed(range(4)):
                          ktranspose(lambda h, n=n: KRb[:, h, n * 128:(n + 1) * 128], B_KRb, n, 4)
                          stg, B_stg = SBst.get()
                          op("pool", I("tensor_copy", out=stg[:], in_=Sb[:].rearrange("p h n -> p (h n)")), reads=[B_Sb], writes=[B_stg])
                          dma("sp", I("dma_start", out=SBd[4 * b + n], in_=stg[:]), reads=[B_stg], writes=[B_SBd])
                          for h in range(4):
                              pt, B_p = pproj()
                              mm(pt[:, 0:256], KD[:, n, h * 128:(h + 1) * 128], VRb[:, n, h * 256:(h + 1) * 256], True, True, [B_KD, B_VRb], [B_p])
                              op("dve", I("scalar_tensor_tensor", out=Sb[:, h, :], in0=Sb[:, h, :], scalar=gC[:, 4 + h:5 + h], in1=pt[:, 0:256], op0=ALU.mult, op1=ALU.add), reads=[B_p, B_gC, B_Sb], writes=[B_Sb])

                  if KSTOP == 2:
                      raise _Stop()
                  op("dve", I("memset", Sf[:], 0.0), writes=[B_Sf])
                  op("pool", I("memset", Sfb[:], 0.0), writes=[B_Sfb])
                  ntile = S // 128
                  for b in range(nb):
                      t0 = b * BLK
                      dma("sp", I("dma_start", out=HT[:], in_=HTd.rearrange("c p t -> p c t")[:, :, t0:t0 + BLK]), reads=[B_HTd], writes=[B_HT])
                      dma("sp", I("dma_start", out=TB[:], in_=tab.rearrange("k p t -> p k t")[:, :, t0:t0 + BLK]), writes=[B_TB])
                      ulo = max(0, 4 * b - 8)
                      uhi = min(ntile, 4 * b + 12)
                      dma("sp", I("dma_start", out=VRb[:], in_=VRd[4 * b:4 * b + 4].rearrange("t p n -> p t n")), reads=[B_VRd], writes=[B_VRb])
                      dma("sp", I("dma_start", out=KRb[:], in_=KRd.rearrange("h p t -> p h t")[:, :, t0:t0 + BLK]), reads=[B_KRd], writes=[B_KRb])
                      hrhs = lambda kc: HT[:, kc, :]
                      Wt, B_W = wnext("in", l, 0)
                      for cc in range(4):
                          pt, B_p = proj_fm(Wt, B_W, cc * 128, 128, hrhs, B_HT)
                          qknorm_rope(QT[:, cc, :], B_QT, pt, B_p, qg)
                      Wz, B_Wz = wnext("in", l, 1536)
                      for cc in range(4):
                          ktw, B_ktw = KTw.get()
                          dma("sp", I("dma_start", out=ktw[:, 0:(uhi - ulo) * 128], in_=KTd[cc][:, ulo * 128:uhi * 128]), reads=[B_KTd], writes=[B_ktw])
                          dma("sp", I("dma_start", out=VA[:, 0:uhi - ulo, :], in_=VAd[ulo:uhi].rearrange("t p n -> p t n")[:, :, cc * 132:(cc + 1) * 132]), reads=[B_VAd], writes=[B_VA])
                          for hp in range(2):
                              h = 2 * cc + hp
                              rows = slice(hp * 64, hp * 64 + 64)
                              for u in range(ulo, uhi):
                                  ps_s, B_s = pscore()
                                  mm(ps_s[:], ktw[rows, (u - ulo) * 128:(u - ulo + 1) * 128], QT[rows, cc, :], True, True, [B_ktw, B_QT], [B_s])
                                  pr, B_pr = PR.get()
                                  op("act", I("activation", out=pr[:], in_=ps_s[:], func=AF.Exp, scale=0.125), reads=[B_s], writes=[B_pr])
                                  rel = u - 4 * b + 8
                                  op("pool", I("tensor_tensor", out=pr[:], in0=pr[:], in1=masks[:, rel, :], op=ALU.mult), reads=[B_pr, B_masks], writes=[B_pr])
                                  mm(pO[0][0:65, :], VA[:, u - ulo, hp * 66:hp * 66 + 65], pr[:], u == ulo, u == uhi - 1, [B_VA, B_pr], [pO[1]])
                              rd, B_rd = R32.get()
                              op("dve", I("reciprocal", out=rd[64:65, :], in_=pO[0][64:65, :]), reads=[pO[1]], writes=[B_rd])
                              mm(pRot[0][0:64, :], onesf[64:65, 0:64], rd[64:65, :], True, True, [B_onesf, B_rd], [pRot[1]])
                              bc, B_bc = R32.get()
                              op("act", I("activation", out=bc[0:64, :], in_=pRot[0][0:64, :], func=AF.Copy), reads=[pRot[1]], writes=[B_bc])
                              ya, B_ya = R32.get()
                              op("dve", I("tensor_tensor", out=ya[0:64, :], in0=pO[0][0:64, :], in1=bc[0:64, :], op=ALU.mult), reads=[pO[1], B_bc], writes=[B_ya])
                              pt, B_p = proj_fm(Wz, B_Wz, h * 64, 64, hrhs, B_HT)
                              sz, B_sz = R32.get()
                              op("act", I("activation", out=sz[0:64, :], in_=pt[0:64, :], func=AF.Silu), reads=[B_p], writes=[B_sz])
                              op("pool", I("tensor_tensor", out=YAG[:, h, :], in0=ya[0:64, :], in1=sz[0:64, :], op=ALU.mult), reads=[B_ya, B_sz], writes=[B_YAG])
                      if KSTOP == 3:
                          raise _Stop()
                      Wt, B_W = wnext("in", l, 2048)
                      for h in range(4):
                          pt, B_p = proj_fm(Wt, B_W, h * 128, 128, hrhs, B_HT)
                          rope(QR[:, h, :], B_QR, pt[:], B_p, RrT, TB[:, 2, :], TB[:, 3, :])
                      Wzr = [wnext("in", l, 4096), wnext("in", l, 4608)]
                      for n in range(4):
                          ktranspose(lambda h, n=n: KRb[:, h, n * 128:(n + 1) * 128], B_KRb, n, 0)
                      po_list = []
                      for h in range(4):
                          pe_ = [PS[4 + (2 * h) % 3] if False else None, None]
                          po = [pscore(), pscore()]
                          SBl, B_SBl = SBlR.get()
                          dma("sp", I("dma_start", out=SBl[:], in_=SBd[4 * b:4 * b + 4].rearrange("t p n -> p t n")[:, :, h * 256:(h + 1) * 256]), reads=[B_SBd], writes=[B_SBl])
                          for n in range(4):
                              cs = slice(n * 128, (n + 1) * 128)
                              ps_s, B_s = pproj()
                              mm(ps_s[:, 0:128], KRb[:, h, cs], QR[:, h, cs], True, True, [B_KRb, B_QR], [B_s])
                              smt, B_sm = SM.get()
                              op("dve", I("tensor_tensor", out=smt[:], in0=ps_s[:, 0:128], in1=MT[:, h, :], op=ALU.mult), reads=[B_s, B_MT], writes=[B_sm])
                              qf, B_qf = QF.get()
                              op("pool", I("tensor_tensor", out=qf[:, 0:128], in0=QR[:, h, cs], in1=qdec[:, h, :], op=ALU.mult), reads=[B_QR, B_qdec], writes=[B_qf])
                              op("pool", I("tensor_tensor", out=qf[:, 128:256], in0=QR[:, h, cs], in1=qdec[:, 4 + h, :], op=ALU.mult), reads=[B_QR, B_qdec], writes=[B_qf])
                              for ev in range(2):
                                  vs_ = slice(h * 256 + ev * 128, h * 256 + ev * 128 + 128)
                                  pt_o, B_o = po[ev]
                                  mm(pt_o[:, cs], VRb[:, n, vs_], smt[:], True, False, [B_VRb, B_sm], [B_o])
                                  mm(pt_o[:, cs], Sfb[:, h, ev * 128:(ev + 1) * 128], qf[:, 0:128], False, False, [B_Sfb, B_qf], [B_o])
                                  mm(pt_o[:, cs], SBl[:, n, ev * 128:(ev + 1) * 128], qf[:, 128:256], False, True, [B_SBl, B_qf], [B_o])
                              pd, B_pd = pproj()
                              mm(pd[:, 0:256], KD[:, n, h * 128:(h + 1) * 128], VRb[:, n, h * 256:(h + 1) * 256], True, True, [B_KD, B_VRb], [B_pd])
                              op("dve", I("scalar_tensor_tensor", out=Sf[:, h, :], in0=Sf[:, h, :], scalar=gC[:, h:h + 1], in1=pd[:, 0:256], op0=ALU.mult, op1=ALU.add), reads=[B_pd, B_gC, B_Sf], writes=[B_Sf])
                              op("act", I("activation", out=Sfb[:, h, :], in_=Sf[:, h, :], func=AF.Copy), reads=[B_Sf], writes=[B_Sfb])
                          yb = [R16.get(), R16.get()]
                          ysq = [R16.get(), R16.get()]
                          for ev in range(2):
                              op("act", I("activation", out=yb[ev][0][:], in_=po[ev][0][:], func=AF.Copy), reads=[po[ev][1]], writes=[yb[ev][1]])
                              op("act", I("activation", out=ysq[ev][0][:], in_=po[ev][0][:], func=AF.Square), reads=[po[ev][1]], writes=[ysq[ev][1]])
                          for ev in range(2):
                              mm(pStat[0][:], ones_full, yb[ev][0][:], ev == 0, ev == 1, [B_cm, yb[ev][1]], [pStat[1]])
                          for ev in range(2):
                              mm(pRot[0][:], ones_full, ysq[ev][0][:], ev == 0, ev == 1, [B_cm, ysq[ev][1]], [pRot[1]])
                          mean, B_mean = R32.get()
                          op("dve", I("tensor_scalar", out=mean[:], in0=pStat[0][:], scalar1=1.0 / 256, scalar2=None, op0=ALU.mult), reads=[pStat[1]], writes=[B_mean])
                          msq, B_msq = R32.get()
                          op("pool", I("tensor_tensor", out=msq[:], in0=mean[:], in1=mean[:], op=ALU.mult), reads=[B_mean], writes=[B_msq])
                          var, B_var = R32.get()
                          op("dve", I("scalar_tensor_tensor", out=var[:], in0=pRot[0][:], scalar=1.0 / 256, in1=msq[:], op0=ALU.mult, op1=ALU.subtract), reads=[pRot[1], B_msq], writes=[B_var])
                          op("act", I("activation", out=var[:], in_=var[:], func=AF.Sqrt, bias=epscol, scale=1.0), reads=[B_var, B_cc], writes=[B_var])
                          op("dve", I("reciprocal", out=var[:], in_=var[:]), reads=[B_var], writes=[B_var])
                          for ev in range(2):
                              c8 = 2 * h + ev
                              Wzt, B_Wzt = Wzr[c8 // 4]
                              pt, B_p = proj_fm(Wzt, B_Wzt, (c8 % 4) * 128, 128, hrhs, B_HT)
                              sz, B_sz = R32.get()
                              op("act", I("activation", out=sz[:], in_=pt[:], func=AF.Silu), reads=[B_p], writes=[B_sz])
                              d1, B_d1 = R32.get()
                              op("dve", I("tensor_tensor", out=d1[:], in0=po[ev][0][:], in1=mean[:], op=ALU.subtract), reads=[po[ev][1], B_mean], writes=[B_d1])
                              op("dve", I("scalar_tensor_tensor", out=d1[:], in0=d1[:], scalar=rgT[:, c8:c8 + 1], in1=var[:], op0=ALU.mult, op1=ALU.mult), reads=[B_d1, B_var, B_rgT], writes=[B_d1])
                              op("pool", I("tensor_tensor", out=YRG[:, c8, :], in0=d1[:], in1=sz[:], op=ALU.mult), reads=[B_d1, B_sz], writes=[B_YRG])
                      if KSTOP == 4:
                          raise _Stop()
                      for q in range(2):
                          Wga, B_Wga = wnext("in", l, 5120 + 512 * q)
                          Wgr, B_Wgr = wnext("in", l, 6144 + 512 * q)
                          Wpa, B_Wpa = wnext("pa", l, 512 * q)
                          Wpb, B_Wpb = wnext("pb", l, 512 * q)
                          for j in range(4):
                              oc = 4 * q + j
                              pa_, B_pa = proj_fm(Wpa, B_Wpa, j * 128, 128, lambda hh: YAG[:, hh, :], B_YAG, nk=8, prow=64)
                              pga, B_pga = pStat
                              for kc in range(8):
                                  mm(pga[:], Wga[:, kc, j * 128:(j + 1) * 128], HT[:, kc, :], kc == 0, kc == 7, [B_Wga, B_HT], [B_pga])
                              sa, B_sa = R32.get()
                              op("act", I("activation", out=sa[:], in_=pga[:], func=AF.Sigmoid), reads=[B_pga], writes=[B_sa])
                              m1, B_m1 = R32.get()
                              op("dve", I("tensor_tensor", out=m1[:], in0=pa_[:], in1=sa[:], op=ALU.mult), reads=[B_pa, B_sa], writes=[B_m1])
                              pb_, B_pb = proj_fm(Wpb, B_Wpb, j * 128, 128, lambda c: YRG[:, c, :], B_YRG)
                              pgr, B_pgr = pRot
                              for kc in range(8):
                                  mm(pgr[:], Wgr[:, kc, j * 128:(j + 1) * 128], HT[:, kc, :], kc == 0, kc == 7, [B_Wgr, B_HT], [B_pgr])
                              sr, B_sr = R32.get()
                              op("act", I("activation", out=sr[:], in_=pgr[:], func=AF.Sigmoid), reads=[B_pgr], writes=[B_sr])
                              m2, B_m2 = R32.get()
                              op("dve", I("tensor_tensor", out=m2[:], in0=pb_[:], in1=sr[:], op=ALU.mult), reads=[B_pb, B_sr], writes=[B_m2])
                              op("pool", I("tensor_tensor", out=MG[:, oc, :], in0=m1[:], in1=m2[:], op=ALU.add), reads=[B_m1, B_m2], writes=[B_MG])
                      for q in range(2):
                          Wo, B_Wo = wnext("wo", l, 512 * q)
                          for j in range(4):
                              oc = 4 * q + j
                              xc, B_xc = XC.get()
                              dma("sp", I("dma_start", out=xc[:], in_=xsrc[oc * 128:(oc + 1) * 128, t0:t0 + BLK]), reads=rd_x, writes=[B_xc])
                              po_, B_po = proj_fm(Wo, B_Wo, j * 128, 128, lambda c: MG[:, c, :], B_MG)
                              xo, B_xo = XO.get()
                              op("dve", I("scalar_tensor_tensor", out=xo[:], in0=po_[:], scalar=modT[:, 16 + oc, si:si + 1], in1=xc[:], op0=ALU.mult, op1=ALU.add), reads=[B_po, B_xc, B_mod], writes=[B_xo])
                              wr_x = [B_xdst] if B_xdst is not None else []
                              dma("sp", I("dma_start", out=xdst[oc * 128:(oc + 1) * 128, t0:t0 + BLK], in_=xo[:]), reads=[B_xo], writes=wr_x, sbuf=B_xo)
        except _Stop:
            pass
        if not KSTOP:
            assert wstate["i"] == len(worder)
        print('nsem', S_.nsem, {e: S_.cnt[e] for e in S_.names})
        S_.finish()
    return nc


def _tables(S):
    pos = np.arange(S, dtype=np.float32)
    p = np.arange(128)
    tab = np.zeros((4, 128, S), np.float32)
    fa = np.exp(np.float32(-math.log(500000.0)) * np.arange(8, dtype=np.float32) / np.float32(8)).astype(np.float32)
    e = p % 64
    tab[0] = 1.0
    for pp in range(128):
        ee = e[pp]
        if ee < 16:
            ang = (pos * fa[ee % 8]).astype(np.float32).astype(np.float64)
            tab[0, pp] = np.cos(ang)
            tab[1, pp] = (-np.sin(ang)) if ee < 8 else np.sin(ang)
    fr = np.exp(np.float32(-math.log(10000.0)) * np.arange(64, dtype=np.float32) / np.float32(64)).astype(np.float32)
    for pp in range(128):
        ang = (pos * fr[pp % 64]).astype(np.float32).astype(np.float64)
        tab[2, pp] = np.cos(ang)
        tab[3, pp] = (-np.sin(ang)) if pp < 64 else np.sin(ang)
    return tab


def _consts():
    p = np.arange(128)
    cmat = np.zeros((6, 128, 128), np.float32)
    cmat[0] = np.eye(128)
    cmat[1] = (p[:, None] // 64 == p[None, :] // 64)
    cmat[2] = 1.0
    for m in range(128):
        e = m % 64
        if e < 8:
            cmat[3, m + 8, m] = 1.0
        elif e < 16:
            cmat[3, m - 8, m] = 1.0
        cmat[4, (m + 64) % 128, m] = 1.0
    cmask = np.zeros((NREL, 128, 512), np.float32)
    for r in range(NREL):
        rel = r - 8
        d = 128 * rel + p[:, None] - np.arange(512)[None, :]
        ad = np.abs(d)
        cmask[r] = (ad <= 64).astype(np.float32) + ((d % 4 == 0) & (ad <= 256)) + ((d % 16 == 0) & (ad <= 1024))
    cret = np.zeros((5, 128, 128), np.float32)
    dd = (np.arange(128)[None, :] - p[:, None]).astype(np.float32)
    cret[0] = np.maximum(dd, 0)
    cret[1] = np.maximum(-dd, 0)
    cret[2] = 1.0 + (dd == 0)
    cret[3] = np.arange(128)[None, :] + 1.0
    cret[4] = 128.0 - np.arange(128)[None, :]
    ccol = np.zeros((128, 8), np.float32)
    ccol[:, 4] = 128.0 ** -0.5
    ccol[:, 0] = 127 - p
    ccol[:, 1] = p
    ccol[:, 2] = 1.0
    ccol[:, 3] = EPS
    return cmat, cmask, cret, ccol


_CACHE = {}


def kernel(x_prompt, x_sample, c_prompt, c_sample, norm_g, w_ada, b_ada, w_in, q_norm_g, k_norm_g,
           ret_decay_logit, ret_norm_g, w_proj_a, w_proj_b, w_out):
    n_cores = 8
    x_prompt = np.asarray(x_prompt, np.float32)
    x_sample = np.asarray(x_sample, np.float32)
    depth = int(np.asarray(norm_g).shape[0])
    seq_lens = [x_prompt.shape[1]] * NP + [x_sample.shape[1]]
    key = (tuple(seq_lens), depth)
    if key not in _CACHE:
        _CACHE[key] = build_nc(seq_lens, depth)
    nc = _CACHE[key]
    cmat, cmask, cret, ccol = _consts()
    tabs = {S: _tables(S) for S in sorted(set(seq_lens))}
    f = lambda a: np.ascontiguousarray(np.asarray(a, np.float32))

    def colT(a, n):
        a = f(a)
        return np.ascontiguousarray(a.reshape(a.shape[0], n, 128).transpose(0, 2, 1))
    qkg = np.zeros((128, 2 * depth), np.float32)
    for l in range(depth):
        qkg[:, 2 * l] = np.tile(f(q_norm_g)[l], 2)
        qkg[:, 2 * l + 1] = np.tile(f(k_norm_g)[l], 2)
    dlog = np.ascontiguousarray(np.broadcast_to(f(ret_decay_logit).reshape(1, -1), (128, 8 * depth)))
    xsT = np.ascontiguousarray(x_sample[0].T)
    common = {
        "w_ada": f(w_ada), "b_adaT": colT(b_ada, 24), "norm_gT": colT(norm_g, 8), "w_in": f(w_in),
        "qkg": qkg, "dlog": dlog, "retgT": colT(ret_norm_g, 8), "w_pa": f(w_proj_a), "w_pb": f(w_proj_b),
        "w_o": f(w_out), "cmat": cmat, "cmask": cmask, "cret": cret, "ccol": ccol,
    }
    for S, t in tabs.items():
        common["tab%d" % S] = t
    in_maps = []
    for c in range(n_cores):
        m = dict(common)
        cs = np.concatenate([f(c_prompt)[c * NP:(c + 1) * NP], f(c_sample)], axis=0)
        m["cT"] = np.ascontiguousarray(cs.reshape(NP + 1, 8, 128).transpose(2, 1, 0))
        for i in range(NP):
            m["x%d" % i] = np.ascontiguousarray(x_prompt[c * NP + i].T)
        m["x%d" % NP] = xsT
        in_maps.append(m)
    res = run_bass_kernel_spmd(nc, in_maps, core_ids=list(range(n_cores)))
    y_prompt = np.empty_like(x_prompt)
    for c in range(n_cores):
        for i in range(NP):
            y_prompt[c * NP + i] = res.results[c]["y%d" % i].T
    y_sample = np.ascontiguousarray(res.results[0]["y%d" % NP].T)[None]
    return (y_prompt, y_sample.astype(np.float32))
```

```python
import contextlib
import math
import numpy as np
import concourse.bass as bass
import concourse.mybir as mybir
from concourse.bass_utils import run_bass_kernel_spmd

F32 = mybir.dt.float32
BF16 = mybir.dt.bfloat16
ALU = mybir.AluOpType
AF = mybir.ActivationFunctionType

D = 1024
DEPTH = 4
SEQ = 2048
DEC_SEQ = 16384
NP = 4
BLK = 512
IN_W = 7168
EPS = 1e-6
EPOCH = 30000
NREL = 20


class Buf:
    __slots__ = ("w", "r", "dsem", "dram", "psum")

    def __init__(self, dram=False, psum=False):
        self.dram = dram
        self.psum = psum
        self.w = {}
        self.r = {}
        self.dsem = None


class Sched:
    def __init__(self, nc, stack):
        self.nc = nc
        self.stack = stack
        self.names = ["pe", "act", "dve", "pool", "sp"]
        self.ops = {e: [] for e in self.names}
        self.cnt = {e: 0 for e in self.names}
        self.seen = {e: {} for e in self.names}
        self.sems = {}
        self.nsem = 0
        self.dcount = {}
        self.waited = {e: set() for e in self.names}

    def _sem(self, key):
        if key not in self.sems:
            self.nsem += 1
            self.sems[key] = self.stack.enter_context(self.nc.semaphore("s%d" % self.nsem))
        return self.sems[key]

    def _deps(self, e, reads, writes):
        deps = {}
        for b in reads:
            for k, v in b.w.items():
                if deps.get(k, 0) < v:
                    deps[k] = v
            if b.psum:
                for k, v in b.r.items():
                    if k[0] == "c" and k[1] != e and deps.get(k, 0) < v:
                        deps[k] = v
        for b in writes:
            for d in (b.w, b.r):
                for k, v in d.items():
                    if deps.get(k, 0) < v:
                        deps[k] = v
        waits = []
        seen = self.seen[e]
        for k, v in deps.items():
            if k[0] == "c" and k[1] == "pe" and e == "pe":
                continue
            if k[0] == "d":
                v = self.dcount[k]
            if seen.get(k, 0) >= v:
                continue
            seen[k] = v
            if k[0] == "c":
                self.waited[k[1]].add(v)
            waits.append((k, v))
        return waits

    def op(self, e, fn, reads=(), writes=()):
        waits = self._deps(e, reads, writes)
        self.cnt[e] += 1
        n = self.cnt[e]
        key = ("c", e)
        for b in writes:
            b.w = {key: n}
            b.r = {}
        for b in reads:
            if b not in writes:
                b.r[key] = n
        self.ops[e].append((waits, fn, ("c", n)))

    def dma(self, e, fn, reads=(), writes=(), sbuf=None):
        waits = self._deps(e, reads, writes)
        tb = sbuf if sbuf is not None else (writes[0] if writes else reads[0])
        if tb.dsem is None:
            tb.dsem = ("d", id(tb))
            self.dcount[tb.dsem] = 0
        key = tb.dsem
        self.dcount[key] += 16
        val = self.dcount[key]
        for b in writes:
            if b.dram:
                b.w[key] = val
            else:
                b.w = {key: val}
            b.r = {}
        for b in reads:
            if b not in writes:
                b.r[key] = val
        self.ops[e].append((waits, fn, ("d", key)))

    def finish(self):
        nc = self.nc
        import bisect
        for e in self.names:
            if self.cnt[e]:
                self.waited[e].add(self.cnt[e])
        wl = {e: sorted(self.waited[e]) for e in self.names}

        def csem(e, n):
            r = bisect.bisect_left(wl[e], n)
            assert wl[e][r] == n
            return self._sem(("c", e, r // EPOCH)), r % EPOCH + 1
        final = [(self._sem(k), v) for k, v in self.dcount.items()]
        last = {e: csem(e, self.cnt[e]) for e in self.names if self.cnt[e]}
        ops = self.ops
        waited = self.waited

        def run(engine, name):
            for waits, fn, evt in ops[name]:
                for (k, v) in waits:
                    if k[0] == "c":
                        s_, v_ = csem(k[1], v)
                    else:
                        s_, v_ = self._sem(k), v
                    engine.wait_ge(s_, v_)
                ins = fn(engine)
                if evt[0] == "d":
                    ins.then_inc(self._sem(evt[1]), 16)
                elif evt[1] in waited[name]:
                    s_, _ = csem(name, evt[1])
                    ins.then_inc(s_, 1)
            if name == "sp":
                for (s_, v_) in final:
                    engine.wait_ge(s_, v_)
                for e2, (s_, v_) in last.items():
                    if e2 != "sp":
                        engine.wait_ge(s_, v_)

        with nc.Block() as block:
            @block.sync
            def _(eng):
                run(eng, "sp")

            @block.scalar
            def _(eng):
                run(eng, "act")

            @block.vector
            def _(eng):
                run(eng, "dve")

            @block.gpsimd
            def _(eng):
                run(eng, "pool")

            @block.tensor
            def _(eng):
                run(eng, "pe")


class Ring:
    def __init__(self, nc, st, name, shape, dtype, n):
        self.items = []
        for i in range(n):
            t = st.enter_context(nc.sbuf_tensor("rg_%s%d" % (name, i), shape, dtype))
            self.items.append((t, Buf()))
        self.i = 0

    def get(self):
        it = self.items[self.i % len(self.items)]
        self.i += 1
        return it


def weight_order(seq_lens, depth):
    order = []
    for l in range(depth):
        for S in seq_lens:
            nb = S // BLK
            for _ in range(nb):
                order += [("in", l, 512), ("in", l, 1024), ("in", l, 2560), ("in", l, 3072), ("in", l, 3584)]
            for _ in range(nb):
                order += [("in", l, 0), ("in", l, 1536), ("in", l, 2048), ("in", l, 4096), ("in", l, 4608)]
                for q in range(2):
                    order += [("in", l, 5120 + 512 * q), ("in", l, 6144 + 512 * q), ("pa", l, 512 * q), ("pb", l, 512 * q)]
                for q in range(2):
                    order += [("wo", l, 512 * q)]
    return order


def build_nc(seq_lens, depth):
    nseq = len(seq_lens)
    SMAX = max(seq_lens)
    nc = bass.Bass("TRN2", target_bir_lowering=False)

    def din(name, shape, dt=F32):
        return nc.dram_tensor(name, list(shape), dt, kind="ExternalInput").ap()

    xin = [din("x%d" % i, [D, S]) for i, S in enumerate(seq_lens)]
    yout = [nc.dram_tensor("y%d" % i, [D, S], F32, kind="ExternalOutput").ap() for i, S in enumerate(seq_lens)]
    cT = din("cT", [128, 8, nseq])
    w_ada = din("w_ada", [depth, D, 3 * D])
    b_adaT = din("b_adaT", [depth, 128, 24])
    norm_gT = din("norm_gT", [depth, 128, 8])
    w_in = din("w_in", [depth, D, IN_W])
    qkg = din("qkg", [128, 2 * depth])
    dlog = din("dlog", [128, 8 * depth])
    retgT = din("retgT", [depth, 128, 8])
    w_pa = din("w_pa", [depth, 512, D])
    w_pb = din("w_pb", [depth, D, D])
    w_o = din("w_o", [depth, D, D])
    cmat = din("cmat", [6, 128, 128])
    cmask = din("cmask", [NREL, 128, 512])
    cret = din("cret", [5, 128, 128])
    ccol = din("ccol", [128, 8])
    tabs = {}
    for S in sorted(set(seq_lens)):
        tabs[S] = din("tab%d" % S, [4, 128, S])

    def dscr(name, shape, dt):
        return nc.dram_tensor(name, list(shape), dt).ap()

    XS = [dscr("xs%d" % i, [D, S], F32) for i, S in enumerate(seq_lens)]
    HTd = dscr("HTd", [8, 128, SMAX], BF16)
    KTd = dscr("KTd", [4, 128, SMAX], BF16)
    VAd = dscr("VAd", [SMAX // 128, 128, 528], BF16)
    KRd = dscr("KRd", [4, 128, SMAX], BF16)
    VRd = dscr("VRd", [SMAX // 128, 128, 1024], BF16)
    SBd = dscr("SBd", [SMAX // 128, 128, 1024], BF16)
    B_XS = [Buf(dram=True) for _ in seq_lens]
    B_HTd, B_KTd, B_VAd, B_KRd, B_VRd, B_SBd = (Buf(dram=True) for _ in range(6))

    worder = weight_order(seq_lens, depth)

    with contextlib.ExitStack() as st:
        S_ = Sched(nc, st)
        op, dma = S_.op, S_.dma

        def I(name, *a, **kw):
            return lambda e: getattr(e, name)(*a, **kw)

        def sb(name, shape, dt):
            return st.enter_context(nc.sbuf_tensor("sb_" + name, list(shape), dt)), Buf()

        cm, B_cm = sb("cm", [128, 6, 128], BF16)
        for i in range(5):
            dma("pool", I("dma_start", out=cm[:, i, :], in_=cmat[i]), writes=[B_cm])
        ident, ones_blk, ones_full, RaT, RrT = (cm[:, i, :] for i in range(5))
        onesf, B_onesf = sb("onesf", [128, 64], F32)
        op("dve", I("memset", onesf[:], 1.0), writes=[B_onesf])
        masks, B_masks = sb("masks", [128, NREL, 512], BF16)
        for i in range(NREL):
            dma("pool", I("dma_start", out=masks[:, i, :], in_=cmask[i]), writes=[B_masks])
        cr, B_cr = sb("cr", [128, 5, 128], F32)
        dma("sp", I("dma_start", out=cr[:], in_=cret.rearrange("c p n -> p c n")), writes=[B_cr])
        cc_, B_cc = sb("ccol", [128, 8], F32)
        dma("sp", I("dma_start", out=cc_[:], in_=ccol), writes=[B_cc])
        onecol, epscol = cc_[:, 2:3], cc_[:, 3:4]
        qkg_s, B_qkg = sb("qkg", [128, 2 * depth], F32)
        dma("sp", I("dma_start", out=qkg_s[:], in_=qkg), writes=[B_qkg])
        lg, B_lg = sb("lg", [128, 8 * depth], F32)
        dma("sp", I("dma_start", out=lg[:], in_=dlog), writes=[B_lg])
        op("act", I("activation", out=lg[:], in_=lg[:], func=AF.Exp, scale=-1.0), reads=[B_lg], writes=[B_lg])
        op("act", I("activation", out=lg[:], in_=lg[:], func=AF.Ln, bias=onecol, scale=1.0), reads=[B_lg, B_cc], writes=[B_lg])
        op("dve", I("tensor_scalar", out=lg[:], in0=lg[:], scalar1=-1.0, scalar2=None, op0=ALU.mult), reads=[B_lg], writes=[B_lg])
        csil, B_csil = sb("csil", [128, 8, nseq], F32)
        dma("sp", I("dma_start", out=csil[:], in_=cT), writes=[B_csil])
        op("act", I("activation", out=csil[:], in_=csil[:], func=AF.Silu), reads=[B_csil], writes=[B_csil])

        modT, B_mod = sb("modT", [128, 24, nseq], F32)
        gmod, B_gmod = sb("gmod", [128, 8, nseq], F32)
        bada, B_bada = sb("bada", [128, 24], F32)
        ngT, B_ngT = sb("ngT", [128, 8], F32)
        rgT, B_rgT = sb("rgT", [128, 8], F32)
        MT, B_MT = sb("MT", [128, 4, 128], BF16)
        qdec, B_qdec = sb("qdec", [128, 8, 128], F32)
        kcol, B_kcol = sb("kcol", [128, 8], F32)
        gC, B_gC = sb("gC", [128, 8], F32)
        wada, B_wada = sb("wada", [128, 8, 128], F32)
        tmpd, B_tmpd = sb("tmpd", [128, 128], F32)
        tmpd2, B_tmpd2 = sb("tmpd2", [128, 128], F32)

        Sf, B_Sf = sb("Sf", [128, 4, 256], F32)
        Sfb, B_Sfb = sb("Sfb", [128, 4, 256], BF16)
        Sb, B_Sb = Sf, B_Sf

        WR = Ring(nc, st, "wr", [128, 8, 512], BF16, 5)
        wstate = {"i": 0, "slots": {}}
        PREFETCH = 1

        def wload(idx):
            kind, l, c0 = worder[idx]
            t, B = WR.get()
            if kind == "in":
                src = w_in[l].rearrange("(kc p) n -> p kc n", p=128)[:, :, c0:c0 + 512]
                dma("pool", I("dma_start", out=t[:], in_=src), writes=[B])
            elif kind == "pb":
                src = w_pb[l].rearrange("(kc p) n -> p kc n", p=128)[:, :, c0:c0 + 512]
                dma("pool", I("dma_start", out=t[:], in_=src), writes=[B])
            elif kind == "wo":
                src = w_o[l].rearrange("(kc p) n -> p kc n", p=128)[:, :, c0:c0 + 512]
                dma("pool", I("dma_start", out=t[:], in_=src), writes=[B])
            else:
                src = w_pa[l].rearrange("(h p) n -> p h n", p=64)[:, :, c0:c0 + 512]
                dma("pool", I("dma_start", out=t[0:64, :, :], in_=src), writes=[B])
            wstate["slots"][idx] = (t, B)

        def wnext(kind, l, c0):
            i = wstate["i"]
            assert worder[i] == (kind, l, c0), (worder[i], kind, l, c0)
            while wstate.get("loaded", 0) < min(len(worder), i + PREFETCH + 1):
                wload(wstate.get("loaded", 0))
                wstate["loaded"] = wstate.get("loaded", 0) + 1
            wstate["i"] = i + 1
            return wstate["slots"].pop(i)

        PS = []
        for i in range(7):
            t = st.enter_context(nc.psum_tensor("ps%d" % i, [128, 512], F32))
            PS.append((t, Buf(psum=True)))
        psT = (st.enter_context(nc.psum_tensor("psT", [128, 1024], BF16)), Buf(psum=True))
        pj = {"i": 0}

        def pproj():
            it = PS[pj["i"] % 2]
            pj["i"] += 1
            return it
        pStat, pRot, pSc0, pSc1, pO = PS[2], PS[3], PS[4], PS[5], PS[6]
        sc = {"i": 0}

        def pscore():
            it = (pSc0, pSc1)[sc["i"] % 2]
            sc["i"] += 1
            return it

        HT, B_HT = sb("HT", [128, 8, 512], BF16)
        TB, B_TB = sb("TB", [128, 4, 512], F32)
        QT, B_QT = sb("QT", [128, 4, 512], BF16)
        QR, B_QR = sb("QR", [128, 4, 512], BF16)
        KTw = Ring(nc, st, "ktw", [128, 2560], BF16, 2)
        VAr = Ring(nc, st, "va", [128, NREL, 132], BF16, 2)
        YAG, B_YAG = sb("YAG", [64, 8, 512], BF16)
        YRG, B_YRG = sb("YRG", [128, 8, 512], BF16)
        VRb, B_VRb = sb("VRb", [128, 4, 1024], BF16)
        MG, B_MG = VRb[:].rearrange("p t (a n) -> p (t a) n", n=512), B_VRb
        KRb, B_KRb = sb("KRb", [128, 4, 512], BF16)
        SBlR = Ring(nc, st, "sbl", [128, 4, 256], BF16, 2)
        KD, B_KD = sb("KD", [128, 4, 512], BF16)
        rstd, B_rstd = sb("rstd", [128, 512], F32)
        R32 = Ring(nc, st, "r32", [128, 512], F32, 6)
        R16 = Ring(nc, st, "r16", [128, 512], BF16, 6)
        PR = Ring(nc, st, "pr", [128, 512], BF16, 3)
        SM = Ring(nc, st, "sm", [128, 128], BF16, 3)
        QF = Ring(nc, st, "qf", [128, 256], BF16, 3)
        SBst = Ring(nc, st, "sbst", [128, 1024], BF16, 2)
        VAst = Ring(nc, st, "vast", [128, 528], BF16, 2)
        XO = Ring(nc, st, "xo", [128, 512], F32, 2)
        XC = Ring(nc, st, "xc", [128, 512], F32, 2)
        for (t, B) in VAst.items:
            op("pool", I("memset", t[:], 1.0), writes=[B])

        def mm(out, lhsT, rhs, start, stop, reads, writes, **kw):
            op("pe", I("matmul", out, lhsT=lhsT, rhs=rhs, start=start, stop=stop, **kw), reads=reads, writes=writes)

        def rsqrt_ps(dst, B_dst, ps, B_ps, scale):
            op("act", I("activation", out=dst, in_=ps, func=AF.Sqrt, bias=epscol, scale=scale), reads=[B_ps, B_cc], writes=[B_dst])
            op("dve", I("reciprocal", out=dst, in_=dst), reads=[B_dst], writes=[B_dst])

        def proj_fm(Wt, B_W, c0, ncols, rhs_of_kc, B_rhs, nk=8, prow=128):
            pt, B_p = pproj()
            for kc in range(nk):
                mm(pt[0:ncols, :], Wt[0:prow, kc, c0:c0 + ncols], rhs_of_kc(kc), kc == 0, kc == nk - 1, [B_W, B_rhs], [B_p])
            return pt, B_p

        def rope(dst, B_dst, src, B_src, RT, tC, tS, mul=None, src_is_psum=True):
            sbf, B_sbf = R16.get()
            op("act", I("activation", out=sbf[:], in_=src, func=AF.Copy), reads=[B_src], writes=[B_sbf])
            mm(pRot[0][:], RT, sbf[:], True, True, [B_cm, B_sbf], [pRot[1]])
            t1, B_t1 = R32.get()
            t2, B_t2 = R32.get()
            if mul is None:
                op("dve", I("tensor_tensor", out=t1[:], in0=src, in1=tC, op=ALU.mult), reads=[B_src, B_TB], writes=[B_t1])
                op("dve", I("tensor_tensor", out=t2[:], in0=pRot[0][:], in1=tS, op=ALU.mult), reads=[pRot[1], B_TB], writes=[B_t2])
            else:
                op("dve", I("scalar_tensor_tensor", out=t1[:], in0=src, scalar=mul, in1=tC, op0=ALU.mult, op1=ALU.mult), reads=[B_src, B_TB, B_cc], writes=[B_t1])
                op("dve", I("scalar_tensor_tensor", out=t2[:], in0=pRot[0][:], scalar=mul, in1=tS, op0=ALU.mult, op1=ALU.mult), reads=[pRot[1], B_TB, B_cc], writes=[B_t2])
            op("pool", I("tensor_tensor", out=dst, in0=t1[:], in1=t2[:], op=ALU.add), reads=[B_t1, B_t2], writes=[B_dst])

        def qknorm_rope(dst, B_dst, pt, B_p, gcol):
            sq, B_sq = R16.get()
            op("act", I("activation", out=sq[:], in_=pt[:], func=AF.Square), reads=[B_p], writes=[B_sq])
            mm(pStat[0][:], ones_blk, sq[:], True, True, [B_cm, B_sq], [pStat[1]])
            rs, B_rs = R32.get()
            rsqrt_ps(rs[:], B_rs, pStat[0][:], pStat[1], 1.0 / 64)
            kn, B_kn = R32.get()
            op("dve", I("scalar_tensor_tensor", out=kn[:], in0=pt[:], scalar=gcol, in1=rs[:], op0=ALU.mult, op1=ALU.mult), reads=[B_p, B_rs, B_qkg], writes=[B_kn])
            rope(dst, B_dst, kn[:], B_kn, RaT, TB[:, 0, :], TB[:, 1, :])

        def ktranspose(src_of_h, B_src, n, colsel):
            for h in range(4):
                op("pe", I("transpose", psT[0][:, h * 128:(h + 1) * 128], src_of_h(h), ident), reads=[B_src, B_cm], writes=[psT[1]])
            for h in range(4):
                op("act", I("activation", out=KD[:, n, h * 128:(h + 1) * 128], in_=psT[0][:, h * 128:(h + 1) * 128], func=AF.Copy, scale=kcol[:, colsel + h:colsel + h + 1]), reads=[psT[1], B_kcol], writes=[B_KD])

        import os
        KSTOP = int(os.environ.get("KSTOP", "0"))

        class _Stop(Exception):
            pass
        try:
          for l in range(depth):
              dma("sp", I("dma_start", out=bada[:], in_=b_adaT[l]), writes=[B_bada])
              dma("sp", I("dma_start", out=ngT[:], in_=norm_gT[l]), writes=[B_ngT])
              dma("sp", I("dma_start", out=rgT[:], in_=retgT[l]), writes=[B_rgT])
              for fc in range(24):
                  if True:
                      dma("sp", I("dma_start", out=wada[:], in_=w_ada[l].rearrange("(kc p) n -> p kc n", p=128)[:, :, fc * 128:(fc + 1) * 128]), writes=[B_wada])
                      pt, B_p = pproj()
                      for kc in range(8):
                          mm(pt[:, 0:nseq], wada[:, kc, :], csil[:, kc, :], kc == 0, kc == 7, [B_wada, B_csil], [B_p])
                      op("dve", I("tensor_scalar", out=modT[:, fc, :], in0=pt[:, 0:nseq], scalar1=bada[:, fc:fc + 1], scalar2=None, op0=ALU.add), reads=[B_p, B_bada], writes=[B_mod])
              for c in range(8):
                  op("dve", I("tensor_scalar", out=gmod[:, c, :], in0=modT[:, 8 + c, :], scalar1=1.0, scalar2=ngT[:, c:c + 1], op0=ALU.add, op1=ALU.mult), reads=[B_mod, B_ngT], writes=[B_gmod])
              lo = 8 * l
              for h in range(4):
                  op("dve", I("tensor_scalar", out=tmpd[:], in0=cr[:, 0, :], scalar1=lg[:, lo + h:lo + h + 1], scalar2=None, op0=ALU.mult), reads=[B_cr, B_lg], writes=[B_tmpd])
                  op("dve", I("scalar_tensor_tensor", out=tmpd2[:], in0=cr[:, 1, :], scalar=lg[:, lo + 4 + h:lo + 5 + h], in1=tmpd[:], op0=ALU.mult, op1=ALU.add), reads=[B_cr, B_lg, B_tmpd], writes=[B_tmpd2])
                  op("act", I("activation", out=tmpd2[:], in_=tmpd2[:], func=AF.Exp), reads=[B_tmpd2], writes=[B_tmpd2])
                  op("dve", I("tensor_tensor", out=MT[:, h, :], in0=tmpd2[:], in1=cr[:, 2, :], op=ALU.mult), reads=[B_tmpd2, B_cr], writes=[B_MT])
                  op("act", I("activation", out=qdec[:, h, :], in_=cr[:, 3, :], func=AF.Exp, scale=lg[:, lo + h:lo + h + 1]), reads=[B_cr, B_lg], writes=[B_qdec])
                  op("act", I("activation", out=qdec[:, 4 + h, :], in_=cr[:, 4, :], func=AF.Exp, scale=lg[:, lo + 4 + h:lo + 5 + h]), reads=[B_cr, B_lg], writes=[B_qdec])
                  op("act", I("activation", out=kcol[:, h:h + 1], in_=cc_[:, 0:1], func=AF.Exp, scale=lg[:, lo + h:lo + h + 1]), reads=[B_cc, B_lg], writes=[B_kcol])
                  op("act", I("activation", out=kcol[:, 4 + h:5 + h], in_=cc_[:, 1:2], func=AF.Exp, scale=lg[:, lo + 4 + h:lo + 5 + h]), reads=[B_cc, B_lg], writes=[B_kcol])
              op("act", I("activation", out=gC[:], in_=lg[:, lo:lo + 8], func=AF.Exp, scale=128.0), reads=[B_lg], writes=[B_gC])
              if KSTOP == 1:
                  raise _Stop()
              qg = qkg_s[:, 2 * l:2 * l + 1]
              kg = qkg_s[:, 2 * l + 1:2 * l + 2]

              for si, S in enumerate(seq_lens):
                  nb = S // BLK
                  xsrc = xin[si] if l == 0 else XS[si]
                  B_xsrc = None if l == 0 else B_XS[si]
                  xdst = yout[si] if l == depth - 1 else XS[si]
                  B_xdst = None if l == depth - 1 else B_XS[si]
                  tab = tabs[S]
                  rd_x = [B_xsrc] if B_xsrc is not None else []

                  op("dve", I("memset", Sb[:], 0.0), writes=[B_Sb])
                  for b in reversed(range(nb)):
                      t0 = b * BLK
                      dma("sp", I("dma_start", out=TB[:], in_=tab.rearrange("k p t -> p k t")[:, :, t0:t0 + BLK]), writes=[B_TB])
                      for c in range(8):
                          sq, B_sq = R16.get()
                          xc, B_xc = XC.get()
                          dma("sp", I("dma_start", out=xc[:], in_=xsrc[c * 128:(c + 1) * 128, t0:t0 + BLK]), reads=rd_x, writes=[B_xc])
                          op("act", I("activation", out=sq[:], in_=xc[:], func=AF.Square), reads=[B_xc], writes=[B_sq])
                          mm(pStat[0][:], ones_full, sq[:], c == 0, c == 7, [B_cm, B_sq], [pStat[1]])
                      rsqrt_ps(rstd[:], B_rstd, pStat[0][:], pStat[1], 1.0 / D)
                      for c in range(8):
                          t1, B_t1 = R32.get()
                          xc, B_xc = XC.get()
                          dma("sp", I("dma_start", out=xc[:], in_=xsrc[c * 128:(c + 1) * 128, t0:t0 + BLK]), reads=rd_x, writes=[B_xc])
                          op("dve", I("scalar_tensor_tensor", out=t1[:], in0=xc[:], scalar=gmod[:, c, si:si + 1], in1=rstd[:], op0=ALU.mult, op1=ALU.mult), reads=[B_xc, B_gmod, B_rstd], writes=[B_t1])
                          op("act", I("activation", out=HT[:, c, :], in_=t1[:], func=AF.Identity, bias=modT[:, c, si:si + 1], scale=1.0), reads=[B_t1, B_mod], writes=[B_HT])
                      dma("sp", I("dma_start", out=HTd.rearrange("c p t -> p c t")[:, :, t0:t0 + BLK], in_=HT[:]), reads=[B_HT], writes=[B_HTd])
                      hrhs = lambda kc: HT[:, kc, :]
                      Wt, B_W = wnext("in", l, 512)
                      for cc in range(4):
                          pt, B_p = proj_fm(Wt, B_W, cc * 128, 128, hrhs, B_HT)
                          ks, B_ks = R16.get()
                          qknorm_rope(ks[:], B_ks, pt, B_p, kg)
                          dma("sp", I("dma_start", out=KTd[cc][:, t0:t0 + BLK], in_=ks[:]), reads=[B_ks], writes=[B_KTd])
                      if KSTOP == 5:
                          raise _Stop()
                      Wt, B_W = wnext("in", l, 1024)
                      for tt in range(4):
                          pt, B_p = pproj()
                          for kc in range(8):
                              mm(pt[:], HT[:, kc, tt * 128:(tt + 1) * 128], Wt[:, kc, :], kc == 0, kc == 7, [B_HT, B_W], [B_p])
                          vs, B_vs = VAst.get()
                          op("act", I("activation", out=vs[:].rearrange("p (h e) -> p h e", e=66)[:, :, 0:64], in_=pt[:].rearrange("p (h e) -> p h e", e=64), func=AF.Copy), reads=[B_p], writes=[B_vs])
                          dma("sp", I("dma_start", out=VAd[4 * b + tt], in_=vs[:]), reads=[B_vs], writes=[B_VAd])
                      if KSTOP == 7:
                          raise _Stop()
                      Wt, B_W = wnext("in", l, 2560)
                      for h in range(4):
                          pt, B_p = proj_fm(Wt, B_W, h * 128, 128, hrhs, B_HT)
                          rope(KRb[:, h, :], B_KRb, pt[:], B_p, RrT, TB[:, 2, :], TB[:, 3, :], mul=cc_[:, 4:5])
                      dma("sp", I("dma_start", out=KRd.rearrange("h p t -> p h t")[:, :, t0:t0 + BLK], in_=KRb[:]), reads=[B_KRb], writes=[B_KRd])
                      if KSTOP == 8:
                          raise _Stop()
                      for g in range(2):
                          Wt, B_W = wnext("in", l, 3072 + 512 * g)
                          for tt in range(4):
                              pt, B_p = pproj()
                              for kc in range(8):
                                  mm(pt[:], HT[:, kc, tt * 128:(tt + 1) * 128], Wt[:, kc, :], kc == 0, kc == 7, [B_HT, B_W], [B_p])
                              op("act", I("activation", out=VRb[:, tt, g * 512:(g + 1) * 512], in_=pt[:], func=AF.Copy), reads=[B_p], writes=[B_VRb])
                      dma("sp", I("dma_start", out=VRd[4 * b:4 * b + 4].rearrange("t p n -> p t n"), in_=VRb[:]), reads=[B_VRb], writes=[B_VRd])
                      if KSTOP == 6:
                          raise _Stop()
                      for n in reversed(range(4)):
                          ktranspose(lambda h, n=n: KRb[:, h, n * 128:(n + 1) * 128], B_KRb, n, 4)
                          stg, B_stg = SBst.get()
                          op("pool", I("tensor_copy", out=stg[:], in_=Sb[:].rearrange("p h n -> p (h n)")), reads=[B_Sb], writes=[B_stg])
                          dma("sp", I("dma_start", out=SBd[4 * b + n], in_=stg[:]), reads=[B_stg], writes=[B_SBd])
                          for h in range(4):
                              pt, B_p = pproj()
                              mm(pt[:, 0:256], KD[:, n, h * 128:(h + 1) * 128], VRb[:, n, h * 256:(h + 1) * 256], True, True, [B_KD, B_VRb], [B_p])
                              op("dve", I("scalar_tensor_tensor", out=Sb[:, h, :], in0=Sb[:, h, :], scalar=gC[:, 4 + h:5 + h], in1=pt[:, 0:256], op0=ALU.mult, op1=ALU.add), reads=[B_p, B_gC, B_Sb], writes=[B_Sb])

                  if KSTOP == 2:
                      raise _Stop()
                  op("dve", I("memset", Sf[:], 0.0), writes=[B_Sf])
                  op("pool", I("memset", Sfb[:], 0.0), writes=[B_Sfb])
                  ntile = S // 128
                  for b in range(nb):
                      t0 = b * BLK
                      dma("sp", I("dma_start", out=HT[:], in_=HTd.rearrange("c p t -> p c t")[:, :, t0:t0 + BLK]), reads=[B_HTd], writes=[B_HT])
                      dma("sp", I("dma_start", out=TB[:], in_=tab.rearrange("k p t -> p k t")[:, :, t0:t0 + BLK]), writes=[B_TB])
                      ulo = max(0, 4 * b - 8)
                      uhi = min(ntile, 4 * b + 12)
                      dma("sp", I("dma_start", out=VRb[:], in_=VRd[4 * b:4 * b + 4].rearrange("t p n -> p t n")), reads=[B_VRd], writes=[B_VRb])
                      dma("sp", I("dma_start", out=KRb[:], in_=KRd.rearrange("h p t -> p h t")[:, :, t0:t0 + BLK]), reads=[B_KRd], writes=[B_KRb])
                      hrhs = lambda kc: HT[:, kc, :]
                      Wt, B_W = wnext("in", l, 0)
                      for cc in range(4):
                          pt, B_p = proj_fm(Wt, B_W, cc * 128, 128, hrhs, B_HT)
                          qknorm_rope(QT[:, cc, :], B_QT, pt, B_p, qg)
                      Wz, B_Wz = wnext("in", l, 1536)
                      po_alt = [pO, pStat]
                      for cc in range(4):
                          ktw, B_ktw = KTw.get()
                          VA, B_VA = VAr.get()
                          dma("sp", I("dma_start", out=ktw[:, 0:(uhi - ulo) * 128], in_=KTd[cc][:, ulo * 128:uhi * 128]), reads=[B_KTd], writes=[B_ktw])
                          dma("sp", I("dma_start", out=VA[:, 0:uhi - ulo, :], in_=VAd[ulo:uhi].rearrange("t p n -> p t n")[:, :, cc * 132:(cc + 1) * 132]), reads=[B_VAd], writes=[B_VA])
                          for hp in range(2):
                              h = 2 * cc + hp
                              rows = slice(hp * 64, hp * 64 + 64)
                              pOh, B_pOh = po_alt[h % 2]
                              pt, B_p = proj_fm(Wz, B_Wz, h * 64, 64, hrhs, B_HT)
                              sz, B_sz = R32.get()
                              op("act", I("activation", out=sz[0:64, :], in_=pt[0:64, :], func=AF.Silu), reads=[B_p], writes=[B_sz])
                              tiles = list(range(ulo, uhi))
                              LA = 2
                              sbufs = {}

                              def qk(u):
                                  ps_s, B_s = pscore()
                                  mm(ps_s[:], ktw[rows, (u - ulo) * 128:(u - ulo + 1) * 128], QT[rows, cc, :], True, True, [B_ktw, B_QT], [B_s])
                                  sbufs[u] = (ps_s, B_s)
                              for u in tiles[:LA]:
                                  qk(u)
                              for i, u in enumerate(tiles):
                                  ps_s, B_s = sbufs.pop(u)
                                  pr, B_pr = PR.get()
                                  op("act", I("activation", out=pr[:], in_=ps_s[:], func=AF.Exp, scale=0.125), reads=[B_s], writes=[B_pr])
                                  rel = u - 4 * b + 8
                                  op("dve", I("tensor_tensor", out=pr[:], in0=pr[:], in1=masks[:, rel, :], op=ALU.mult), reads=[B_pr, B_masks], writes=[B_pr])
                                  if i + LA < len(tiles):
                                      qk(tiles[i + LA])
                                  mm(pOh[0:65, :], VA[:, u - ulo, hp * 66:hp * 66 + 65], pr[:], u == ulo, u == uhi - 1, [B_VA, B_pr], [B_pOh])
                              rd, B_rd = R32.get()
                              op("dve", I("reciprocal", out=rd[64:65, :], in_=pOh[64:65, :]), reads=[B_pOh], writes=[B_rd])
                              mm(pRot[0][0:64, :], onesf[64:65, 0:64], rd[64:65, :], True, True, [B_onesf, B_rd], [pRot[1]])
                              bc, B_bc = R32.get()
                              op("act", I("activation", out=bc[0:64, :], in_=pRot[0][0:64, :], func=AF.Copy), reads=[pRot[1]], writes=[B_bc])
                              ya, B_ya = R32.get()
                              op("dve", I("tensor_tensor", out=ya[0:64, :], in0=pOh[0:64, :], in1=bc[0:64, :], op=ALU.mult), reads=[B_pOh, B_bc], writes=[B_ya])
                              op("pool", I("tensor_tensor", out=YAG[:, h, :], in0=ya[0:64, :], in1=sz[0:64, :], op=ALU.mult), reads=[B_ya, B_sz], writes=[B_YAG])
                      if KSTOP == 3:
                          raise _Stop()
                      Wt, B_W = wnext("in", l, 2048)
                      for h in range(4):
                          pt, B_p = proj_fm(Wt, B_W, h * 128, 128, hrhs, B_HT)
                          rope(QR[:, h, :], B_QR, pt[:], B_p, RrT, TB[:, 2, :], TB[:, 3, :])
                      Wzr = [wnext("in", l, 4096), wnext("in", l, 4608)]
                      for n in range(4):
                          ktranspose(lambda h, n=n: KRb[:, h, n * 128:(n + 1) * 128], B_KRb, n, 0)
                      po_list = []
                      for h in range(4):
                          pe_ = [PS[4 + (2 * h) % 3] if False else None, None]
                          po = [pscore(), pscore()]
                          SBl, B_SBl = SBlR.get()
                          dma("sp", I("dma_start", out=SBl[:], in_=SBd[4 * b:4 * b + 4].rearrange("t p n -> p t n")[:, :, h * 256:(h + 1) * 256]), reads=[B_SBd], writes=[B_SBl])
                          for n in range(4):
                              cs = slice(n * 128, (n + 1) * 128)
                              ps_s, B_s = pproj()
                              mm(ps_s[:, 0:128], KRb[:, h, cs], QR[:, h, cs], True, True, [B_KRb, B_QR], [B_s])
                              smt, B_sm = SM.get()
                              op("dve", I("tensor_tensor", out=smt[:], in0=ps_s[:, 0:128], in1=MT[:, h, :], op=ALU.mult), reads=[B_s, B_MT], writes=[B_sm])
                              qf, B_qf = QF.get()
                              op("pool", I("tensor_tensor", out=qf[:, 0:128], in0=QR[:, h, cs], in1=qdec[:, h, :], op=ALU.mult), reads=[B_QR, B_qdec], writes=[B_qf])
                              op("pool", I("tensor_tensor", out=qf[:, 128:256], in0=QR[:, h, cs], in1=qdec[:, 4 + h, :], op=ALU.mult), reads=[B_QR, B_qdec], writes=[B_qf])
                              for ev in range(2):
                                  vs_ = slice(h * 256 + ev * 128, h * 256 + ev * 128 + 128)
                                  pt_o, B_o = po[ev]
                                  mm(pt_o[:, cs], VRb[:, n, vs_], smt[:], True, False, [B_VRb, B_sm], [B_o])
                                  mm(pt_o[:, cs], Sfb[:, h, ev * 128:(ev + 1) * 128], qf[:, 0:128], False, False, [B_Sfb, B_qf], [B_o])
                                  mm(pt_o[:, cs], SBl[:, n, ev * 128:(ev + 1) * 128], qf[:, 128:256], False, True, [B_SBl, B_qf], [B_o])
                              pd, B_pd = pproj()
                              mm(pd[:, 0:256], KD[:, n, h * 128:(h + 1) * 128], VRb[:, n, h * 256:(h + 1) * 256], True, True, [B_KD, B_VRb], [B_pd])
                              op("dve", I("scalar_tensor_tensor", out=Sf[:, h, :], in0=Sf[:, h, :], scalar=gC[:, h:h + 1], in1=pd[:, 0:256], op0=ALU.mult, op1=ALU.add), reads=[B_pd, B_gC, B_Sf], writes=[B_Sf])
                              op("act", I("activation", out=Sfb[:, h, :], in_=Sf[:, h, :], func=AF.Copy), reads=[B_Sf], writes=[B_Sfb])
                          yb = [R16.get(), R16.get()]
                          ysq = [R16.get(), R16.get()]
                          for ev in range(2):
                              op("act", I("activation", out=yb[ev][0][:], in_=po[ev][0][:], func=AF.Copy), reads=[po[ev][1]], writes=[yb[ev][1]])
                              op("act", I("activation", out=ysq[ev][0][:], in_=po[ev][0][:], func=AF.Square), reads=[po[ev][1]], writes=[ysq[ev][1]])
                          for ev in range(2):
                              mm(pStat[0][:], ones_full, yb[ev][0][:], ev == 0, ev == 1, [B_cm, yb[ev][1]], [pStat[1]])
                          for ev in range(2):
                              mm(pRot[0][:], ones_full, ysq[ev][0][:], ev == 0, ev == 1, [B_cm, ysq[ev][1]], [pRot[1]])
                          mean, B_mean = R32.get()
                          op("dve", I("tensor_scalar", out=mean[:], in0=pStat[0][:], scalar1=1.0 / 256, scalar2=None, op0=ALU.mult), reads=[pStat[1]], writes=[B_mean])
                          msq, B_msq = R32.get()
                          op("pool", I("tensor_tensor", out=msq[:], in0=mean[:], in1=mean[:], op=ALU.mult), reads=[B_mean], writes=[B_msq])
                          var, B_var = R32.get()
                          op("dve", I("scalar_tensor_tensor", out=var[:], in0=pRot[0][:], scalar=1.0 / 256, in1=msq[:], op0=ALU.mult, op1=ALU.subtract), reads=[pRot[1], B_msq], writes=[B_var])
                          op("act", I("activation", out=var[:], in_=var[:], func=AF.Sqrt, bias=epscol, scale=1.0), reads=[B_var, B_cc], writes=[B_var])
                          op("dve", I("reciprocal", out=var[:], in_=var[:]), reads=[B_var], writes=[B_var])
                          for ev in range(2):
                              c8 = 2 * h + ev
                              Wzt, B_Wzt = Wzr[c8 // 4]
                              pt, B_p = proj_fm(Wzt, B_Wzt, (c8 % 4) * 128, 128, hrhs, B_HT)
                              sz, B_sz = R32.get()
                              op("act", I("activation", out=sz[:], in_=pt[:], func=AF.Silu), reads=[B_p], writes=[B_sz])
                              d1, B_d1 = R32.get()
                              op("dve", I("tensor_tensor", out=d1[:], in0=po[ev][0][:], in1=mean[:], op=ALU.subtract), reads=[po[ev][1], B_mean], writes=[B_d1])
                              op("dve", I("scalar_tensor_tensor", out=d1[:], in0=d1[:], scalar=rgT[:, c8:c8 + 1], in1=var[:], op0=ALU.mult, op1=ALU.mult), reads=[B_d1, B_var, B_rgT], writes=[B_d1])
                              op("pool", I("tensor_tensor", out=YRG[:, c8, :], in0=d1[:], in1=sz[:], op=ALU.mult), reads=[B_d1, B_sz], writes=[B_YRG])
                      if KSTOP == 4:
                          raise _Stop()
                      for q in range(2):
                          Wga, B_Wga = wnext("in", l, 5120 + 512 * q)
                          Wgr, B_Wgr = wnext("in", l, 6144 + 512 * q)
                          Wpa, B_Wpa = wnext("pa", l, 512 * q)
                          Wpb, B_Wpb = wnext("pb", l, 512 * q)
                          for j in range(4):
                              oc = 4 * q + j
                              pa_, B_pa = proj_fm(Wpa, B_Wpa, j * 128, 128, lambda hh: YAG[:, hh, :], B_YAG, nk=8, prow=64)
                              pga, B_pga = pStat
                              for kc in range(8):
                                  mm(pga[:], Wga[:, kc, j * 128:(j + 1) * 128], HT[:, kc, :], kc == 0, kc == 7, [B_Wga, B_HT], [B_pga])
                              sa, B_sa = R32.get()
                              op("act", I("activation", out=sa[:], in_=pga[:], func=AF.Sigmoid), reads=[B_pga], writes=[B_sa])
                              m1, B_m1 = R32.get()
                              op("dve", I("tensor_tensor", out=m1[:], in0=pa_[:], in1=sa[:], op=ALU.mult), reads=[B_pa, B_sa], writes=[B_m1])
                              pb_, B_pb = proj_fm(Wpb, B_Wpb, j * 128, 128, lambda c: YRG[:, c, :], B_YRG)
                              pgr, B_pgr = pRot
                              for kc in range(8):
                                  mm(pgr[:], Wgr[:, kc, j * 128:(j + 1) * 128], HT[:, kc, :], kc == 0, kc == 7, [B_Wgr, B_HT], [B_pgr])
                              sr, B_sr = R32.get()
                              op("act", I("activation", out=sr[:], in_=pgr[:], func=AF.Sigmoid), reads=[B_pgr], writes=[B_sr])
                              m2, B_m2 = R32.get()
                              op("dve", I("tensor_tensor", out=m2[:], in0=pb_[:], in1=sr[:], op=ALU.mult), reads=[B_pb, B_sr], writes=[B_m2])
                              op("pool", I("tensor_tensor", out=MG[:, oc, :], in0=m1[:], in1=m2[:], op=ALU.add), reads=[B_m1, B_m2], writes=[B_MG])
                      for q in range(2):
                          Wo, B_Wo = wnext("wo", l, 512 * q)
                          for j in range(4):
                              oc = 4 * q + j
                              xc, B_xc = XC.get()
                              dma("sp", I("dma_start", out=xc[:], in_=xsrc[oc * 128:(oc + 1) * 128, t0:t0 + BLK]), reads=rd_x, writes=[B_xc])
                              po_, B_po = proj_fm(Wo, B_Wo, j * 128, 128, lambda c: MG[:, c, :], B_MG)
                              xo, B_xo = XO.get()
                              op("dve", I("scalar_tensor_tensor", out=xo[:], in0=po_[:], scalar=modT[:, 16 + oc, si:si + 1], in1=xc[:], op0=ALU.mult, op1=ALU.add), reads=[B_po, B_xc, B_mod], writes=[B_xo])
                              wr_x = [B_xdst] if B_xdst is not None else []
                              dma("sp", I("dma_start", out=xdst[oc * 128:(oc + 1) * 128, t0:t0 + BLK], in_=xo[:]), reads=[B_xo], writes=wr_x, sbuf=B_xo)
        except _Stop:
            pass
        if not KSTOP:
            assert wstate["i"] == len(worder)
        print('sbuf_left', nc.sbuf_bytes_remaining, 'nsem', S_.nsem, {e: S_.cnt[e] for e in S_.names})
        S_.finish()
    return nc


def _tables(S):
    pos = np.arange(S, dtype=np.float32)
    p = np.arange(128)
    tab = np.zeros((4, 128, S), np.float32)
    fa = np.exp(np.float32(-math.log(500000.0)) * np.arange(8, dtype=np.float32) / np.float32(8)).astype(np.float32)
    e = p % 64
    tab[0] = 1.0
    for pp in range(128):
        ee = e[pp]
        if ee < 16:
            ang = (pos * fa[ee % 8]).astype(np.float32).astype(np.float64)
            tab[0, pp] = np.cos(ang)
            tab[1, pp] = (-np.sin(ang)) if ee < 8 else np.sin(ang)
    fr = np.exp(np.float32(-math.log(10000.0)) * np.arange(64, dtype=np.float32) / np.float32(64)).astype(np.float32)
    for pp in range(128):
        ang = (pos * fr[pp % 64]).astype(np.float32).astype(np.float64)
        tab[2, pp] = np.cos(ang)
        tab[3, pp] = (-np.sin(ang)) if pp < 64 else np.sin(ang)
    return tab


def _consts():
    p = np.arange(128)
    cmat = np.zeros((6, 128, 128), np.float32)
    cmat[0] = np.eye(128)
    cmat[1] = (p[:, None] // 64 == p[None, :] // 64)
    cmat[2] = 1.0
    for m in range(128):
        e = m % 64
        if e < 8:
            cmat[3, m + 8, m] = 1.0
        elif e < 16:
            cmat[3, m - 8, m] = 1.0
        cmat[4, (m + 64) % 128, m] = 1.0
    cmask = np.zeros((NREL, 128, 512), np.float32)
    for r in range(NREL):
        rel = r - 8
        d = 128 * rel + p[:, None] - np.arange(512)[None, :]
        ad = np.abs(d)
        cmask[r] = (ad <= 64).astype(np.float32) + ((d % 4 == 0) & (ad <= 256)) + ((d % 16 == 0) & (ad <= 1024))
    cret = np.zeros((5, 128, 128), np.float32)
    dd = (np.arange(128)[None, :] - p[:, None]).astype(np.float32)
    cret[0] = np.maximum(dd, 0)
    cret[1] = np.maximum(-dd, 0)
    cret[2] = 1.0 + (dd == 0)
    cret[3] = np.arange(128)[None, :] + 1.0
    cret[4] = 128.0 - np.arange(128)[None, :]
    ccol = np.zeros((128, 8), np.float32)
    ccol[:, 4] = 128.0 ** -0.5
    ccol[:, 0] = 127 - p
    ccol[:, 1] = p
    ccol[:, 2] = 1.0
    ccol[:, 3] = EPS
    return cmat, cmask, cret, ccol


_CACHE = {}


def kernel(x_prompt, x_sample, c_prompt, c_sample, norm_g, w_ada, b_ada, w_in, q_norm_g, k_norm_g,
           ret_decay_logit, ret_norm_g, w_proj_a, w_proj_b, w_out):
    n_cores = 8
    x_prompt = np.asarray(x_prompt, np.float32)
    x_sample = np.asarray(x_sample, np.float32)
    depth = int(np.asarray(norm_g).shape[0])
    seq_lens = [x_prompt.shape[1]] * NP + [x_sample.shape[1]]
    key = (tuple(seq_lens), depth)
    if key not in _CACHE:
        _CACHE[key] = build_nc(seq_lens, depth)
    nc = _CACHE[key]
    cmat, cmask, cret, ccol = _consts()
    tabs = {S: _tables(S) for S in sorted(set(seq_lens))}
    f = lambda a: np.ascontiguousarray(np.asarray(a, np.float32))

    def colT(a, n):
        a = f(a)
        return np.ascontiguousarray(a.reshape(a.shape[0], n, 128).transpose(0, 2, 1))
    qkg = np.zeros((128, 2 * depth), np.float32)
    for l in range(depth):
        qkg[:, 2 * l] = np.tile(f(q_norm_g)[l], 2)
        qkg[:, 2 * l + 1] = np.tile(f(k_norm_g)[l], 2)
    dlog = np.ascontiguousarray(np.broadcast_to(f(ret_decay_logit).reshape(1, -1), (128, 8 * depth)))
    xsT = np.ascontiguousarray(x_sample[0].T)
    common = {
        "w_ada": f(w_ada), "b_adaT": colT(b_ada, 24), "norm_gT": colT(norm_g, 8), "w_in": f(w_in),
        "qkg": qkg, "dlog": dlog, "retgT": colT(ret_norm_g, 8), "w_pa": f(w_proj_a), "w_pb": f(w_proj_b),
        "w_o": f(w_out), "cmat": cmat, "cmask": cmask, "cret": cret, "ccol": ccol,
    }
    for S, t in tabs.items():
        common["tab%d" % S] = t
    in_maps = []
    for c in range(n_cores):
        m = dict(common)
        cs = np.concatenate([f(c_prompt)[c * NP:(c + 1) * NP], f(c_sample)], axis=0)
        m["cT"] = np.ascontiguousarray(cs.reshape(NP + 1, 8, 128).transpose(2, 1, 0))
        for i in range(NP):
            m["x%d" % i] = np.ascontiguousarray(x_prompt[c * NP + i].T)
        m["x%d" % NP] = xsT
        in_maps.append(m)
    res = run_bass_kernel_spmd(nc, in_maps, core_ids=list(range(n_cores)))
    y_prompt = np.empty_like(x_prompt)
    for c in range(n_cores):
        for i in range(NP):
            y_prompt[c * NP + i] = res.results[c]["y%d" % i].T
    y_sample = np.ascontiguousarray(res.results[0]["y%d" % NP].T)[None]
    return (y_prompt, y_sample.astype(np.float32))
```

```python
import contextlib
import math
import numpy as np
import concourse.bass as bass
import concourse.mybir as mybir
from concourse.bass_utils import run_bass_kernel_spmd

F32 = mybir.dt.float32
BF16 = mybir.dt.bfloat16
ALU = mybir.AluOpType
AF = mybir.ActivationFunctionType

D = 1024
DEPTH = 4
SEQ = 2048
DEC_SEQ = 16384
NP = 4
BLK = 512
IN_W = 7168
EPS = 1e-6
EPOCH = 30000
NREL = 20


class Buf:
    __slots__ = ("w", "r", "dsem", "dram", "psum")

    def __init__(self, dram=False, psum=False):
        self.dram = dram
        self.psum = psum
        self.w = {}
        self.r = {}
        self.dsem = None


class Sched:
    def __init__(self, nc, stack):
        self.nc = nc
        self.stack = stack
        self.names = ["pe", "act", "dve", "pool", "sp"]
        self.ops = {e: [] for e in self.names}
        self.cnt = {e: 0 for e in self.names}
        self.seen = {e: {} for e in self.names}
        self.sems = {}
        self.nsem = 0
        self.dcount = {}
        self.waited = {e: set() for e in self.names}

    def _sem(self, key):
        if key not in self.sems:
            self.nsem += 1
            self.sems[key] = self.stack.enter_context(self.nc.semaphore("s%d" % self.nsem))
        return self.sems[key]

    def _deps(self, e, reads, writes):
        deps = {}
        for b in reads:
            for k, v in b.w.items():
                if deps.get(k, 0) < v:
                    deps[k] = v
            if b.psum:
                for k, v in b.r.items():
                    if k[0] == "c" and k[1] != e and deps.get(k, 0) < v:
                        deps[k] = v
        for b in writes:
            for d in (b.w, b.r):
                for k, v in d.items():
                    if deps.get(k, 0) < v:
                        deps[k] = v
        waits = []
        seen = self.seen[e]
        for k, v in deps.items():
            if k[0] == "c" and k[1] == "pe" and e == "pe":
                continue
            if k[0] == "d":
                v = self.dcount[k]
            if seen.get(k, 0) >= v:
                continue
            seen[k] = v
            if k[0] == "c":
                self.waited[k[1]].add(v)
            waits.append((k, v))
        return waits

    def op(self, e, fn, reads=(), writes=()):
        waits = self._deps(e, reads, writes)
        self.cnt[e] += 1
        n = self.cnt[e]
        key = ("c", e)
        for b in writes:
            b.w = {key: n}
            b.r = {}
        for b in reads:
            if b not in writes:
                b.r[key] = n
        self.ops[e].append((waits, fn, ("c", n)))

    def dma(self, e, fn, reads=(), writes=(), sbuf=None):
        waits = self._deps(e, reads, writes)
        tb = sbuf if sbuf is not None else (writes[0] if writes else reads[0])
        if tb.dsem is None:
            tb.dsem = ("d", id(tb))
            self.dcount[tb.dsem] = 0
        key = tb.dsem
        self.dcount[key] += 16
        val = self.dcount[key]
        for b in writes:
            if b.dram:
                b.w[key] = val
            else:
                b.w = {key: val}
            b.r = {}
        for b in reads:
            if b not in writes:
                b.r[key] = val
        self.ops[e].append((waits, fn, ("d", key)))

    def finish(self):
        nc = self.nc
        import bisect
        for e in self.names:
            if self.cnt[e]:
                self.waited[e].add(self.cnt[e])
        wl = {e: sorted(self.waited[e]) for e in self.names}

        def csem(e, n):
            r = bisect.bisect_left(wl[e], n)
            assert wl[e][r] == n
            return self._sem(("c", e, r // EPOCH)), r % EPOCH + 1
        final = [(self._sem(k), v) for k, v in self.dcount.items()]
        last = {e: csem(e, self.cnt[e]) for e in self.names if self.cnt[e]}
        ops = self.ops
        waited = self.waited

        def run(engine, name):
            for waits, fn, evt in ops[name]:
                for (k, v) in waits:
                    if k[0] == "c":
                        s_, v_ = csem(k[1], v)
                    else:
                        s_, v_ = self._sem(k), v
                    engine.wait_ge(s_, v_)
                ins = fn(engine)
                if evt[0] == "d":
                    ins.then_inc(self._sem(evt[1]), 16)
                elif evt[1] in waited[name]:
                    s_, _ = csem(name, evt[1])
                    ins.then_inc(s_, 1)
            if name == "sp":
                for (s_, v_) in final:
                    engine.wait_ge(s_, v_)
                for e2, (s_, v_) in last.items():
                    if e2 != "sp":
                        engine.wait_ge(s_, v_)

        with nc.Block() as block:
            @block.sync
            def _(eng):
                run(eng, "sp")

            @block.scalar
            def _(eng):
                run(eng, "act")

            @block.vector
            def _(eng):
                run(eng, "dve")

            @block.gpsimd
            def _(eng):
                run(eng, "pool")

            @block.tensor
            def _(eng):
                run(eng, "pe")


class Ring:
    def __init__(self, nc, st, name, shape, dtype, n):
        self.items = []
        for i in range(n):
            t = st.enter_context(nc.sbuf_tensor("rg_%s%d" % (name, i), shape, dtype))
            self.items.append((t, Buf()))
        self.i = 0

    def get(self):
        it = self.items[self.i % len(self.items)]
        self.i += 1
        return it


def weight_order(seq_lens, depth):
    order = []
    for l in range(depth):
        for S in seq_lens:
            nb = S // BLK
            for _ in range(nb):
                order += [("in", l, 512), ("in", l, 1024), ("in", l, 2560), ("in", l, 3072), ("in", l, 3584)]
            for _ in range(nb):
                order += [("in", l, 0), ("in", l, 1536), ("in", l, 2048), ("in", l, 4096), ("in", l, 4608)]
                for oc in range(8):
                    order += [("mg", l, oc)]
                for q in range(2):
                    order += [("wo", l, 512 * q)]
    return order


def build_nc(seq_lens, depth):
    nseq = len(seq_lens)
    SMAX = max(seq_lens)
    nc = bass.Bass("TRN2", target_bir_lowering=False)

    def din(name, shape, dt=F32):
        return nc.dram_tensor(name, list(shape), dt, kind="ExternalInput").ap()

    xin = [din("x%d" % i, [D, S]) for i, S in enumerate(seq_lens)]
    yout = [nc.dram_tensor("y%d" % i, [D, S], F32, kind="ExternalOutput").ap() for i, S in enumerate(seq_lens)]
    cT = din("cT", [128, 8, nseq])
    w_ada = din("w_ada", [depth, D, 3 * D])
    b_adaT = din("b_adaT", [depth, 128, 24])
    norm_gT = din("norm_gT", [depth, 128, 8])
    w_in = din("w_in", [depth, D, IN_W])
    qkg = din("qkg", [128, 2 * depth])
    dlog = din("dlog", [128, 8 * depth])
    retgT = din("retgT", [depth, 128, 8])
    w_pa = din("w_pa", [depth, 512, D])
    w_pb = din("w_pb", [depth, D, D])
    w_o = din("w_o", [depth, D, D])
    cmat = din("cmat", [6, 128, 128])
    cmask = din("cmask", [NREL, 128, 512])
    cret = din("cret", [5, 128, 128])
    ccol = din("ccol", [128, 8])
    tabs = {}
    for S in sorted(set(seq_lens)):
        tabs[S] = din("tab%d" % S, [4, 128, S])

    def dscr(name, shape, dt):
        return nc.dram_tensor(name, list(shape), dt).ap()

    XS = [dscr("xs%d" % i, [D, S], F32) for i, S in enumerate(seq_lens)]
    HTd = dscr("HTd", [8, 128, SMAX], BF16)
    KTd = dscr("KTd", [4, 128, SMAX], BF16)
    VAd = dscr("VAd", [SMAX // 128, 128, 528], BF16)
    KRd = dscr("KRd", [4, 128, SMAX], BF16)
    VRd = dscr("VRd", [SMAX // 128, 128, 1024], BF16)
    SBd = dscr("SBd", [SMAX // 128, 128, 1024], BF16)
    B_XS = [Buf(dram=True) for _ in seq_lens]
    B_HTd, B_KTd, B_VAd, B_KRd, B_VRd, B_SBd = (Buf(dram=True) for _ in range(6))

    worder = weight_order(seq_lens, depth)

    with contextlib.ExitStack() as st:
        S_ = Sched(nc, st)
        op, dma = S_.op, S_.dma

        def I(name, *a, **kw):
            return lambda e: getattr(e, name)(*a, **kw)

        def sb(name, shape, dt):
            return st.enter_context(nc.sbuf_tensor("sb_" + name, list(shape), dt)), Buf()

        cm, B_cm = sb("cm", [128, 6, 128], BF16)
        for i in range(5):
            dma("pool", I("dma_start", out=cm[:, i, :], in_=cmat[i]), writes=[B_cm])
        ident, ones_blk, ones_full, RaT, RrT = (cm[:, i, :] for i in range(5))
        onesf, B_onesf = sb("onesf", [128, 64], F32)
        op("dve", I("memset", onesf[:], 1.0), writes=[B_onesf])
        masks, B_masks = sb("masks", [128, NREL, 512], BF16)
        for i in range(NREL):
            dma("pool", I("dma_start", out=masks[:, i, :], in_=cmask[i]), writes=[B_masks])
        cr, B_cr = sb("cr", [128, 5, 128], F32)
        dma("sp", I("dma_start", out=cr[:], in_=cret.rearrange("c p n -> p c n")), writes=[B_cr])
        cc_, B_cc = sb("ccol", [128, 8], F32)
        dma("sp", I("dma_start", out=cc_[:], in_=ccol), writes=[B_cc])
        onecol, epscol = cc_[:, 2:3], cc_[:, 3:4]
        qkg_s, B_qkg = sb("qkg", [128, 2 * depth], F32)
        dma("sp", I("dma_start", out=qkg_s[:], in_=qkg), writes=[B_qkg])
        lg, B_lg = sb("lg", [128, 8 * depth], F32)
        dma("sp", I("dma_start", out=lg[:], in_=dlog), writes=[B_lg])
        op("act", I("activation", out=lg[:], in_=lg[:], func=AF.Exp, scale=-1.0), reads=[B_lg], writes=[B_lg])
        op("act", I("activation", out=lg[:], in_=lg[:], func=AF.Ln, bias=onecol, scale=1.0), reads=[B_lg, B_cc], writes=[B_lg])
        op("dve", I("tensor_scalar", out=lg[:], in0=lg[:], scalar1=-1.0, scalar2=None, op0=ALU.mult), reads=[B_lg], writes=[B_lg])
        csil, B_csil = sb("csil", [128, 8, nseq], F32)
        dma("sp", I("dma_start", out=csil[:], in_=cT), writes=[B_csil])
        op("act", I("activation", out=csil[:], in_=csil[:], func=AF.Silu), reads=[B_csil], writes=[B_csil])

        modT, B_mod = sb("modT", [128, 24, nseq], F32)
        gmod, B_gmod = sb("gmod", [128, 8, nseq], F32)
        bada, B_bada = sb("bada", [128, 24], F32)
        ngT, B_ngT = sb("ngT", [128, 8], F32)
        rgT, B_rgT = sb("rgT", [128, 8], F32)
        MT, B_MT = sb("MT", [128, 4, 128], BF16)
        qdec, B_qdec = sb("qdec", [128, 8, 128], F32)
        kcol, B_kcol = sb("kcol", [128, 8], F32)
        gC, B_gC = sb("gC", [128, 8], F32)
        wada, B_wada = sb("wada", [128, 8, 128], F32)
        tmpd, B_tmpd = sb("tmpd", [128, 128], F32)
        tmpd2, B_tmpd2 = sb("tmpd2", [128, 128], F32)

        Sf, B_Sf = sb("Sf", [128, 4, 256], F32)
        B_Sfh = [Buf() for _ in range(4)]
        Sb = Sf
        SFlR = Ring(nc, st, "sfl", [128, 4, 256], BF16, 2)

        WR = Ring(nc, st, "wr", [128, 8, 512], BF16, 5)
        wstate = {"i": 0, "slots": {}}
        PREFETCH = 3

        def wload(idx):
            kind, l, c0 = worder[idx]
            t, B = WR.get()
            if kind == "in":
                src = w_in[l].rearrange("(kc p) n -> p kc n", p=128)[:, :, c0:c0 + 512]
                dma("pool", I("dma_start", out=t[:], in_=src), writes=[B])
            elif kind == "pb":
                src = w_pb[l].rearrange("(kc p) n -> p kc n", p=128)[:, :, c0:c0 + 512]
                dma("pool", I("dma_start", out=t[:], in_=src), writes=[B])
            elif kind == "wo":
                src = w_o[l].rearrange("(kc p) n -> p kc n", p=128)[:, :, c0:c0 + 512]
                dma("pool", I("dma_start", out=t[:], in_=src), writes=[B])
            else:
                oc = c0
                win = w_in[l].rearrange("(kc p) n -> p kc n", p=128)
                dma("pool", I("dma_start", out=t[:, :, 0:128], in_=win[:, :, 5120 + 128 * oc:5248 + 128 * oc]), writes=[B])
                dma("pool", I("dma_start", out=t[:, :, 128:256], in_=win[:, :, 6144 + 128 * oc:6272 + 128 * oc]), writes=[B])
                dma("pool", I("dma_start", out=t[0:64, :, 256:384], in_=w_pa[l].rearrange("(h p) n -> p h n", p=64)[:, :, 128 * oc:128 * oc + 128]), writes=[B])
                dma("pool", I("dma_start", out=t[:, :, 384:512], in_=w_pb[l].rearrange("(kc p) n -> p kc n", p=128)[:, :, 128 * oc:128 * oc + 128]), writes=[B])
            wstate["slots"][idx] = (t, B)

        def wnext(kind, l, c0):
            i = wstate["i"]
            assert worder[i] == (kind, l, c0), (worder[i], kind, l, c0)
            while wstate.get("loaded", 0) < min(len(worder), i + PREFETCH + 1):
                wload(wstate.get("loaded", 0))
                wstate["loaded"] = wstate.get("loaded", 0) + 1
            wstate["i"] = i + 1
            return wstate["slots"].pop(i)

        PS = []
        for i in range(7):
            t = st.enter_context(nc.psum_tensor("ps%d" % i, [128, 512], F32))
            PS.append((t, Buf(psum=True)))
        psT = (st.enter_context(nc.psum_tensor("psT", [128, 1024], BF16)), Buf(psum=True))
        pj = {"i": 0}

        def pproj():
            it = PS[pj["i"] % 2]
            pj["i"] += 1
            return it
        pStat, pRot, pSc0, pSc1, pO = PS[2], PS[3], PS[4], PS[5], PS[6]
        sc = {"i": 0}

        def pscore():
            it = (pSc0, pSc1)[sc["i"] % 2]
            sc["i"] += 1
            return it

        HT, B_HT = sb("HT", [128, 8, 512], BF16)
        TB, B_TB = sb("TB", [128, 4, 512], F32)
        QT, B_QT = sb("QT", [128, 4, 512], BF16)
        QR, B_QR = sb("QR", [128, 4, 512], BF16)
        KTw = Ring(nc, st, "ktw", [128, 2560], BF16, 2)
        VAr = Ring(nc, st, "va", [128, NREL, 132], BF16, 2)
        YAG, B_YAG = sb("YAG", [64, 8, 512], BF16)
        YRG, B_YRG = sb("YRG", [128, 8, 512], BF16)
        VRb, B_VRb = sb("VRb", [128, 4, 1024], BF16)
        MG, B_MG = VRb[:].rearrange("p t (a n) -> p (t a) n", n=512), B_VRb
        KRb, B_KRb = sb("KRb", [128, 4, 512], BF16)
        SBlR = Ring(nc, st, "sbl", [128, 4, 256], BF16, 2)
        KD, B_KD = sb("KD", [128, 4, 512], BF16)
        rstd, B_rstd = sb("rstd", [128, 512], F32)
        R32 = Ring(nc, st, "r32", [128, 512], F32, 6)
        R16 = Ring(nc, st, "r16", [128, 512], BF16, 6)
        PR = Ring(nc, st, "pr", [128, 512], BF16, 3)
        SM = Ring(nc, st, "sm", [128, 128], BF16, 3)
        QF = Ring(nc, st, "qf", [128, 256], BF16, 3)
        SBst = Ring(nc, st, "sbst", [128, 1024], BF16, 2)
        VAst = Ring(nc, st, "vast", [128, 528], BF16, 2)
        SZ = Ring(nc, st, "sz", [64, 512], F32, 2)
        XO = Ring(nc, st, "xo", [128, 512], F32, 2)
        XC = Ring(nc, st, "xc", [128, 512], F32, 2)
        for (t, B) in VAst.items:
            op("pool", I("memset", t[:], 1.0), writes=[B])

        def mm(out, lhsT, rhs, start, stop, reads, writes, **kw):
            op("pe", I("matmul", out, lhsT=lhsT, rhs=rhs, start=start, stop=stop, **kw), reads=reads, writes=writes)

        def rsqrt_ps(dst, B_dst, ps, B_ps, scale):
            op("act", I("activation", out=dst, in_=ps, func=AF.Sqrt, bias=epscol, scale=scale), reads=[B_ps, B_cc], writes=[B_dst])
            op("dve", I("reciprocal", out=dst, in_=dst), reads=[B_dst], writes=[B_dst])

        def proj_fm(Wt, B_W, c0, ncols, rhs_of_kc, B_rhs, nk=8, prow=128):
            pt, B_p = pproj()
            for kc in range(nk):
                mm(pt[0:ncols, :], Wt[0:prow, kc, c0:c0 + ncols], rhs_of_kc(kc), kc == 0, kc == nk - 1, [B_W, B_rhs], [B_p])
            return pt, B_p

        def rope(dst, B_dst, src, B_src, RT, tC, tS, mul=None, src_is_psum=True):
            sbf, B_sbf = R16.get()
            op("act", I("activation", out=sbf[:], in_=src, func=AF.Copy), reads=[B_src], writes=[B_sbf])
            mm(pRot[0][:], RT, sbf[:], True, True, [B_cm, B_sbf], [pRot[1]])
            t1, B_t1 = R32.get()
            t2, B_t2 = R32.get()
            if mul is None:
                op("dve", I("tensor_tensor", out=t1[:], in0=src, in1=tC, op=ALU.mult), reads=[B_src, B_TB], writes=[B_t1])
                op("dve", I("tensor_tensor", out=t2[:], in0=pRot[0][:], in1=tS, op=ALU.mult), reads=[pRot[1], B_TB], writes=[B_t2])
            else:
                op("dve", I("scalar_tensor_tensor", out=t1[:], in0=src, scalar=mul, in1=tC, op0=ALU.mult, op1=ALU.mult), reads=[B_src, B_TB, B_cc], writes=[B_t1])
                op("dve", I("scalar_tensor_tensor", out=t2[:], in0=pRot[0][:], scalar=mul, in1=tS, op0=ALU.mult, op1=ALU.mult), reads=[pRot[1], B_TB, B_cc], writes=[B_t2])
            op("pool", I("tensor_tensor", out=dst, in0=t1[:], in1=t2[:], op=ALU.add), reads=[B_t1, B_t2], writes=[B_dst])

        def qknorm_rope(dst, B_dst, pt, B_p, gcol):
            sq, B_sq = R16.get()
            op("act", I("activation", out=sq[:], in_=pt[:], func=AF.Square), reads=[B_p], writes=[B_sq])
            mm(pStat[0][:], ones_blk, sq[:], True, True, [B_cm, B_sq], [pStat[1]])
            rs, B_rs = R32.get()
            rsqrt_ps(rs[:], B_rs, pStat[0][:], pStat[1], 1.0 / 64)
            kn, B_kn = R32.get()
            op("dve", I("scalar_tensor_tensor", out=kn[:], in0=pt[:], scalar=gcol, in1=rs[:], op0=ALU.mult, op1=ALU.mult), reads=[B_p, B_rs, B_qkg], writes=[B_kn])
            rope(dst, B_dst, kn[:], B_kn, RaT, TB[:, 0, :], TB[:, 1, :])

        def chunk_pipeline(n, proj_fn, fin_fn):
            cur = proj_fn(0)
            for c in range(n):
                nxt = proj_fn(c + 1) if c + 1 < n else None
                fin_fn(c, cur)
                cur = nxt

        def ktranspose(src_of_h, B_src, n, colsel):
            for h in range(4):
                op("pe", I("transpose", psT[0][:, h * 128:(h + 1) * 128], src_of_h(h), ident), reads=[B_src, B_cm], writes=[psT[1]])
            for h in range(4):
                op("act", I("activation", out=KD[:, n, h * 128:(h + 1) * 128], in_=psT[0][:, h * 128:(h + 1) * 128], func=AF.Copy, scale=kcol[:, colsel + h:colsel + h + 1]), reads=[psT[1], B_kcol], writes=[B_KD])

        import os
        KSTOP = int(os.environ.get("KSTOP", "0"))

        class _Stop(Exception):
            pass
        try:
          for l in range(depth):
              dma("sp", I("dma_start", out=bada[:], in_=b_adaT[l]), writes=[B_bada])
              dma("sp", I("dma_start", out=ngT[:], in_=norm_gT[l]), writes=[B_ngT])
              dma("sp", I("dma_start", out=rgT[:], in_=retgT[l]), writes=[B_rgT])
              for fc in range(24):
                  if True:
                      dma("sp", I("dma_start", out=wada[:], in_=w_ada[l].rearrange("(kc p) n -> p kc n", p=128)[:, :, fc * 128:(fc + 1) * 128]), writes=[B_wada])
                      pt, B_p = pproj()
                      for kc in range(8):
                          mm(pt[:, 0:nseq], wada[:, kc, :], csil[:, kc, :], kc == 0, kc == 7, [B_wada, B_csil], [B_p])
                      op("dve", I("tensor_scalar", out=modT[:, fc, :], in0=pt[:, 0:nseq], scalar1=bada[:, fc:fc + 1], scalar2=None, op0=ALU.add), reads=[B_p, B_bada], writes=[B_mod])
              for c in range(8):
                  op("dve", I("tensor_scalar", out=gmod[:, c, :], in0=modT[:, 8 + c, :], scalar1=1.0, scalar2=ngT[:, c:c + 1], op0=ALU.add, op1=ALU.mult), reads=[B_mod, B_ngT], writes=[B_gmod])
              lo = 8 * l
              for h in range(4):
                  op("dve", I("tensor_scalar", out=tmpd[:], in0=cr[:, 0, :], scalar1=lg[:, lo + h:lo + h + 1], scalar2=None, op0=ALU.mult), reads=[B_cr, B_lg], writes=[B_tmpd])
                  op("dve", I("scalar_tensor_tensor", out=tmpd2[:], in0=cr[:, 1, :], scalar=lg[:, lo + 4 + h:lo + 5 + h], in1=tmpd[:], op0=ALU.mult, op1=ALU.add), reads=[B_cr, B_lg, B_tmpd], writes=[B_tmpd2])
                  op("act", I("activation", out=tmpd2[:], in_=tmpd2[:], func=AF.Exp), reads=[B_tmpd2], writes=[B_tmpd2])
                  op("dve", I("tensor_tensor", out=MT[:, h, :], in0=tmpd2[:], in1=cr[:, 2, :], op=ALU.mult), reads=[B_tmpd2, B_cr], writes=[B_MT])
                  op("act", I("activation", out=qdec[:, h, :], in_=cr[:, 3, :], func=AF.Exp, scale=lg[:, lo + h:lo + h + 1]), reads=[B_cr, B_lg], writes=[B_qdec])
                  op("act", I("activation", out=qdec[:, 4 + h, :], in_=cr[:, 4, :], func=AF.Exp, scale=lg[:, lo + 4 + h:lo + 5 + h]), reads=[B_cr, B_lg], writes=[B_qdec])
                  op("act", I("activation", out=kcol[:, h:h + 1], in_=cc_[:, 0:1], func=AF.Exp, scale=lg[:, lo + h:lo + h + 1]), reads=[B_cc, B_lg], writes=[B_kcol])
                  op("act", I("activation", out=kcol[:, 4 + h:5 + h], in_=cc_[:, 1:2], func=AF.Exp, scale=lg[:, lo + 4 + h:lo + 5 + h]), reads=[B_cc, B_lg], writes=[B_kcol])
              op("act", I("activation", out=gC[:], in_=lg[:, lo:lo + 8], func=AF.Exp, scale=128.0), reads=[B_lg], writes=[B_gC])
              if KSTOP == 1:
                  raise _Stop()
              qg = qkg_s[:, 2 * l:2 * l + 1]
              kg = qkg_s[:, 2 * l + 1:2 * l + 2]

              for si, S in enumerate(seq_lens):
                  nb = S // BLK
                  xsrc = xin[si] if l == 0 else XS[si]
                  B_xsrc = None if l == 0 else B_XS[si]
                  xdst = yout[si] if l == depth - 1 else XS[si]
                  B_xdst = None if l == depth - 1 else B_XS[si]
                  tab = tabs[S]
                  rd_x = [B_xsrc] if B_xsrc is not None else []

                  op("dve", I("memset", Sb[:], 0.0), writes=B_Sfh)
                  for b in reversed(range(nb)):
                      t0 = b * BLK
                      dma("sp", I("dma_start", out=TB[:], in_=tab.rearrange("k p t -> p k t")[:, :, t0:t0 + BLK]), writes=[B_TB])
                      for c in range(8):
                          sq, B_sq = R16.get()
                          xc, B_xc = XC.get()
                          dma("sp", I("dma_start", out=xc[:], in_=xsrc[c * 128:(c + 1) * 128, t0:t0 + BLK]), reads=rd_x, writes=[B_xc])
                          op("act", I("activation", out=sq[:], in_=xc[:], func=AF.Square), reads=[B_xc], writes=[B_sq])
                          mm(pStat[0][:], ones_full, sq[:], c == 0, c == 7, [B_cm, B_sq], [pStat[1]])
                      rsqrt_ps(rstd[:], B_rstd, pStat[0][:], pStat[1], 1.0 / D)
                      for c in range(8):
                          t1, B_t1 = R32.get()
                          xc, B_xc = XC.get()
                          dma("sp", I("dma_start", out=xc[:], in_=xsrc[c * 128:(c + 1) * 128, t0:t0 + BLK]), reads=rd_x, writes=[B_xc])
                          op("dve", I("scalar_tensor_tensor", out=t1[:], in0=xc[:], scalar=gmod[:, c, si:si + 1], in1=rstd[:], op0=ALU.mult, op1=ALU.mult), reads=[B_xc, B_gmod, B_rstd], writes=[B_t1])
                          op("act", I("activation", out=HT[:, c, :], in_=t1[:], func=AF.Identity, bias=modT[:, c, si:si + 1], scale=1.0), reads=[B_t1, B_mod], writes=[B_HT])
                      dma("sp", I("dma_start", out=HTd.rearrange("c p t -> p c t")[:, :, t0:t0 + BLK], in_=HT[:]), reads=[B_HT], writes=[B_HTd])
                      hrhs = lambda kc: HT[:, kc, :]
                      Wt, B_W = wnext("in", l, 512)
                      def ka_fin(cc, pp):
                          pt, B_p = pp
                          ks, B_ks = R16.get()
                          qknorm_rope(ks[:], B_ks, pt, B_p, kg)
                          dma("sp", I("dma_start", out=KTd[cc][:, t0:t0 + BLK], in_=ks[:]), reads=[B_ks], writes=[B_KTd])
                      chunk_pipeline(4, lambda cc: proj_fm(Wt, B_W, cc * 128, 128, hrhs, B_HT), ka_fin)
                      if KSTOP == 5:
                          raise _Stop()
                      Wt, B_W = wnext("in", l, 1024)
                      for tt in range(4):
                          pt, B_p = pproj()
                          for kc in range(8):
                              mm(pt[:], HT[:, kc, tt * 128:(tt + 1) * 128], Wt[:, kc, :], kc == 0, kc == 7, [B_HT, B_W], [B_p])
                          vs, B_vs = VAst.get()
                          op("act", I("activation", out=vs[:].rearrange("p (h e) -> p h e", e=66)[:, :, 0:64], in_=pt[:].rearrange("p (h e) -> p h e", e=64), func=AF.Copy), reads=[B_p], writes=[B_vs])
                          dma("sp", I("dma_start", out=VAd[4 * b + tt], in_=vs[:]), reads=[B_vs], writes=[B_VAd])
                      if KSTOP == 7:
                          raise _Stop()
                      Wt, B_W = wnext("in", l, 2560)
                      chunk_pipeline(4, lambda h: proj_fm(Wt, B_W, h * 128, 128, hrhs, B_HT),
                                     lambda h, pp: rope(KRb[:, h, :], B_KRb, pp[0][:], pp[1], RrT, TB[:, 2, :], TB[:, 3, :], mul=cc_[:, 4:5]))
                      dma("sp", I("dma_start", out=KRd.rearrange("h p t -> p h t")[:, :, t0:t0 + BLK], in_=KRb[:]), reads=[B_KRb], writes=[B_KRd])
                      if KSTOP == 8:
                          raise _Stop()
                      for g in range(2):
                          Wt, B_W = wnext("in", l, 3072 + 512 * g)
                          for tt in range(4):
                              pt, B_p = pproj()
                              for kc in range(8):
                                  mm(pt[:], HT[:, kc, tt * 128:(tt + 1) * 128], Wt[:, kc, :], kc == 0, kc == 7, [B_HT, B_W], [B_p])
                              op("act", I("activation", out=VRb[:, tt, g * 512:(g + 1) * 512], in_=pt[:], func=AF.Copy), reads=[B_p], writes=[B_VRb])
                      dma("sp", I("dma_start", out=VRd[4 * b:4 * b + 4].rearrange("t p n -> p t n"), in_=VRb[:]), reads=[B_VRb], writes=[B_VRd])
                      if KSTOP == 6:
                          raise _Stop()
                      for n in reversed(range(4)):
                          ktranspose(lambda h, n=n: KRb[:, h, n * 128:(n + 1) * 128], B_KRb, n, 4)
                          stg, B_stg = SBst.get()
                          op("pool", I("tensor_copy", out=stg[:], in_=Sb[:].rearrange("p h n -> p (h n)")), reads=B_Sfh, writes=[B_stg])
                          dma("sp", I("dma_start", out=SBd[4 * b + n], in_=stg[:]), reads=[B_stg], writes=[B_SBd])
                          for h in range(4):
                              pt, B_p = pproj()
                              mm(pt[:, 0:256], KD[:, n, h * 128:(h + 1) * 128], VRb[:, n, h * 256:(h + 1) * 256], True, True, [B_KD, B_VRb], [B_p])
                              op("dve", I("scalar_tensor_tensor", out=Sb[:, h, :], in0=Sb[:, h, :], scalar=gC[:, 4 + h:5 + h], in1=pt[:, 0:256], op0=ALU.mult, op1=ALU.add), reads=[B_p, B_gC, B_Sfh[h]], writes=[B_Sfh[h]])

                  if KSTOP == 2:
                      raise _Stop()
                  op("dve", I("memset", Sf[:], 0.0), writes=B_Sfh)
                  ntile = S // 128
                  for b in range(nb):
                      t0 = b * BLK
                      dma("sp", I("dma_start", out=HT[:], in_=HTd.rearrange("c p t -> p c t")[:, :, t0:t0 + BLK]), reads=[B_HTd], writes=[B_HT])
                      dma("sp", I("dma_start", out=TB[:], in_=tab.rearrange("k p t -> p k t")[:, :, t0:t0 + BLK]), writes=[B_TB])
                      ulo = max(0, 4 * b - 8)
                      uhi = min(ntile, 4 * b + 12)
                      dma("sp", I("dma_start", out=VRb[:], in_=VRd[4 * b:4 * b + 4].rearrange("t p n -> p t n")), reads=[B_VRd], writes=[B_VRb])
                      dma("sp", I("dma_start", out=KRb[:], in_=KRd.rearrange("h p t -> p h t")[:, :, t0:t0 + BLK]), reads=[B_KRd], writes=[B_KRb])
                      hrhs = lambda kc: HT[:, kc, :]
                      Wt, B_W = wnext("in", l, 0)
                      chunk_pipeline(4, lambda cc: proj_fm(Wt, B_W, cc * 128, 128, hrhs, B_HT),
                                     lambda cc, pp: qknorm_rope(QT[:, cc, :], B_QT, pp[0], pp[1], qg))
                      Wz, B_Wz = wnext("in", l, 1536)
                      po_alt = [pO, pStat]

                      def load_pair(cc):
                          ktw, B_ktw = KTw.get()
                          VA, B_VA = VAr.get()
                          dma("sp", I("dma_start", out=ktw[:, 0:(uhi - ulo) * 128], in_=KTd[cc][:, ulo * 128:uhi * 128]), reads=[B_KTd], writes=[B_ktw])
                          dma("sp", I("dma_start", out=VA[:, 0:uhi - ulo, :], in_=VAd[ulo:uhi].rearrange("t p n -> p t n")[:, :, cc * 132:(cc + 1) * 132]), reads=[B_VAd], writes=[B_VA])
                          return ktw, B_ktw, VA, B_VA

                      def normalise_steps(h, pOh, B_pOh, sz, B_sz):
                          rd, B_rd = R32.get()
                          op("dve", I("reciprocal", out=rd[64:65, :], in_=pOh[64:65, :]), reads=[B_pOh], writes=[B_rd])
                          rh, B_rh = R16.get()
                          rl, B_rl = R16.get()
                          op("dve", I("tensor_copy", out=rh[64:65, :], in_=rd[64:65, :]), reads=[B_rd], writes=[B_rh])
                          op("dve", I("tensor_tensor", out=rl[64:65, :], in0=rd[64:65, :], in1=rh[64:65, :], op=ALU.subtract), reads=[B_rd, B_rh], writes=[B_rl])
                          yield
                          yield
                          mm(pRot[0][0:64, :], ones_full[64:65, 0:64], rh[64:65, :], True, False, [B_cm, B_rh], [pRot[1]])
                          mm(pRot[0][0:64, :], ones_full[64:65, 0:64], rl[64:65, :], False, True, [B_cm, B_rl], [pRot[1]])
                          yield
                          yield
                          bc, B_bc = R32.get()
                          op("act", I("activation", out=bc[0:64, :], in_=pRot[0][0:64, :], func=AF.Copy), reads=[pRot[1]], writes=[B_bc])
                          yield
                          yield
                          ya, B_ya = R32.get()
                          op("dve", I("tensor_tensor", out=ya[0:64, :], in0=pOh[0:64, :], in1=bc[0:64, :], op=ALU.mult), reads=[B_pOh, B_bc], writes=[B_ya])
                          yield
                          op("pool", I("tensor_tensor", out=YAG[:, h, :], in0=ya[0:64, :], in1=sz[0:64, :], op=ALU.mult), reads=[B_ya, B_sz], writes=[B_YAG])
                          yield

                      def run_all(gen):
                          for _ in gen:
                              pass
                      tiles = list(range(ulo, uhi))
                      nt = len(tiles)
                      items = [(h, i) for h in range(8) for i in range(nt)]
                      pairs = {0: load_pair(0)}
                      sbufs = {}
                      LA = 2

                      def qk(k):
                          h, i = items[k]
                          cc, hp = h // 2, h % 2
                          if cc not in pairs:
                              pairs[cc] = load_pair(cc)
                          ktw, B_ktw, VA, B_VA = pairs[cc]
                          rows = slice(hp * 64, hp * 64 + 64)
                          u = tiles[i]
                          ps_s, B_s = pscore()
                          mm(ps_s[:], ktw[rows, (u - ulo) * 128:(u - ulo + 1) * 128], QT[rows, cc, :], True, True, [B_ktw, B_QT], [B_s])
                          sbufs[k] = (ps_s, B_s)
                      for k in range(LA):
                          qk(k)
                      pending = None
                      cur = None
                      ngen = None
                      zpend = None
                      for k, (h, i) in enumerate(items):
                          cc, hp = h // 2, h % 2
                          u = tiles[i]
                          if i == 0:
                              if hp == 0 and cc + 1 < 4 and (cc + 1) not in pairs:
                                  pairs[cc + 1] = load_pair(cc + 1)
                              pOh, B_pOh = po_alt[h % 2]
                              pt, B_p = proj_fm(Wz, B_Wz, h * 64, 64, hrhs, B_HT)
                              sz, B_sz = SZ.get()
                              cur = (h, pOh, B_pOh, sz, B_sz)
                              zpend = (pt, B_p, sz, B_sz)
                          if i == 2:
                              op("act", I("activation", out=zpend[2][0:64, :], in_=zpend[0][0:64, :], func=AF.Silu), reads=[zpend[1]], writes=[zpend[3]])
                          ktw, B_ktw, VA, B_VA = pairs[cc]
                          ps_s, B_s = sbufs.pop(k)
                          pr, B_pr = PR.get()
                          op("act", I("activation", out=pr[:], in_=ps_s[:], func=AF.Exp, scale=0.125), reads=[B_s], writes=[B_pr])
                          rel = u - 4 * b + 8
                          op("dve", I("tensor_tensor", out=pr[:], in0=pr[:], in1=masks[:, rel, :], op=ALU.mult), reads=[B_pr, B_masks], writes=[B_pr])
                          if k + LA < len(items):
                              qk(k + LA)
                          mm(cur[1][0:65, :], VA[:, u - ulo, hp * 66:hp * 66 + 65], pr[:], i == 0, i == nt - 1, [B_VA, B_pr], [cur[2]])
                          if i == 1 and pending is not None:
                              ngen = normalise_steps(*pending)
                              pending = None
                          if ngen is not None and i >= 1:
                              if next(ngen, "end") == "end":
                                  ngen = None
                          if i == nt - 1:
                              if ngen is not None:
                                  run_all(ngen)
                                  ngen = None
                              pending = cur
                      run_all(normalise_steps(*pending))
                      if KSTOP == 3:
                          raise _Stop()
                      Wt, B_W = wnext("in", l, 2048)
                      chunk_pipeline(4, lambda h: proj_fm(Wt, B_W, h * 128, 128, hrhs, B_HT),
                                     lambda h, pp: rope(QR[:, h, :], B_QR, pp[0][:], pp[1], RrT, TB[:, 2, :], TB[:, 3, :]))
                      Wzr = [wnext("in", l, 4096), wnext("in", l, 4608)]
                      for n in range(4):
                          ktranspose(lambda h, n=n: KRb[:, h, n * 128:(n + 1) * 128], B_KRb, n, 0)
                      def stateA(h):
                          SFl, B_SFl = SFlR.get()
                          for n in range(4):
                              op("act", I("activation", out=SFl[:, n, :], in_=Sf[:, h, :], func=AF.Copy), reads=[B_Sfh[h]], writes=[B_SFl])
                              pd, B_pd = pproj()
                              mm(pd[:, 0:256], KD[:, n, h * 128:(h + 1) * 128], VRb[:, n, h * 256:(h + 1) * 256], True, True, [B_KD, B_VRb], [B_pd])
                              op("dve", I("scalar_tensor_tensor", out=Sf[:, h, :], in0=Sf[:, h, :], scalar=gC[:, h:h + 1], in1=pd[:, 0:256], op0=ALU.mult, op1=ALU.add), reads=[B_pd, B_gC, B_Sfh[h]], writes=[B_Sfh[h]])
                          return SFl, B_SFl
                      sf_next = stateA(0)
                      for h in range(4):
                          SFl, B_SFl = sf_next
                          if h + 1 < 4:
                              sf_next = stateA(h + 1)
                          po = [pscore(), pscore()]
                          SBl, B_SBl = SBlR.get()
                          dma("sp", I("dma_start", out=SBl[:], in_=SBd[4 * b:4 * b + 4].rearrange("t p n -> p t n")[:, :, h * 256:(h + 1) * 256]), reads=[B_SBd], writes=[B_SBl])
                          scd = {}

                          def scores(n):
                              cs = slice(n * 128, (n + 1) * 128)
                              ps_s, B_s = pproj()
                              mm(ps_s[:, 0:128], KRb[:, h, cs], QR[:, h, cs], True, True, [B_KRb, B_QR], [B_s])
                              smt, B_sm = SM.get()
                              op("dve", I("tensor_tensor", out=smt[:], in0=ps_s[:, 0:128], in1=MT[:, h, :], op=ALU.mult), reads=[B_s, B_MT], writes=[B_sm])
                              qf, B_qf = QF.get()
                              op("pool", I("tensor_tensor", out=qf[:, 0:128], in0=QR[:, h, cs], in1=qdec[:, h, :], op=ALU.mult), reads=[B_QR, B_qdec], writes=[B_qf])
                              op("pool", I("tensor_tensor", out=qf[:, 128:256], in0=QR[:, h, cs], in1=qdec[:, 4 + h, :], op=ALU.mult), reads=[B_QR, B_qdec], writes=[B_qf])
                              scd[n] = (smt, B_sm, qf, B_qf)
                          scores(0)
                          for n in range(4):
                              cs = slice(n * 128, (n + 1) * 128)
                              if n + 1 < 4:
                                  scores(n + 1)
                              smt, B_sm, qf, B_qf = scd.pop(n)
                              for ev in range(2):
                                  vs_ = slice(h * 256 + ev * 128, h * 256 + ev * 128 + 128)
                                  pt_o, B_o = po[ev]
                                  mm(pt_o[:, cs], VRb[:, n, vs_], smt[:], True, False, [B_VRb, B_sm], [B_o])
                                  mm(pt_o[:, cs], SFl[:, n, ev * 128:(ev + 1) * 128], qf[:, 0:128], False, False, [B_SFl, B_qf], [B_o])
                                  mm(pt_o[:, cs], SBl[:, n, ev * 128:(ev + 1) * 128], qf[:, 128:256], False, True, [B_SBl, B_qf], [B_o])
                          yb = [R16.get(), R16.get()]
                          ysq = [R16.get(), R16.get()]
                          for ev in range(2):
                              op("act", I("activation", out=yb[ev][0][:], in_=po[ev][0][:], func=AF.Copy), reads=[po[ev][1]], writes=[yb[ev][1]])
                              op("act", I("activation", out=ysq[ev][0][:], in_=po[ev][0][:], func=AF.Square), reads=[po[ev][1]], writes=[ysq[ev][1]])
                          for ev in range(2):
                              mm(pStat[0][:], ones_full, yb[ev][0][:], ev == 0, ev == 1, [B_cm, yb[ev][1]], [pStat[1]])
                          for ev in range(2):
                              mm(pRot[0][:], ones_full, ysq[ev][0][:], ev == 0, ev == 1, [B_cm, ysq[ev][1]], [pRot[1]])
                          mean, B_mean = R32.get()
                          op("dve", I("tensor_scalar", out=mean[:], in0=pStat[0][:], scalar1=1.0 / 256, scalar2=None, op0=ALU.mult), reads=[pStat[1]], writes=[B_mean])
                          msq, B_msq = R32.get()
                          op("pool", I("tensor_tensor", out=msq[:], in0=mean[:], in1=mean[:], op=ALU.mult), reads=[B_mean], writes=[B_msq])
                          var, B_var = R32.get()
                          op("dve", I("scalar_tensor_tensor", out=var[:], in0=pRot[0][:], scalar=1.0 / 256, in1=msq[:], op0=ALU.mult, op1=ALU.subtract), reads=[pRot[1], B_msq], writes=[B_var])
                          op("act", I("activation", out=var[:], in_=var[:], func=AF.Sqrt, bias=epscol, scale=1.0), reads=[B_var, B_cc], writes=[B_var])
                          op("dve", I("reciprocal", out=var[:], in_=var[:]), reads=[B_var], writes=[B_var])
                          for ev in range(2):
                              c8 = 2 * h + ev
                              Wzt, B_Wzt = Wzr[c8 // 4]
                              pt, B_p = proj_fm(Wzt, B_Wzt, (c8 % 4) * 128, 128, hrhs, B_HT)
                              sz, B_sz = R32.get()
                              op("act", I("activation", out=sz[:], in_=pt[:], func=AF.Silu), reads=[B_p], writes=[B_sz])
                              d1, B_d1 = R32.get()
                              op("dve", I("tensor_tensor", out=d1[:], in0=po[ev][0][:], in1=mean[:], op=ALU.subtract), reads=[po[ev][1], B_mean], writes=[B_d1])
                              op("dve", I("scalar_tensor_tensor", out=d1[:], in0=d1[:], scalar=rgT[:, c8:c8 + 1], in1=var[:], op0=ALU.mult, op1=ALU.mult), reads=[B_d1, B_var, B_rgT], writes=[B_d1])
                              op("pool", I("tensor_tensor", out=YRG[:, c8, :], in0=d1[:], in1=sz[:], op=ALU.mult), reads=[B_d1, B_sz], writes=[B_YRG])
                      if KSTOP == 4:
                          raise _Stop()
                      for oc in range(8):
                          Wm, B_Wm = wnext("mg", l, oc)
                          pa_, B_pa = proj_fm(Wm, B_Wm, 256, 128, lambda hh: YAG[:, hh, :], B_YAG, nk=8, prow=64)
                          pga, B_pga = pStat
                          for kc in range(8):
                              mm(pga[:], Wm[:, kc, 0:128], HT[:, kc, :], kc == 0, kc == 7, [B_Wm, B_HT], [B_pga])
                          sa, B_sa = R32.get()
                          op("act", I("activation", out=sa[:], in_=pga[:], func=AF.Sigmoid), reads=[B_pga], writes=[B_sa])
                          m1, B_m1 = R32.get()
                          op("dve", I("tensor_tensor", out=m1[:], in0=pa_[:], in1=sa[:], op=ALU.mult), reads=[B_pa, B_sa], writes=[B_m1])
                          pb_, B_pb = proj_fm(Wm, B_Wm, 384, 128, lambda c: YRG[:, c, :], B_YRG)
                          pgr, B_pgr = pRot
                          for kc in range(8):
                              mm(pgr[:], Wm[:, kc, 128:256], HT[:, kc, :], kc == 0, kc == 7, [B_Wm, B_HT], [B_pgr])
                          sr, B_sr = R32.get()
                          op("act", I("activation", out=sr[:], in_=pgr[:], func=AF.Sigmoid), reads=[B_pgr], writes=[B_sr])
                          m2, B_m2 = R32.get()
                          op("dve", I("tensor_tensor", out=m2[:], in0=pb_[:], in1=sr[:], op=ALU.mult), reads=[B_pb, B_sr], writes=[B_m2])
                          op("pool", I("tensor_tensor", out=MG[:, oc, :], in0=m1[:], in1=m2[:], op=ALU.add), reads=[B_m1, B_m2], writes=[B_MG])
                      for q in range(2):
                          Wo, B_Wo = wnext("wo", l, 512 * q)
                          for j in range(4):
                              oc = 4 * q + j
                              xc, B_xc = XC.get()
                              dma("sp", I("dma_start", out=xc[:], in_=xsrc[oc * 128:(oc + 1) * 128, t0:t0 + BLK]), reads=rd_x, writes=[B_xc])
                              po_, B_po = proj_fm(Wo, B_Wo, j * 128, 128, lambda c: MG[:, c, :], B_MG)
                              xo, B_xo = XO.get()
                              op("dve", I("scalar_tensor_tensor", out=xo[:], in0=po_[:], scalar=modT[:, 16 + oc, si:si + 1], in1=xc[:], op0=ALU.mult, op1=ALU.add), reads=[B_po, B_xc, B_mod], writes=[B_xo])
                              wr_x = [B_xdst] if B_xdst is not None else []
                              dma("sp", I("dma_start", out=xdst[oc * 128:(oc + 1) * 128, t0:t0 + BLK], in_=xo[:]), reads=[B_xo], writes=wr_x, sbuf=B_xo)
        except _Stop:
            pass
        if not KSTOP:
            assert wstate["i"] == len(worder)
        print('sbuf_left', nc.sbuf_bytes_remaining, 'nsem', S_.nsem, {e: S_.cnt[e] for e in S_.names})
        S_.finish()
    return nc


def _tables(S):
    pos = np.arange(S, dtype=np.float32)
    p = np.arange(128)
    tab = np.zeros((4, 128, S), np.float32)
    fa = np.exp(np.float32(-math.log(500000.0)) * np.arange(8, dtype=np.float32) / np.float32(8)).astype(np.float32)
    e = p % 64
    tab[0] = 1.0
    for pp in range(128):
        ee = e[pp]
        if ee < 16:
            ang = (pos * fa[ee % 8]).astype(np.float32).astype(np.float64)
            tab[0, pp] = np.cos(ang)
            tab[1, pp] = (-np.sin(ang)) if ee < 8 else np.sin(ang)
    fr = np.exp(np.float32(-math.log(10000.0)) * np.arange(64, dtype=np.float32) / np.float32(64)).astype(np.float32)
    for pp in range(128):
        ang = (pos * fr[pp % 64]).astype(np.float32).astype(np.float64)
        tab[2, pp] = np.cos(ang)
        tab[3, pp] = (-np.sin(ang)) if pp < 64 else np.sin(ang)
    return tab


def _consts():
    p = np.arange(128)
    cmat = np.zeros((6, 128, 128), np.float32)
    cmat[0] = np.eye(128)
    cmat[1] = (p[:, None] // 64 == p[None, :] // 64)
    cmat[2] = 1.0
    for m in range(128):
        e = m % 64
        if e < 8:
            cmat[3, m + 8, m] = 1.0
        elif e < 16:
            cmat[3, m - 8, m] = 1.0
        cmat[4, (m + 64) % 128, m] = 1.0
    cmask = np.zeros((NREL, 128, 512), np.float32)
    for r in range(NREL):
        rel = r - 8
        d = 128 * rel + p[:, None] - np.arange(512)[None, :]
        ad = np.abs(d)
        cmask[r] = (ad <= 64).astype(np.float32) + ((d % 4 == 0) & (ad <= 256)) + ((d % 16 == 0) & (ad <= 1024))
    cret = np.zeros((5, 128, 128), np.float32)
    dd = (np.arange(128)[None, :] - p[:, None]).astype(np.float32)
    cret[0] = np.maximum(dd, 0)
    cret[1] = np.maximum(-dd, 0)
    cret[2] = 1.0 + (dd == 0)
    cret[3] = np.arange(128)[None, :] + 1.0
    cret[4] = 128.0 - np.arange(128)[None, :]
    ccol = np.zeros((128, 8), np.float32)
    ccol[:, 4] = 128.0 ** -0.5
    ccol[:, 0] = 127 - p
    ccol[:, 1] = p
    ccol[:, 2] = 1.0
    ccol[:, 3] = EPS
    return cmat, cmask, cret, ccol


_CACHE = {}


def kernel(x_prompt, x_sample, c_prompt, c_sample, norm_g, w_ada, b_ada, w_in, q_norm_g, k_norm_g,
           ret_decay_logit, ret_norm_g, w_proj_a, w_proj_b, w_out):
    n_cores = 8
    x_prompt = np.asarray(x_prompt, np.float32)
    x_sample = np.asarray(x_sample, np.float32)
    depth = int(np.asarray(norm_g).shape[0])
    seq_lens = [x_prompt.shape[1]] * NP + [x_sample.shape[1]]
    key = (tuple(seq_lens), depth)
    if key not in _CACHE:
        _CACHE[key] = build_nc(seq_lens, depth)
    nc = _CACHE[key]
    cmat, cmask, cret, ccol = _consts()
    tabs = {S: _tables(S) for S in sorted(set(seq_lens))}
    f = lambda a: np.ascontiguousarray(np.asarray(a, np.float32))

    def colT(a, n):
        a = f(a)
        return np.ascontiguousarray(a.reshape(a.shape[0], n, 128).transpose(0, 2, 1))
    qkg = np.zeros((128, 2 * depth), np.float32)
    for l in range(depth):
        qkg[:, 2 * l] = np.tile(f(q_norm_g)[l], 2)
        qkg[:, 2 * l + 1] = np.tile(f(k_norm_g)[l], 2)
    dlog = np.ascontiguousarray(np.broadcast_to(f(ret_decay_logit).reshape(1, -1), (128, 8 * depth)))
    xsT = np.ascontiguousarray(x_sample[0].T)
    common = {
        "w_ada": f(w_ada), "b_adaT": colT(b_ada, 24), "norm_gT": colT(norm_g, 8), "w_in": f(w_in),
        "qkg": qkg, "dlog": dlog, "retgT": colT(ret_norm_g, 8), "w_pa": f(w_proj_a), "w_pb": f(w_proj_b),
        "w_o": f(w_out), "cmat": cmat, "cmask": cmask, "cret": cret, "ccol": ccol,
    }
    for S, t in tabs.items():
        common["tab%d" % S] = t
    in_maps = []
    for c in range(n_cores):
        m = dict(common)
        cs = np.concatenate([f(c_prompt)[c * NP:(c + 1) * NP], f(c_sample)], axis=0)
        m["cT"] = np.ascontiguousarray(cs.reshape(NP + 1, 8, 128).transpose(2, 1, 0))
        for i in range(NP):
            m["x%d" % i] = np.ascontiguousarray(x_prompt[c * NP + i].T)
        m["x%d" % NP] = xsT
        in_maps.append(m)
    res = run_bass_kernel_spmd(nc, in_maps, core_ids=list(range(n_cores)))
    y_prompt = np.empty_like(x_prompt)
    for c in range(n_cores):
        for i in range(NP):
            y_prompt[c * NP + i] = res.results[c]["y%d" % i].T
    y_sample = np.ascontiguousarray(res.results[0]["y%d" % NP].T)[None]
    return (y_prompt, y_sample.astype(np.float32))
```

```python
import contextlib
import math
import numpy as np
import concourse.bass as bass
import concourse.mybir as mybir
from concourse.bass_utils import run_bass_kernel_spmd

F32 = mybir.dt.float32
BF16 = mybir.dt.bfloat16
ALU = mybir.AluOpType
AF = mybir.ActivationFunctionType

D = 1024
DEPTH = 4
SEQ = 2048
DEC_SEQ = 16384
NP = 4
BLK = 512
IN_W = 7168
EPS = 1e-6
EPOCH = 30000
NREL = 20


class Buf:
    __slots__ = ("w", "r", "dsem", "dram", "psum")

    def __init__(self, dram=False, psum=False):
        self.dram = dram
        self.psum = psum
        self.w = {}
        self.r = {}
        self.dsem = None


class Sched:
    def __init__(self, nc, stack):
        self.nc = nc
        self.stack = stack
        self.names = ["pe", "act", "dve", "pool", "sp"]
        self.ops = {e: [] for e in self.names}
        self.cnt = {e: 0 for e in self.names}
        self.seen = {e: {} for e in self.names}
        self.sems = {}
        self.nsem = 0
        self.dcount = {}
        self.waited = {e: set() for e in self.names}

    def _sem(self, key):
        if key not in self.sems:
            self.nsem += 1
            self.sems[key] = self.stack.enter_context(self.nc.semaphore("s%d" % self.nsem))
        return self.sems[key]

    def _deps(self, e, reads, writes):
        deps = {}
        for b in reads:
            for k, v in b.w.items():
                if deps.get(k, 0) < v:
                    deps[k] = v
            if b.psum:
                for k, v in b.r.items():
                    if k[0] == "c" and k[1] != e and deps.get(k, 0) < v:
                        deps[k] = v
        for b in writes:
            for d in (b.w, b.r):
                for k, v in d.items():
                    if deps.get(k, 0) < v:
                        deps[k] = v
        waits = []
        seen = self.seen[e]
        for k, v in deps.items():
            if k[0] == "c" and k[1] == "pe" and e == "pe":
                continue
            if k[0] == "d":
                v = self.dcount[k]
            if seen.get(k, 0) >= v:
                continue
            seen[k] = v
            if k[0] == "c":
                self.waited[k[1]].add(v)
            waits.append((k, v))
        return waits

    def op(self, e, fn, reads=(), writes=()):
        waits = self._deps(e, reads, writes)
        self.cnt[e] += 1
        n = self.cnt[e]
        key = ("c", e)
        for b in writes:
            b.w = {key: n}
            b.r = {}
        for b in reads:
            if b not in writes:
                b.r[key] = n
        self.ops[e].append((waits, fn, ("c", n)))

    def dma(self, e, fn, reads=(), writes=(), sbuf=None):
        waits = self._deps(e, reads, writes)
        tb = sbuf if sbuf is not None else (writes[0] if writes else reads[0])
        if tb.dsem is None:
            tb.dsem = ("d", id(tb))
            self.dcount[tb.dsem] = 0
        key = tb.dsem
        self.dcount[key] += 16
        val = self.dcount[key]
        for b in writes:
            if b.dram:
                b.w[key] = val
            else:
                b.w = {key: val}
            b.r = {}
        for b in reads:
            if b not in writes:
                b.r[key] = val
        self.ops[e].append((waits, fn, ("d", key)))

    def finish(self):
        nc = self.nc
        import bisect
        for e in self.names:
            if self.cnt[e]:
                self.waited[e].add(self.cnt[e])
        wl = {e: sorted(self.waited[e]) for e in self.names}

        def csem(e, n):
            r = bisect.bisect_left(wl[e], n)
            assert wl[e][r] == n
            return self._sem(("c", e, r // EPOCH)), r % EPOCH + 1
        final = [(self._sem(k), v) for k, v in self.dcount.items()]
        last = {e: csem(e, self.cnt[e]) for e in self.names if self.cnt[e]}
        ops = self.ops
        waited = self.waited

        def run(engine, name):
            for waits, fn, evt in ops[name]:
                for (k, v) in waits:
                    if k[0] == "c":
                        s_, v_ = csem(k[1], v)
                    else:
                        s_, v_ = self._sem(k), v
                    engine.wait_ge(s_, v_)
                ins = fn(engine)
                if evt[0] == "d":
                    ins.then_inc(self._sem(evt[1]), 16)
                elif evt[1] in waited[name]:
                    s_, _ = csem(name, evt[1])
                    ins.then_inc(s_, 1)
            if name == "sp":
                for (s_, v_) in final:
                    engine.wait_ge(s_, v_)
                for e2, (s_, v_) in last.items():
                    if e2 != "sp":
                        engine.wait_ge(s_, v_)

        with nc.Block() as block:
            @block.sync
            def _(eng):
                run(eng, "sp")

            @block.scalar
            def _(eng):
                run(eng, "act")

            @block.vector
            def _(eng):
                run(eng, "dve")

            @block.gpsimd
            def _(eng):
                run(eng, "pool")

            @block.tensor
            def _(eng):
                run(eng, "pe")


class Ring:
    def __init__(self, nc, st, name, shape, dtype, n):
        self.items = []
        for i in range(n):
            t = st.enter_context(nc.sbuf_tensor("rg_%s%d" % (name, i), shape, dtype))
            self.items.append((t, Buf()))
        self.i = 0

    def get(self):
        it = self.items[self.i % len(self.items)]
        self.i += 1
        return it


def weight_order(seq_lens, depth):
    order = []
    for l in range(depth):
        for S in seq_lens:
            nb = S // BLK
            for _ in range(nb):
                order += [("in", l, 512), ("in", l, 1024), ("in", l, 2560), ("in", l, 3072), ("in", l, 3584)]
            for _ in range(nb):
                order += [("in", l, 0), ("in", l, 1536), ("in", l, 2048), ("in", l, 4096), ("in", l, 4608)]
                for oc in range(8):
                    order += [("mg", l, oc)]
                for q in range(2):
                    order += [("wo", l, 512 * q)]
    return order


def build_nc(seq_lens, depth):
    nseq = len(seq_lens)
    SMAX = max(seq_lens)
    nc = bass.Bass("TRN2", target_bir_lowering=False)

    def din(name, shape, dt=F32):
        return nc.dram_tensor(name, list(shape), dt, kind="ExternalInput").ap()

    xin = [din("x%d" % i, [D, S]) for i, S in enumerate(seq_lens)]
    yout = [nc.dram_tensor("y%d" % i, [D, S], F32, kind="ExternalOutput").ap() for i, S in enumerate(seq_lens)]
    cT = din("cT", [128, 8, nseq])
    w_ada = din("w_ada", [depth, D, 3 * D])
    b_adaT = din("b_adaT", [depth, 128, 24])
    norm_gT = din("norm_gT", [depth, 128, 8])
    w_in = din("w_in", [depth, D, IN_W])
    qkg = din("qkg", [128, 2 * depth])
    dlog = din("dlog", [128, 8 * depth])
    retgT = din("retgT", [depth, 128, 8])
    w_pa = din("w_pa", [depth, 512, D])
    w_pb = din("w_pb", [depth, D, D])
    w_o = din("w_o", [depth, D, D])
    cmat = din("cmat", [6, 128, 128])
    cmask = din("cmask", [NREL, 128, 512])
    cret = din("cret", [5, 128, 128])
    ccol = din("ccol", [128, 8])
    tabs = {}
    for S in sorted(set(seq_lens)):
        tabs[S] = din("tab%d" % S, [4, 128, S])

    def dscr(name, shape, dt):
        return nc.dram_tensor(name, list(shape), dt).ap()

    XS = [dscr("xs%d" % i, [D, S], F32) for i, S in enumerate(seq_lens)]
    HTd = dscr("HTd", [8, 128, SMAX], BF16)
    KTd = dscr("KTd", [4, 128, SMAX], BF16)
    VAd = dscr("VAd", [SMAX // 128, 128, 528], BF16)
    KRd = dscr("KRd", [4, 128, SMAX], BF16)
    VRd = dscr("VRd", [SMAX // 128, 128, 1024], BF16)
    SBd = dscr("SBd", [SMAX // 128, 128, 1024], BF16)
    B_XS = [Buf(dram=True) for _ in seq_lens]
    B_HTd, B_KTd, B_VAd, B_KRd, B_VRd, B_SBd = (Buf(dram=True) for _ in range(6))

    worder = weight_order(seq_lens, depth)

    with contextlib.ExitStack() as st:
        S_ = Sched(nc, st)
        op, dma = S_.op, S_.dma

        def I(name, *a, **kw):
            return lambda e: getattr(e, name)(*a, **kw)

        def sb(name, shape, dt):
            return st.enter_context(nc.sbuf_tensor("sb_" + name, list(shape), dt)), Buf()

        cm, B_cm = sb("cm", [128, 6, 128], BF16)
        for i in range(5):
            dma("pool", I("dma_start", out=cm[:, i, :], in_=cmat[i]), writes=[B_cm])
        ident, ones_blk, ones_full, RaT, RrT = (cm[:, i, :] for i in range(5))
        onesf, B_onesf = sb("onesf", [128, 64], F32)
        op("dve", I("memset", onesf[:], 1.0), writes=[B_onesf])
        masks, B_masks = sb("masks", [128, NREL, 512], BF16)
        for i in range(NREL):
            dma("pool", I("dma_start", out=masks[:, i, :], in_=cmask[i]), writes=[B_masks])
        cr, B_cr = sb("cr", [128, 5, 128], F32)
        dma("sp", I("dma_start", out=cr[:], in_=cret.rearrange("c p n -> p c n")), writes=[B_cr])
        cc_, B_cc = sb("ccol", [128, 8], F32)
        dma("sp", I("dma_start", out=cc_[:], in_=ccol), writes=[B_cc])
        onecol, epscol = cc_[:, 2:3], cc_[:, 3:4]
        qkg_s, B_qkg = sb("qkg", [128, 2 * depth], F32)
        dma("sp", I("dma_start", out=qkg_s[:], in_=qkg), writes=[B_qkg])
        lg, B_lg = sb("lg", [128, 8 * depth], F32)
        dma("sp", I("dma_start", out=lg[:], in_=dlog), writes=[B_lg])
        op("act", I("activation", out=lg[:], in_=lg[:], func=AF.Exp, scale=-1.0), reads=[B_lg], writes=[B_lg])
        op("act", I("activation", out=lg[:], in_=lg[:], func=AF.Ln, bias=onecol, scale=1.0), reads=[B_lg, B_cc], writes=[B_lg])
        op("dve", I("tensor_scalar", out=lg[:], in0=lg[:], scalar1=-1.0, scalar2=None, op0=ALU.mult), reads=[B_lg], writes=[B_lg])
        csil, B_csil = sb("csil", [128, 8, nseq], F32)
        dma("sp", I("dma_start", out=csil[:], in_=cT), writes=[B_csil])
        op("act", I("activation", out=csil[:], in_=csil[:], func=AF.Silu), reads=[B_csil], writes=[B_csil])

        modT, B_mod = sb("modT", [128, 24, nseq], F32)
        gmod, B_gmod = sb("gmod", [128, 8, nseq], F32)
        bada, B_bada = sb("bada", [128, 24], F32)
        ngT, B_ngT = sb("ngT", [128, 8], F32)
        rgT, B_rgT = sb("rgT", [128, 8], F32)
        MT, B_MT = sb("MT", [128, 4, 128], BF16)
        qdec, B_qdec = sb("qdec", [128, 8, 128], F32)
        kcol, B_kcol = sb("kcol", [128, 8], F32)
        gC, B_gC = sb("gC", [128, 8], F32)
        wada, B_wada = sb("wada", [128, 8, 128], F32)
        RgT, B_RgT = sb("RgT", [128, 2, 128], BF16)
        tmpd, B_tmpd = sb("tmpd", [128, 128], F32)
        tmpd2, B_tmpd2 = sb("tmpd2", [128, 128], F32)

        Sf, B_Sf = sb("Sf", [128, 4, 256], F32)
        B_Sfh = [Buf() for _ in range(4)]
        Sb = Sf
        SFlR = Ring(nc, st, "sfl", [128, 4, 256], BF16, 2)

        WR = Ring(nc, st, "wr", [128, 8, 512], BF16, 5)
        wstate = {"i": 0, "slots": {}}
        PREFETCH = 3

        def wload(idx):
            kind, l, c0 = worder[idx]
            t, B = WR.get()
            if kind == "in":
                src = w_in[l].rearrange("(kc p) n -> p kc n", p=128)[:, :, c0:c0 + 512]
                dma("pool", I("dma_start", out=t[:], in_=src), writes=[B])
            elif kind == "pb":
                src = w_pb[l].rearrange("(kc p) n -> p kc n", p=128)[:, :, c0:c0 + 512]
                dma("pool", I("dma_start", out=t[:], in_=src), writes=[B])
            elif kind == "wo":
                src = w_o[l].rearrange("(kc p) n -> p kc n", p=128)[:, :, c0:c0 + 512]
                dma("pool", I("dma_start", out=t[:], in_=src), writes=[B])
            else:
                oc = c0
                win = w_in[l].rearrange("(kc p) n -> p kc n", p=128)
                dma("pool", I("dma_start", out=t[:, :, 0:128], in_=win[:, :, 5120 + 128 * oc:5248 + 128 * oc]), writes=[B])
                dma("pool", I("dma_start", out=t[:, :, 128:256], in_=win[:, :, 6144 + 128 * oc:6272 + 128 * oc]), writes=[B])
                dma("pool", I("dma_start", out=t[0:64, :, 256:384], in_=w_pa[l].rearrange("(h p) n -> p h n", p=64)[:, :, 128 * oc:128 * oc + 128]), writes=[B])
                dma("pool", I("dma_start", out=t[:, :, 384:512], in_=w_pb[l].rearrange("(kc p) n -> p kc n", p=128)[:, :, 128 * oc:128 * oc + 128]), writes=[B])
            wstate["slots"][idx] = (t, B)

        def wnext(kind, l, c0):
            i = wstate["i"]
            assert worder[i] == (kind, l, c0), (worder[i], kind, l, c0)
            while wstate.get("loaded", 0) < min(len(worder), i + PREFETCH + 1):
                wload(wstate.get("loaded", 0))
                wstate["loaded"] = wstate.get("loaded", 0) + 1
            wstate["i"] = i + 1
            return wstate["slots"].pop(i)

        PS = []
        for i in range(7):
            t = st.enter_context(nc.psum_tensor("ps%d" % i, [128, 512], F32))
            PS.append((t, Buf(psum=True)))
        psT = (st.enter_context(nc.psum_tensor("psT", [128, 1024], BF16)), Buf(psum=True))
        pj = {"i": 0}

        def pproj():
            it = PS[pj["i"] % 2]
            pj["i"] += 1
            return it
        pStat, pRot, pSc0, pSc1, pO = PS[2], PS[3], PS[4], PS[5], PS[6]
        sc = {"i": 0}

        def pscore():
            it = (pSc0, pSc1)[sc["i"] % 2]
            sc["i"] += 1
            return it

        HT, B_HT = sb("HT", [128, 8, 512], BF16)
        TB, B_TB = sb("TB", [128, 4, 512], F32)
        QT, B_QT = sb("QT", [128, 4, 512], BF16)
        QR, B_QR = sb("QR", [128, 4, 512], BF16)
        KTw = Ring(nc, st, "ktw", [128, 2560], BF16, 2)
        VAr = Ring(nc, st, "va", [128, NREL, 132], BF16, 2)
        YAG, B_YAG = sb("YAG", [64, 8, 512], BF16)
        YRG, B_YRG = sb("YRG", [128, 8, 512], BF16)
        VRb, B_VRb = sb("VRb", [128, 4, 1024], BF16)
        MG, B_MG = VRb[:].rearrange("p t (a n) -> p (t a) n", n=512), B_VRb
        KRb, B_KRb = sb("KRb", [128, 4, 512], BF16)
        SBlR = Ring(nc, st, "sbl", [128, 4, 256], BF16, 2)
        KD, B_KD = sb("KD", [128, 4, 512], BF16)
        rstd, B_rstd = sb("rstd", [128, 512], F32)
        R32 = Ring(nc, st, "r32", [128, 512], F32, 6)
        R16 = Ring(nc, st, "r16", [128, 512], BF16, 6)
        PR = Ring(nc, st, "pr", [128, 512], BF16, 3)
        SM = Ring(nc, st, "sm", [128, 128], BF16, 3)
        QF = Ring(nc, st, "qf", [128, 256], BF16, 3)
        SBst = Ring(nc, st, "sbst", [128, 1024], BF16, 2)
        VAst = Ring(nc, st, "vast", [128, 528], BF16, 2)
        SZ = Ring(nc, st, "sz", [64, 512], F32, 2)
        XO = Ring(nc, st, "xo", [128, 512], F32, 2)
        XC = Ring(nc, st, "xc", [128, 512], F32, 2)
        for (t, B) in VAst.items:
            op("pool", I("memset", t[:], 1.0), writes=[B])

        def mm(out, lhsT, rhs, start, stop, reads, writes, **kw):
            op("pe", I("matmul", out, lhsT=lhsT, rhs=rhs, start=start, stop=stop, **kw), reads=reads, writes=writes)

        def rsqrt_ps(dst, B_dst, ps, B_ps, scale):
            op("act", I("activation", out=dst, in_=ps, func=AF.Sqrt, bias=epscol, scale=scale), reads=[B_ps, B_cc], writes=[B_dst])
            op("dve", I("reciprocal", out=dst, in_=dst), reads=[B_dst], writes=[B_dst])

        def proj_fm(Wt, B_W, c0, ncols, rhs_of_kc, B_rhs, nk=8, prow=128):
            pt, B_p = pproj()
            for kc in range(nk):
                mm(pt[0:ncols, :], Wt[0:prow, kc, c0:c0 + ncols], rhs_of_kc(kc), kc == 0, kc == nk - 1, [B_W, B_rhs], [B_p])
            return pt, B_p

        def rope(dst, B_dst, src, B_src, RT, tC, tS, mul=None, src_is_psum=True):
            sbf, B_sbf = R16.get()
            op("act", I("activation", out=sbf[:], in_=src, func=AF.Copy), reads=[B_src], writes=[B_sbf])
            mm(pRot[0][:], RT, sbf[:], True, True, [B_cm, B_sbf], [pRot[1]])
            t1, B_t1 = R32.get()
            t2, B_t2 = R32.get()
            if mul is None:
                op("dve", I("tensor_tensor", out=t1[:], in0=src, in1=tC, op=ALU.mult), reads=[B_src, B_TB], writes=[B_t1])
                op("dve", I("tensor_tensor", out=t2[:], in0=pRot[0][:], in1=tS, op=ALU.mult), reads=[pRot[1], B_TB], writes=[B_t2])
            else:
                op("dve", I("scalar_tensor_tensor", out=t1[:], in0=src, scalar=mul, in1=tC, op0=ALU.mult, op1=ALU.mult), reads=[B_src, B_TB, B_cc], writes=[B_t1])
                op("dve", I("scalar_tensor_tensor", out=t2[:], in0=pRot[0][:], scalar=mul, in1=tS, op0=ALU.mult, op1=ALU.mult), reads=[pRot[1], B_TB, B_cc], writes=[B_t2])
            op("pool", I("tensor_tensor", out=dst, in0=t1[:], in1=t2[:], op=ALU.add), reads=[B_t1, B_t2], writes=[B_dst])

        def qknorm_rope(dst, B_dst, pt, B_p, gcol, Rg):
            sq, B_sq = R16.get()
            op("act", I("activation", out=sq[:], in_=pt[:], func=AF.Square), reads=[B_p], writes=[B_sq])
            mm(pStat[0][:], ones_blk, sq[:], True, True, [B_cm, B_sq], [pStat[1]])
            sbf, B_sbf = R16.get()
            op("act", I("activation", out=sbf[:], in_=pt[:], func=AF.Copy), reads=[B_p], writes=[B_sbf])
            mm(pRot[0][:], Rg, sbf[:], True, True, [B_RgT, B_sbf], [pRot[1]])
            t1, B_t1 = R32.get()
            op("dve", I("scalar_tensor_tensor", out=t1[:], in0=pt[:], scalar=gcol, in1=TB[:, 0, :], op0=ALU.mult, op1=ALU.mult), reads=[B_p, B_TB, B_qkg], writes=[B_t1])
            rs, B_rs = R32.get()
            op("act", I("activation", out=rs[:], in_=pStat[0][:], func=AF.Sqrt, bias=epscol, scale=1.0 / 64), reads=[pStat[1], B_cc], writes=[B_rs])
            t2, B_t2 = R32.get()
            op("dve", I("tensor_tensor", out=t2[:], in0=pRot[0][:], in1=TB[:, 1, :], op=ALU.mult), reads=[pRot[1], B_TB], writes=[B_t2])
            op("dve", I("reciprocal", out=rs[:], in_=rs[:]), reads=[B_rs], writes=[B_rs])
            op("pool", I("tensor_tensor", out=t1[:], in0=t1[:], in1=t2[:], op=ALU.add), reads=[B_t1, B_t2], writes=[B_t1])
            op("dve", I("tensor_tensor", out=dst, in0=t1[:], in1=rs[:], op=ALU.mult), reads=[B_t1, B_rs], writes=[B_dst])

        def chunk_pipeline(n, proj_fn, fin_fn):
            cur = proj_fn(0)
            for c in range(n):
                nxt = proj_fn(c + 1) if c + 1 < n else None
                fin_fn(c, cur)
                cur = nxt

        def ktranspose(src_of_h, B_src, n, colsel):
            for h in range(4):
                op("pe", I("transpose", psT[0][:, h * 128:(h + 1) * 128], src_of_h(h), ident), reads=[B_src, B_cm], writes=[psT[1]])
            for h in range(4):
                op("act", I("activation", out=KD[:, n, h * 128:(h + 1) * 128], in_=psT[0][:, h * 128:(h + 1) * 128], func=AF.Copy, scale=kcol[:, colsel + h:colsel + h + 1]), reads=[psT[1], B_kcol], writes=[B_KD])

        import os
        KSTOP = int(os.environ.get("KSTOP", "0"))

        class _Stop(Exception):
            pass
        try:
          for l in range(depth):
              dma("sp", I("dma_start", out=bada[:], in_=b_adaT[l]), writes=[B_bada])
              dma("sp", I("dma_start", out=ngT[:], in_=norm_gT[l]), writes=[B_ngT])
              dma("sp", I("dma_start", out=rgT[:], in_=retgT[l]), writes=[B_rgT])
              for fc in range(24):
                  if True:
                      dma("sp", I("dma_start", out=wada[:], in_=w_ada[l].rearrange("(kc p) n -> p kc n", p=128)[:, :, fc * 128:(fc + 1) * 128]), writes=[B_wada])
                      pt, B_p = pproj()
                      for kc in range(8):
                          mm(pt[:, 0:nseq], wada[:, kc, :], csil[:, kc, :], kc == 0, kc == 7, [B_wada, B_csil], [B_p])
                      op("dve", I("tensor_scalar", out=modT[:, fc, :], in0=pt[:, 0:nseq], scalar1=bada[:, fc:fc + 1], scalar2=None, op0=ALU.add), reads=[B_p, B_bada], writes=[B_mod])
              for c in range(8):
                  op("dve", I("tensor_scalar", out=gmod[:, c, :], in0=modT[:, 8 + c, :], scalar1=1.0, scalar2=ngT[:, c:c + 1], op0=ALU.add, op1=ALU.mult), reads=[B_mod, B_ngT], writes=[B_gmod])
              lo = 8 * l
              for h in range(4):
                  op("dve", I("tensor_scalar", out=tmpd[:], in0=cr[:, 0, :], scalar1=lg[:, lo + h:lo + h + 1], scalar2=None, op0=ALU.mult), reads=[B_cr, B_lg], writes=[B_tmpd])
                  op("dve", I("scalar_tensor_tensor", out=tmpd2[:], in0=cr[:, 1, :], scalar=lg[:, lo + 4 + h:lo + 5 + h], in1=tmpd[:], op0=ALU.mult, op1=ALU.add), reads=[B_cr, B_lg, B_tmpd], writes=[B_tmpd2])
                  op("act", I("activation", out=tmpd2[:], in_=tmpd2[:], func=AF.Exp), reads=[B_tmpd2], writes=[B_tmpd2])
                  op("dve", I("tensor_tensor", out=MT[:, h, :], in0=tmpd2[:], in1=cr[:, 2, :], op=ALU.mult), reads=[B_tmpd2, B_cr], writes=[B_MT])
                  op("act", I("activation", out=qdec[:, h, :], in_=cr[:, 3, :], func=AF.Exp, scale=lg[:, lo + h:lo + h + 1]), reads=[B_cr, B_lg], writes=[B_qdec])
                  op("act", I("activation", out=qdec[:, 4 + h, :], in_=cr[:, 4, :], func=AF.Exp, scale=lg[:, lo + 4 + h:lo + 5 + h]), reads=[B_cr, B_lg], writes=[B_qdec])
                  op("act", I("activation", out=kcol[:, h:h + 1], in_=cc_[:, 0:1], func=AF.Exp, scale=lg[:, lo + h:lo + h + 1]), reads=[B_cc, B_lg], writes=[B_kcol])
                  op("act", I("activation", out=kcol[:, 4 + h:5 + h], in_=cc_[:, 1:2], func=AF.Exp, scale=lg[:, lo + 4 + h:lo + 5 + h]), reads=[B_cc, B_lg], writes=[B_kcol])
              op("act", I("activation", out=gC[:], in_=lg[:, lo:lo + 8], func=AF.Exp, scale=128.0), reads=[B_lg], writes=[B_gC])
              if KSTOP == 1:
                  raise _Stop()
              for j_ in range(2):
                  op("dve", I("tensor_scalar", out=RgT[:, j_, :], in0=RaT, scalar1=qkg_s[:, 2 * l + j_:2 * l + j_ + 1], scalar2=None, op0=ALU.mult), reads=[B_cm, B_qkg], writes=[B_RgT])
              qg = qkg_s[:, 2 * l:2 * l + 1]
              kg = qkg_s[:, 2 * l + 1:2 * l + 2]

              for si, S in enumerate(seq_lens):
                  nb = S // BLK
                  xsrc = xin[si] if l == 0 else XS[si]
                  B_xsrc = None if l == 0 else B_XS[si]
                  xdst = yout[si] if l == depth - 1 else XS[si]
                  B_xdst = None if l == depth - 1 else B_XS[si]
                  tab = tabs[S]
                  rd_x = [B_xsrc] if B_xsrc is not None else []

                  op("dve", I("memset", Sb[:], 0.0), writes=B_Sfh)
                  for b in reversed(range(nb)):
                      t0 = b * BLK
                      dma("sp", I("dma_start", out=TB[:], in_=tab.rearrange("k p t -> p k t")[:, :, t0:t0 + BLK]), writes=[B_TB])
                      for c in range(8):
                          sq, B_sq = R16.get()
                          xc, B_xc = XC.get()
                          dma("sp", I("dma_start", out=xc[:], in_=xsrc[c * 128:(c + 1) * 128, t0:t0 + BLK]), reads=rd_x, writes=[B_xc])
                          op("act", I("activation", out=sq[:], in_=xc[:], func=AF.Square), reads=[B_xc], writes=[B_sq])
                          mm(pStat[0][:], ones_full, sq[:], c == 0, c == 7, [B_cm, B_sq], [pStat[1]])
                      rsqrt_ps(rstd[:], B_rstd, pStat[0][:], pStat[1], 1.0 / D)
                      for c in range(8):
                          t1, B_t1 = R32.get()
                          xc, B_xc = XC.get()
                          dma("sp", I("dma_start", out=xc[:], in_=xsrc[c * 128:(c + 1) * 128, t0:t0 + BLK]), reads=rd_x, writes=[B_xc])
                          op("dve", I("scalar_tensor_tensor", out=t1[:], in0=xc[:], scalar=gmod[:, c, si:si + 1], in1=rstd[:], op0=ALU.mult, op1=ALU.mult), reads=[B_xc, B_gmod, B_rstd], writes=[B_t1])
                          op("act", I("activation", out=HT[:, c, :], in_=t1[:], func=AF.Identity, bias=modT[:, c, si:si + 1], scale=1.0), reads=[B_t1, B_mod], writes=[B_HT])
                      dma("sp", I("dma_start", out=HTd.rearrange("c p t -> p c t")[:, :, t0:t0 + BLK], in_=HT[:]), reads=[B_HT], writes=[B_HTd])
                      hrhs = lambda kc: HT[:, kc, :]
                      Wt, B_W = wnext("in", l, 512)
                      def ka_fin(cc, pp):
                          pt, B_p = pp
                          ks, B_ks = R16.get()
                          qknorm_rope(ks[:], B_ks, pt, B_p, kg, RgT[:, 1, :])
                          dma("sp", I("dma_start", out=KTd[cc][:, t0:t0 + BLK], in_=ks[:]), reads=[B_ks], writes=[B_KTd])
                      chunk_pipeline(4, lambda cc: proj_fm(Wt, B_W, cc * 128, 128, hrhs, B_HT), ka_fin)
                      if KSTOP == 5:
                          raise _Stop()
                      Wt, B_W = wnext("in", l, 1024)
                      for tt in range(4):
                          pt, B_p = pproj()
                          for kc in range(8):
                              mm(pt[:], HT[:, kc, tt * 128:(tt + 1) * 128], Wt[:, kc, :], kc == 0, kc == 7, [B_HT, B_W], [B_p])
                          vs, B_vs = VAst.get()
                          op("act", I("activation", out=vs[:].rearrange("p (h e) -> p h e", e=66)[:, :, 0:64], in_=pt[:].rearrange("p (h e) -> p h e", e=64), func=AF.Copy), reads=[B_p], writes=[B_vs])
                          dma("sp", I("dma_start", out=VAd[4 * b + tt], in_=vs[:]), reads=[B_vs], writes=[B_VAd])
                      if KSTOP == 7:
                          raise _Stop()
                      Wt, B_W = wnext("in", l, 2560)
                      chunk_pipeline(4, lambda h: proj_fm(Wt, B_W, h * 128, 128, hrhs, B_HT),
                                     lambda h, pp: rope(KRb[:, h, :], B_KRb, pp[0][:], pp[1], RrT, TB[:, 2, :], TB[:, 3, :], mul=cc_[:, 4:5]))
                      dma("sp", I("dma_start", out=KRd.rearrange("h p t -> p h t")[:, :, t0:t0 + BLK], in_=KRb[:]), reads=[B_KRb], writes=[B_KRd])
                      if KSTOP == 8:
                          raise _Stop()
                      for g in range(2):
                          Wt, B_W = wnext("in", l, 3072 + 512 * g)
                          for tt in range(4):
                              pt, B_p = pproj()
                              for kc in range(8):
                                  mm(pt[:], HT[:, kc, tt * 128:(tt + 1) * 128], Wt[:, kc, :], kc == 0, kc == 7, [B_HT, B_W], [B_p])
                              op("act", I("activation", out=VRb[:, tt, g * 512:(g + 1) * 512], in_=pt[:], func=AF.Copy), reads=[B_p], writes=[B_VRb])
                      dma("sp", I("dma_start", out=VRd[4 * b:4 * b + 4].rearrange("t p n -> p t n"), in_=VRb[:]), reads=[B_VRb], writes=[B_VRd])
                      if KSTOP == 6:
                          raise _Stop()
                      for n in reversed(range(4)):
                          ktranspose(lambda h, n=n: KRb[:, h, n * 128:(n + 1) * 128], B_KRb, n, 4)
                          stg, B_stg = SBst.get()
                          op("pool", I("tensor_copy", out=stg[:], in_=Sb[:].rearrange("p h n -> p (h n)")), reads=B_Sfh, writes=[B_stg])
                          dma("sp", I("dma_start", out=SBd[4 * b + n], in_=stg[:]), reads=[B_stg], writes=[B_SBd])
                          for h in range(4):
                              pt, B_p = pproj()
                              mm(pt[:, 0:256], KD[:, n, h * 128:(h + 1) * 128], VRb[:, n, h * 256:(h + 1) * 256], True, True, [B_KD, B_VRb], [B_p])
                              op("dve", I("scalar_tensor_tensor", out=Sb[:, h, :], in0=Sb[:, h, :], scalar=gC[:, 4 + h:5 + h], in1=pt[:, 0:256], op0=ALU.mult, op1=ALU.add), reads=[B_p, B_gC, B_Sfh[h]], writes=[B_Sfh[h]])

                  if KSTOP == 2:
                      raise _Stop()
                  op("dve", I("memset", Sf[:], 0.0), writes=B_Sfh)
                  ntile = S // 128
                  for b in range(nb):
                      t0 = b * BLK
                      dma("sp", I("dma_start", out=HT[:], in_=HTd.rearrange("c p t -> p c t")[:, :, t0:t0 + BLK]), reads=[B_HTd], writes=[B_HT])
                      dma("sp", I("dma_start", out=TB[:], in_=tab.rearrange("k p t -> p k t")[:, :, t0:t0 + BLK]), writes=[B_TB])
                      ulo = max(0, 4 * b - 8)
                      uhi = min(ntile, 4 * b + 12)
                      dma("sp", I("dma_start", out=VRb[:], in_=VRd[4 * b:4 * b + 4].rearrange("t p n -> p t n")), reads=[B_VRd], writes=[B_VRb])
                      dma("sp", I("dma_start", out=KRb[:], in_=KRd.rearrange("h p t -> p h t")[:, :, t0:t0 + BLK]), reads=[B_KRd], writes=[B_KRb])
                      hrhs = lambda kc: HT[:, kc, :]
                      Wt, B_W = wnext("in", l, 0)
                      chunk_pipeline(4, lambda cc: proj_fm(Wt, B_W, cc * 128, 128, hrhs, B_HT),
                                     lambda cc, pp: qknorm_rope(QT[:, cc, :], B_QT, pp[0], pp[1], qg, RgT[:, 0, :]))
                      Wz, B_Wz = wnext("in", l, 1536)
                      po_alt = [pO, pStat]

                      def load_pair(cc):
                          ktw, B_ktw = KTw.get()
                          VA, B_VA = VAr.get()
                          dma("sp", I("dma_start", out=ktw[:, 0:(uhi - ulo) * 128], in_=KTd[cc][:, ulo * 128:uhi * 128]), reads=[B_KTd], writes=[B_ktw])
                          dma("sp", I("dma_start", out=VA[:, 0:uhi - ulo, :], in_=VAd[ulo:uhi].rearrange("t p n -> p t n")[:, :, cc * 132:(cc + 1) * 132]), reads=[B_VAd], writes=[B_VA])
                          return ktw, B_ktw, VA, B_VA

                      def normalise_steps(h, pOh, B_pOh, sz, B_sz):
                          rd, B_rd = R32.get()
                          op("dve", I("reciprocal", out=rd[64:65, :], in_=pOh[64:65, :]), reads=[B_pOh], writes=[B_rd])
                          rh, B_rh = R16.get()
                          rl, B_rl = R16.get()
                          op("dve", I("tensor_copy", out=rh[64:65, :], in_=rd[64:65, :]), reads=[B_rd], writes=[B_rh])
                          op("dve", I("tensor_tensor", out=rl[64:65, :], in0=rd[64:65, :], in1=rh[64:65, :], op=ALU.subtract), reads=[B_rd, B_rh], writes=[B_rl])
                          yield
                          yield
                          mm(pRot[0][0:64, :], ones_full[64:65, 0:64], rh[64:65, :], True, False, [B_cm, B_rh], [pRot[1]])
                          mm(pRot[0][0:64, :], ones_full[64:65, 0:64], rl[64:65, :], False, True, [B_cm, B_rl], [pRot[1]])
                          yield
                          yield
                          bc, B_bc = R32.get()
                          op("act", I("activation", out=bc[0:64, :], in_=pRot[0][0:64, :], func=AF.Copy), reads=[pRot[1]], writes=[B_bc])
                          yield
                          yield
                          ya, B_ya = R32.get()
                          op("dve", I("tensor_tensor", out=ya[0:64, :], in0=pOh[0:64, :], in1=bc[0:64, :], op=ALU.mult), reads=[B_pOh, B_bc], writes=[B_ya])
                          yield
                          op("pool", I("tensor_tensor", out=YAG[:, h, :], in0=ya[0:64, :], in1=sz[0:64, :], op=ALU.mult), reads=[B_ya, B_sz], writes=[B_YAG])
                          yield

                      def run_all(gen):
                          for _ in gen:
                              pass
                      tiles = list(range(ulo, uhi))
                      nt = len(tiles)
                      items = [(h, i) for h in range(8) for i in range(nt)]
                      pairs = {0: load_pair(0)}
                      sbufs = {}
                      LA = 2

                      def qk(k):
                          h, i = items[k]
                          cc, hp = h // 2, h % 2
                          if cc not in pairs:
                              pairs[cc] = load_pair(cc)
                          ktw, B_ktw, VA, B_VA = pairs[cc]
                          rows = slice(hp * 64, hp * 64 + 64)
                          u = tiles[i]
                          ps_s, B_s = pscore()
                          mm(ps_s[:], ktw[rows, (u - ulo) * 128:(u - ulo + 1) * 128], QT[rows, cc, :], True, True, [B_ktw, B_QT], [B_s])
                          sbufs[k] = (ps_s, B_s)
                      for k in range(LA):
                          qk(k)
                      pending = None
                      cur = None
                      ngen = None
                      zpend = None
                      for k, (h, i) in enumerate(items):
                          cc, hp = h // 2, h % 2
                          u = tiles[i]
                          if i == 0:
                              if hp == 0 and cc + 1 < 4 and (cc + 1) not in pairs:
                                  pairs[cc + 1] = load_pair(cc + 1)
                              pOh, B_pOh = po_alt[h % 2]
                              pt, B_p = proj_fm(Wz, B_Wz, h * 64, 64, hrhs, B_HT)
                              sz, B_sz = SZ.get()
                              cur = (h, pOh, B_pOh, sz, B_sz)
                              zpend = (pt, B_p, sz, B_sz)
                          if i == 2:
                              op("act", I("activation", out=zpend[2][0:64, :], in_=zpend[0][0:64, :], func=AF.Silu), reads=[zpend[1]], writes=[zpend[3]])
                          ktw, B_ktw, VA, B_VA = pairs[cc]
                          ps_s, B_s = sbufs.pop(k)
                          pr, B_pr = PR.get()
                          op("act", I("activation", out=pr[:], in_=ps_s[:], func=AF.Exp, scale=0.125), reads=[B_s], writes=[B_pr])
                          rel = u - 4 * b + 8
                          op("dve", I("tensor_tensor", out=pr[:], in0=pr[:], in1=masks[:, rel, :], op=ALU.mult), reads=[B_pr, B_masks], writes=[B_pr])
                          if k + LA < len(items):
                              qk(k + LA)
                          mm(cur[1][0:65, :], VA[:, u - ulo, hp * 66:hp * 66 + 65], pr[:], i == 0, i == nt - 1, [B_VA, B_pr], [cur[2]])
                          if i == 1 and pending is not None:
                              ngen = normalise_steps(*pending)
                              pending = None
                          if ngen is not None and i >= 1:
                              if next(ngen, "end") == "end":
                                  ngen = None
                          if i == nt - 1:
                              if ngen is not None:
                                  run_all(ngen)
                                  ngen = None
                              pending = cur
                      run_all(normalise_steps(*pending))
                      if KSTOP == 3:
                          raise _Stop()
                      Wt, B_W = wnext("in", l, 2048)
                      chunk_pipeline(4, lambda h: proj_fm(Wt, B_W, h * 128, 128, hrhs, B_HT),
                                     lambda h, pp: rope(QR[:, h, :], B_QR, pp[0][:], pp[1], RrT, TB[:, 2, :], TB[:, 3, :]))
                      Wzr = [wnext("in", l, 4096), wnext("in", l, 4608)]
                      for n in range(4):
                          ktranspose(lambda h, n=n: KRb[:, h, n * 128:(n + 1) * 128], B_KRb, n, 0)
                      def stateA(h):
                          SFl, B_SFl = SFlR.get()
                          for n in range(4):
                              op("act", I("activation", out=SFl[:, n, :], in_=Sf[:, h, :], func=AF.Copy), reads=[B_Sfh[h]], writes=[B_SFl])
                              pd, B_pd = pproj()
                              mm(pd[:, 0:256], KD[:, n, h * 128:(h + 1) * 128], VRb[:, n, h * 256:(h + 1) * 256], True, True, [B_KD, B_VRb], [B_pd])
                              op("dve", I("scalar_tensor_tensor", out=Sf[:, h, :], in0=Sf[:, h, :], scalar=gC[:, h:h + 1], in1=pd[:, 0:256], op0=ALU.mult, op1=ALU.add), reads=[B_pd, B_gC, B_Sfh[h]], writes=[B_Sfh[h]])
                          return SFl, B_SFl
                      sf_next = stateA(0)
                      for h in range(4):
                          SFl, B_SFl = sf_next
                          if h + 1 < 4:
                              sf_next = stateA(h + 1)
                          po = [pscore(), pscore()]
                          SBl, B_SBl = SBlR.get()
                          dma("sp", I("dma_start", out=SBl[:], in_=SBd[4 * b:4 * b + 4].rearrange("t p n -> p t n")[:, :, h * 256:(h + 1) * 256]), reads=[B_SBd], writes=[B_SBl])
                          scd = {}

                          def scores(n):
                              cs = slice(n * 128, (n + 1) * 128)
                              ps_s, B_s = pproj()
                              mm(ps_s[:, 0:128], KRb[:, h, cs], QR[:, h, cs], True, True, [B_KRb, B_QR], [B_s])
                              smt, B_sm = SM.get()
                              op("dve", I("tensor_tensor", out=smt[:], in0=ps_s[:, 0:128], in1=MT[:, h, :], op=ALU.mult), reads=[B_s, B_MT], writes=[B_sm])
                              qf, B_qf = QF.get()
                              op("pool", I("tensor_tensor", out=qf[:, 0:128], in0=QR[:, h, cs], in1=qdec[:, h, :], op=ALU.mult), reads=[B_QR, B_qdec], writes=[B_qf])
                              op("pool", I("tensor_tensor", out=qf[:, 128:256], in0=QR[:, h, cs], in1=qdec[:, 4 + h, :], op=ALU.mult), reads=[B_QR, B_qdec], writes=[B_qf])
                              scd[n] = (smt, B_sm, qf, B_qf)
                          scores(0)
                          for n in range(4):
                              cs = slice(n * 128, (n + 1) * 128)
                              if n + 1 < 4:
                                  scores(n + 1)
                              smt, B_sm, qf, B_qf = scd.pop(n)
                              for ev in range(2):
                                  vs_ = slice(h * 256 + ev * 128, h * 256 + ev * 128 + 128)
                                  pt_o, B_o = po[ev]
                                  mm(pt_o[:, cs], VRb[:, n, vs_], smt[:], True, False, [B_VRb, B_sm], [B_o])
                                  mm(pt_o[:, cs], SFl[:, n, ev * 128:(ev + 1) * 128], qf[:, 0:128], False, False, [B_SFl, B_qf], [B_o])
                                  mm(pt_o[:, cs], SBl[:, n, ev * 128:(ev + 1) * 128], qf[:, 128:256], False, True, [B_SBl, B_qf], [B_o])
                          yb = [R16.get(), R16.get()]
                          ysq = [R16.get(), R16.get()]
                          for ev in range(2):
                              op("act", I("activation", out=yb[ev][0][:], in_=po[ev][0][:], func=AF.Copy), reads=[po[ev][1]], writes=[yb[ev][1]])
                              op("act", I("activation", out=ysq[ev][0][:], in_=po[ev][0][:], func=AF.Square), reads=[po[ev][1]], writes=[ysq[ev][1]])
                          for ev in range(2):
                              mm(pStat[0][:], ones_full, yb[ev][0][:], ev == 0, ev == 1, [B_cm, yb[ev][1]], [pStat[1]])
                          for ev in range(2):
                              mm(pRot[0][:], ones_full, ysq[ev][0][:], ev == 0, ev == 1, [B_cm, ysq[ev][1]], [pRot[1]])
                          mean, B_mean = R32.get()
                          op("dve", I("tensor_scalar", out=mean[:], in0=pStat[0][:], scalar1=1.0 / 256, scalar2=None, op0=ALU.mult), reads=[pStat[1]], writes=[B_mean])
                          msq, B_msq = R32.get()
                          op("pool", I("tensor_tensor", out=msq[:], in0=mean[:], in1=mean[:], op=ALU.mult), reads=[B_mean], writes=[B_msq])
                          var, B_var = R32.get()
                          op("dve", I("scalar_tensor_tensor", out=var[:], in0=pRot[0][:], scalar=1.0 / 256, in1=msq[:], op0=ALU.mult, op1=ALU.subtract), reads=[pRot[1], B_msq], writes=[B_var])
                          op("act", I("activation", out=var[:], in_=var[:], func=AF.Sqrt, bias=epscol, scale=1.0), reads=[B_var, B_cc], writes=[B_var])
                          op("dve", I("reciprocal", out=var[:], in_=var[:]), reads=[B_var], writes=[B_var])
                          for ev in range(2):
                              c8 = 2 * h + ev
                              Wzt, B_Wzt = Wzr[c8 // 4]
                              pt, B_p = proj_fm(Wzt, B_Wzt, (c8 % 4) * 128, 128, hrhs, B_HT)
                              sz, B_sz = R32.get()
                              op("act", I("activation", out=sz[:], in_=pt[:], func=AF.Silu), reads=[B_p], writes=[B_sz])
                              d1, B_d1 = R32.get()
                              op("dve", I("tensor_tensor", out=d1[:], in0=po[ev][0][:], in1=mean[:], op=ALU.subtract), reads=[po[ev][1], B_mean], writes=[B_d1])
                              op("dve", I("scalar_tensor_tensor", out=d1[:], in0=d1[:], scalar=rgT[:, c8:c8 + 1], in1=var[:], op0=ALU.mult, op1=ALU.mult), reads=[B_d1, B_var, B_rgT], writes=[B_d1])
                              op("pool", I("tensor_tensor", out=YRG[:, c8, :], in0=d1[:], in1=sz[:], op=ALU.mult), reads=[B_d1, B_sz], writes=[B_YRG])
                      if KSTOP == 4:
                          raise _Stop()
                      for oc in range(8):
                          Wm, B_Wm = wnext("mg", l, oc)
                          pa_, B_pa = proj_fm(Wm, B_Wm, 256, 128, lambda hh: YAG[:, hh, :], B_YAG, nk=8, prow=64)
                          pga, B_pga = pStat
                          for kc in range(8):
                              mm(pga[:], Wm[:, kc, 0:128], HT[:, kc, :], kc == 0, kc == 7, [B_Wm, B_HT], [B_pga])
                          sa, B_sa = R32.get()
                          op("act", I("activation", out=sa[:], in_=pga[:], func=AF.Sigmoid), reads=[B_pga], writes=[B_sa])
                          m1, B_m1 = R32.get()
                          op("dve", I("tensor_tensor", out=m1[:], in0=pa_[:], in1=sa[:], op=ALU.mult), reads=[B_pa, B_sa], writes=[B_m1])
                          pb_, B_pb = proj_fm(Wm, B_Wm, 384, 128, lambda c: YRG[:, c, :], B_YRG)
                          pgr, B_pgr = pRot
                          for kc in range(8):
                              mm(pgr[:], Wm[:, kc, 128:256], HT[:, kc, :], kc == 0, kc == 7, [B_Wm, B_HT], [B_pgr])
                          sr, B_sr = R32.get()
                          op("act", I("activation", out=sr[:], in_=pgr[:], func=AF.Sigmoid), reads=[B_pgr], writes=[B_sr])
                          m2, B_m2 = R32.get()
                          op("dve", I("tensor_tensor", out=m2[:], in0=pb_[:], in1=sr[:], op=ALU.mult), reads=[B_pb, B_sr], writes=[B_m2])
                          op("pool", I("tensor_tensor", out=MG[:, oc, :], in0=m1[:], in1=m2[:], op=ALU.add), reads=[B_m1, B_m2], writes=[B_MG])
                      for q in range(2):
                          Wo, B_Wo = wnext("wo", l, 512 * q)
                          for j in range(4):
                              oc = 4 * q + j
                              xc, B_xc = XC.get()
                              dma("sp", I("dma_start", out=xc[:], in_=xsrc[oc * 128:(oc + 1) * 128, t0:t0 + BLK]), reads=rd_x, writes=[B_xc])
                              po_, B_po = proj_fm(Wo, B_Wo, j * 128, 128, lambda c: MG[:, c, :], B_MG)
                              xo, B_xo = XO.get()
                              op("dve", I("scalar_tensor_tensor", out=xo[:], in0=po_[:], scalar=modT[:, 16 + oc, si:si + 1], in1=xc[:], op0=ALU.mult, op1=ALU.add), reads=[B_po, B_xc, B_mod], writes=[B_xo])
                              wr_x = [B_xdst] if B_xdst is not None else []
                              dma("sp", I("dma_start", out=xdst[oc * 128:(oc + 1) * 128, t0:t0 + BLK], in_=xo[:]), reads=[B_xo], writes=wr_x, sbuf=B_xo)
        except _Stop:
            pass
        if not KSTOP:
            assert wstate["i"] == len(worder)
        print('sbuf_left', nc.sbuf_bytes_remaining, 'nsem', S_.nsem, {e: S_.cnt[e] for e in S_.names})
        S_.finish()
    return nc


def _tables(S):
    pos = np.arange(S, dtype=np.float32)
    p = np.arange(128)
    tab = np.zeros((4, 128, S), np.float32)
    fa = np.exp(np.float32(-math.log(500000.0)) * np.arange(8, dtype=np.float32) / np.float32(8)).astype(np.float32)
    e = p % 64
    tab[0] = 1.0
    for pp in range(128):
        ee = e[pp]
        if ee < 16:
            ang = (pos * fa[ee % 8]).astype(np.float32).astype(np.float64)
            tab[0, pp] = np.cos(ang)
            tab[1, pp] = (-np.sin(ang)) if ee < 8 else np.sin(ang)
    fr = np.exp(np.float32(-math.log(10000.0)) * np.arange(64, dtype=np.float32) / np.float32(64)).astype(np.float32)
    for pp in range(128):
        ang = (pos * fr[pp % 64]).astype(np.float32).astype(np.float64)
        tab[2, pp] = np.cos(ang)
        tab[3, pp] = (-np.sin(ang)) if pp < 64 else np.sin(ang)
    return tab


def _consts():
    p = np.arange(128)
    cmat = np.zeros((6, 128, 128), np.float32)
    cmat[0] = np.eye(128)
    cmat[1] = (p[:, None] // 64 == p[None, :] // 64)
    cmat[2] = 1.0
    for m in range(128):
        e = m % 64
        if e < 8:
            cmat[3, m + 8, m] = 1.0
        elif e < 16:
            cmat[3, m - 8, m] = 1.0
        cmat[4, (m + 64) % 128, m] = 1.0
    cmask = np.zeros((NREL, 128, 512), np.float32)
    for r in range(NREL):
        rel = r - 8
        d = 128 * rel + p[:, None] - np.arange(512)[None, :]
        ad = np.abs(d)
        cmask[r] = (ad <= 64).astype(np.float32) + ((d % 4 == 0) & (ad <= 256)) + ((d % 16 == 0) & (ad <= 1024))
    cret = np.zeros((5, 128, 128), np.float32)
    dd = (np.arange(128)[None, :] - p[:, None]).astype(np.float32)
    cret[0] = np.maximum(dd, 0)
    cret[1] = np.maximum(-dd, 0)
    cret[2] = 1.0 + (dd == 0)
    cret[3] = np.arange(128)[None, :] + 1.0
    cret[4] = 128.0 - np.arange(128)[None, :]
    ccol = np.zeros((128, 8), np.float32)
    ccol[:, 4] = 128.0 ** -0.5
    ccol[:, 0] = 127 - p
    ccol[:, 1] = p
    ccol[:, 2] = 1.0
    ccol[:, 3] = EPS
    return cmat, cmask, cret, ccol


_CACHE = {}


def kernel(x_prompt, x_sample, c_prompt, c_sample, norm_g, w_ada, b_ada, w_in, q_norm_g, k_norm_g,
           ret_decay_logit, ret_norm_g, w_proj_a, w_proj_b, w_out):
    n_cores = 8
    x_prompt = np.asarray(x_prompt, np.float32)
    x_sample = np.asarray(x_sample, np.float32)
    depth = int(np.asarray(norm_g).shape[0])
    seq_lens = [x_prompt.shape[1]] * NP + [x_sample.shape[1]]
    key = (tuple(seq_lens), depth)
    if key not in _CACHE:
        _CACHE[key] = build_nc(seq_lens, depth)
    nc = _CACHE[key]
    cmat, cmask, cret, ccol = _consts()
    tabs = {S: _tables(S) for S in sorted(set(seq_lens))}
    f = lambda a: np.ascontiguousarray(np.asarray(a, np.float32))

    def colT(a, n):
        a = f(a)
        return np.ascontiguousarray(a.reshape(a.shape[0], n, 128).transpose(0, 2, 1))
    qkg = np.zeros((128, 2 * depth), np.float32)
    for l in range(depth):
        qkg[:, 2 * l] = np.tile(f(q_norm_g)[l], 2)
        qkg[:, 2 * l + 1] = np.tile(f(k_norm_g)[l], 2)
    dlog = np.ascontiguousarray(np.broadcast_to(f(ret_decay_logit).reshape(1, -1), (128, 8 * depth)))
    xsT = np.ascontiguousarray(x_sample[0].T)
    common = {
        "w_ada": f(w_ada), "b_adaT": colT(b_ada, 24), "norm_gT": colT(norm_g, 8), "w_in": f(w_in),
        "qkg": qkg, "dlog": dlog, "retgT": colT(ret_norm_g, 8), "w_pa": f(w_proj_a), "w_pb": f(w_proj_b),
        "w_o": f(w_out), "cmat": cmat, "cmask": cmask, "cret": cret, "ccol": ccol,
    }
    for S, t in tabs.items():
        common["tab%d" % S] = t
    in_maps = []
    for c in range(n_cores):
        m = dict(common)
        cs = np.concatenate([f(c_prompt)[c * NP:(c + 1) * NP], f(c_sample)], axis=0)
        m["cT"] = np.ascontiguousarray(cs.reshape(NP + 1, 8, 128).transpose(2, 1, 0))
        for i in range(NP):
            m["x%d" % i] = np.ascontiguousarray(x_prompt[c * NP + i].T)
        m["x%d" % NP] = xsT
        in_maps.append(m)
    res = run_bass_kernel_spmd(nc, in_maps, core_ids=list(range(n_cores)))
    y_prompt = np.empty_like(x_prompt)
    for c in range(n_cores):
        for i in range(NP):
            y_prompt[c * NP + i] = res.results[c]["y%d" % i].T
    y_sample = np.ascontiguousarray(res.results[0]["y%d" % NP].T)[None]
    return (y_prompt, y_sample.astype(np.float32))
```

```python
import contextlib
import math
import numpy as np
import concourse.bass as bass
import concourse.mybir as mybir
from concourse.bass_utils import run_bass_kernel_spmd

F32 = mybir.dt.float32
BF16 = mybir.dt.bfloat16
ALU = mybir.AluOpType
AF = mybir.ActivationFunctionType

D = 1024
DEPTH = 4
SEQ = 2048
DEC_SEQ = 16384
NP = 4
BLK = 512
IN_W = 7168
EPS = 1e-6
EPOCH = 30000
NREL = 20


class Buf:
    __slots__ = ("w", "r", "dsem", "dram", "psum")

    def __init__(self, dram=False, psum=False):
        self.dram = dram
        self.psum = psum
        self.w = {}
        self.r = {}
        self.dsem = None


class Sched:
    def __init__(self, nc, stack):
        self.nc = nc
        self.stack = stack
        self.names = ["pe", "act", "dve", "pool", "sp"]
        self.ops = {e: [] for e in self.names}
        self.cnt = {e: 0 for e in self.names}
        self.seen = {e: {} for e in self.names}
        self.sems = {}
        self.nsem = 0
        self.dcount = {}
        self.waited = {e: set() for e in self.names}

    def _sem(self, key):
        if key not in self.sems:
            self.nsem += 1
            self.sems[key] = self.stack.enter_context(self.nc.semaphore("s%d" % self.nsem))
        return self.sems[key]

    def _deps(self, e, reads, writes):
        deps = {}
        for b in reads:
            for k, v in b.w.items():
                if deps.get(k, 0) < v:
                    deps[k] = v
            if b.psum:
                for k, v in b.r.items():
                    if k[0] == "c" and k[1] != e and deps.get(k, 0) < v:
                        deps[k] = v
        for b in writes:
            for d in (b.w, b.r):
                for k, v in d.items():
                    if deps.get(k, 0) < v:
                        deps[k] = v
        waits = []
        seen = self.seen[e]
        for k, v in deps.items():
            if k[0] == "c" and k[1] == "pe" and e == "pe":
                continue
            if k[0] == "d":
                v = self.dcount[k]
            if seen.get(k, 0) >= v:
                continue
            seen[k] = v
            if k[0] == "c":
                self.waited[k[1]].add(v)
            waits.append((k, v))
        return waits

    def op(self, e, fn, reads=(), writes=()):
        waits = self._deps(e, reads, writes)
        self.cnt[e] += 1
        n = self.cnt[e]
        key = ("c", e)
        for b in writes:
            b.w = {key: n}
            b.r = {}
        for b in reads:
            if b not in writes:
                b.r[key] = n
        self.ops[e].append((waits, fn, ("c", n)))

    def dma(self, e, fn, reads=(), writes=(), sbuf=None):
        waits = self._deps(e, reads, writes)
        tb = sbuf if sbuf is not None else (writes[0] if writes else reads[0])
        if tb.dsem is None:
            tb.dsem = ("d", id(tb))
            self.dcount[tb.dsem] = 0
        key = tb.dsem
        self.dcount[key] += 16
        val = self.dcount[key]
        for b in writes:
            if b.dram:
                b.w[key] = val
            else:
                b.w = {key: val}
            b.r = {}
        for b in reads:
            if b not in writes:
                b.r[key] = val
        self.ops[e].append((waits, fn, ("d", key)))

    def finish(self):
        nc = self.nc
        import bisect
        for e in self.names:
            if self.cnt[e]:
                self.waited[e].add(self.cnt[e])
        wl = {e: sorted(self.waited[e]) for e in self.names}

        def csem(e, n):
            r = bisect.bisect_left(wl[e], n)
            assert wl[e][r] == n
            return self._sem(("c", e, r // EPOCH)), r % EPOCH + 1
        final = [(self._sem(k), v) for k, v in self.dcount.items()]
        last = {e: csem(e, self.cnt[e]) for e in self.names if self.cnt[e]}
        ops = self.ops
        waited = self.waited

        def run(engine, name):
            for waits, fn, evt in ops[name]:
                for (k, v) in waits:
                    if k[0] == "c":
                        s_, v_ = csem(k[1], v)
                    else:
                        s_, v_ = self._sem(k), v
                    engine.wait_ge(s_, v_)
                ins = fn(engine)
                if evt[0] == "d":
                    ins.then_inc(self._sem(evt[1]), 16)
                elif evt[1] in waited[name]:
                    s_, _ = csem(name, evt[1])
                    ins.then_inc(s_, 1)
            if name == "sp":
                for (s_, v_) in final:
                    engine.wait_ge(s_, v_)
                for e2, (s_, v_) in last.items():
                    if e2 != "sp":
                        engine.wait_ge(s_, v_)

        with nc.Block() as block:
            @block.sync
            def _(eng):
                run(eng, "sp")

            @block.scalar
            def _(eng):
                run(eng, "act")

            @block.vector
            def _(eng):
                run(eng, "dve")

            @block.gpsimd
            def _(eng):
                run(eng, "pool")

            @block.tensor
            def _(eng):
                run(eng, "pe")


class Ring:
    def __init__(self, nc, st, name, shape, dtype, n):
        self.items = []
        for i in range(n):
            t = st.enter_context(nc.sbuf_tensor("rg_%s%d" % (name, i), shape, dtype))
            self.items.append((t, Buf()))
        self.i = 0

    def get(self):
        it = self.items[self.i % len(self.items)]
        self.i += 1
        return it


def weight_order(seq_lens, depth):
    order = []
    for l in range(depth):
        for S in seq_lens:
            nb = S // BLK
            for _ in range(nb):
                order += [("in", l, 512), ("in", l, 1024), ("in", l, 2560), ("in", l, 3072), ("in", l, 3584)]
            for _ in range(nb):
                order += [("in", l, 0), ("in", l, 1536), ("in", l, 2048), ("in", l, 4096), ("in", l, 4608)]
                for oc in range(8):
                    order += [("mg", l, oc)]
                for q in range(2):
                    order += [("wo", l, 512 * q)]
    return order


def build_nc(seq_lens, depth):
    nseq = len(seq_lens)
    SMAX = max(seq_lens)
    nc = bass.Bass("TRN2", target_bir_lowering=False)

    def din(name, shape, dt=F32):
        return nc.dram_tensor(name, list(shape), dt, kind="ExternalInput").ap()

    xin = [din("x%d" % i, [D, S]) for i, S in enumerate(seq_lens)]
    yout = [nc.dram_tensor("y%d" % i, [D, S], F32, kind="ExternalOutput").ap() for i, S in enumerate(seq_lens)]
    cT = din("cT", [128, 8, nseq])
    w_ada = din("w_ada", [depth, D, 3 * D])
    b_adaT = din("b_adaT", [depth, 128, 24])
    norm_gT = din("norm_gT", [depth, 128, 8])
    w_in = din("w_in", [depth, D, IN_W])
    qkg = din("qkg", [128, 2 * depth])
    dlog = din("dlog", [128, 8 * depth])
    retgT = din("retgT", [depth, 128, 8])
    w_pa = din("w_pa", [depth, 512, D])
    w_pb = din("w_pb", [depth, D, D])
    w_o = din("w_o", [depth, D, D])
    cmat = din("cmat", [6, 128, 128])
    cmask = din("cmask", [128, 2944])
    cret = din("cret", [5, 128, 128])
    ccol = din("ccol", [128, 8])
    tabs = {}
    for S in sorted(set(seq_lens)):
        tabs[S] = din("tab%d" % S, [4, 128, S])

    def dscr(name, shape, dt):
        return nc.dram_tensor(name, list(shape), dt).ap()

    XS = [dscr("xs%d" % i, [D, S], F32) for i, S in enumerate(seq_lens)]
    HTd = dscr("HTd", [8, 128, SMAX], BF16)
    KTd = dscr("KTd", [4, 128, SMAX], BF16)
    VAd = dscr("VAd", [SMAX // 128, 128, 528], BF16)
    KRd = dscr("KRd", [4, 128, SMAX], BF16)
    VRd = dscr("VRd", [SMAX // 128, 128, 1024], BF16)
    SBd = dscr("SBd", [SMAX // 128, 128, 1024], BF16)
    B_XS = [Buf(dram=True) for _ in seq_lens]
    B_HTd, B_KTd, B_VAd, B_KRd, B_VRd, B_SBd = (Buf(dram=True) for _ in range(6))

    worder = weight_order(seq_lens, depth)

    with contextlib.ExitStack() as st:
        S_ = Sched(nc, st)
        op, dma = S_.op, S_.dma

        def I(name, *a, **kw):
            return lambda e: getattr(e, name)(*a, **kw)

        def sb(name, shape, dt):
            return st.enter_context(nc.sbuf_tensor("sb_" + name, list(shape), dt)), Buf()

        cm, B_cm = sb("cm", [128, 6, 128], BF16)
        for i in range(5):
            dma("pool", I("dma_start", out=cm[:, i, :], in_=cmat[i]), writes=[B_cm])
        ident, ones_blk, ones_full, RaT, RrT = (cm[:, i, :] for i in range(5))
        onesf, B_onesf = sb("onesf", [128, 64], F32)
        op("dve", I("memset", onesf[:], 1.0), writes=[B_onesf])
        masks, B_masks = sb("masks", [128, 2944], BF16)
        dma("pool", I("dma_start", out=masks[:], in_=cmask), writes=[B_masks])
        cr, B_cr = sb("cr", [128, 5, 128], F32)
        dma("sp", I("dma_start", out=cr[:], in_=cret.rearrange("c p n -> p c n")), writes=[B_cr])
        cc_, B_cc = sb("ccol", [128, 8], F32)
        dma("sp", I("dma_start", out=cc_[:], in_=ccol), writes=[B_cc])
        onecol, epscol = cc_[:, 2:3], cc_[:, 3:4]
        qkg_s, B_qkg = sb("qkg", [128, 2 * depth], F32)
        dma("sp", I("dma_start", out=qkg_s[:], in_=qkg), writes=[B_qkg])
        lg, B_lg = sb("lg", [128, 8 * depth], F32)
        dma("sp", I("dma_start", out=lg[:], in_=dlog), writes=[B_lg])
        op("act", I("activation", out=lg[:], in_=lg[:], func=AF.Exp, scale=-1.0), reads=[B_lg], writes=[B_lg])
        op("act", I("activation", out=lg[:], in_=lg[:], func=AF.Ln, bias=onecol, scale=1.0), reads=[B_lg, B_cc], writes=[B_lg])
        op("dve", I("tensor_scalar", out=lg[:], in0=lg[:], scalar1=-1.0, scalar2=None, op0=ALU.mult), reads=[B_lg], writes=[B_lg])
        csil, B_csil = sb("csil", [128, 8, nseq], F32)
        dma("sp", I("dma_start", out=csil[:], in_=cT), writes=[B_csil])
        op("act", I("activation", out=csil[:], in_=csil[:], func=AF.Silu), reads=[B_csil], writes=[B_csil])

        modT, B_mod = sb("modT", [128, 24, nseq], F32)
        gmod, B_gmod = sb("gmod", [128, 8, nseq], F32)
        bada, B_bada = sb("bada", [128, 24], F32)
        ngT, B_ngT = sb("ngT", [128, 8], F32)
        rgT, B_rgT = sb("rgT", [128, 8], F32)
        MT, B_MT = sb("MT", [128, 4, 128], BF16)
        qdec, B_qdec = sb("qdec", [128, 8, 128], F32)
        kcol, B_kcol = sb("kcol", [128, 8], F32)
        gC, B_gC = sb("gC", [128, 8], F32)
        wada, B_wada = sb("wada", [128, 8, 128], F32)
        RgT, B_RgT = sb("RgT", [128, 2, 128], BF16)
        tmpd, B_tmpd = sb("tmpd", [128, 128], F32)
        tmpd2, B_tmpd2 = sb("tmpd2", [128, 128], F32)

        Sf, B_Sf = sb("Sf", [128, 4, 256], F32)
        B_Sfh = [Buf() for _ in range(4)]
        Sb = Sf
        SFlR = Ring(nc, st, "sfl", [128, 4, 256], BF16, 2)

        WR = Ring(nc, st, "wr", [128, 8, 512], BF16, 5)
        wstate = {"i": 0, "slots": {}}
        PREFETCH = 3

        GK = [("in", 512), ("in", 1024), ("in", 2560), ("in", 3072), ("in", 3584),
              ("in", 0), ("in", 1536), ("in", 2048), ("in", 4096), ("in", 4608)] + \
             [("mg", oc) for oc in range(8)] + [("wo", 0), ("wo", 512)]
        GIDX = {k: i for i, k in enumerate(GK)}
        WB = nc.dram_tensor("WB", [depth, len(GK), 128, 4096], BF16).ap()
        B_WB = Buf(dram=True)
        BP = {}
        for (t_, B_) in WR.items:
            BP[id(B_)] = Buf()
            op("pool", I("memset", t_[:], 0.0), writes=[B_, BP[id(B_)]])
        for l_ in range(depth):
            for g_, (kind, c0) in enumerate(GK):
                t, B0 = WR.get()
                B = BP[id(B0)]
                if kind == "in":
                    src = w_in[l_].rearrange("(kc p) n -> p kc n", p=128)[:, :, c0:c0 + 512]
                    dma("pool", I("dma_start", out=t[:], in_=src), writes=[B])
                elif kind == "wo":
                    src = w_o[l_].rearrange("(kc p) n -> p kc n", p=128)[:, :, c0:c0 + 512]
                    dma("pool", I("dma_start", out=t[:], in_=src), writes=[B])
                else:
                    oc = c0
                    win = w_in[l_].rearrange("(kc p) n -> p kc n", p=128)
                    dma("pool", I("dma_start", out=t[:, :, 0:128], in_=win[:, :, 5120 + 128 * oc:5248 + 128 * oc]), writes=[B])
                    dma("pool", I("dma_start", out=t[:, :, 128:256], in_=win[:, :, 6144 + 128 * oc:6272 + 128 * oc]), writes=[B])
                    dma("pool", I("dma_start", out=t[0:64, :, 256:384], in_=w_pa[l_].rearrange("(h p) n -> p h n", p=64)[:, :, 128 * oc:128 * oc + 128]), writes=[B])
                    dma("pool", I("dma_start", out=t[:, :, 384:512], in_=w_pb[l_].rearrange("(kc p) n -> p kc n", p=128)[:, :, 128 * oc:128 * oc + 128]), writes=[B])
                dma("sp", I("dma_start", out=WB[l_, g_], in_=t[:].rearrange("p k n -> p (k n)")), reads=[B], writes=[B_WB], sbuf=B_WB)

        def wload(idx):
            kind, l, c0 = worder[idx]
            t, B = WR.get()
            dma("sp", I("dma_start", out=t[:].rearrange("p k n -> p (k n)"), in_=WB[l, GIDX[(kind, c0)]]), reads=[B_WB], writes=[B])
            wstate["slots"][idx] = (t, B)

        def wnext(kind, l, c0):
            i = wstate["i"]
            assert worder[i] == (kind, l, c0), (worder[i], kind, l, c0)
            while wstate.get("loaded", 0) < min(len(worder), i + PREFETCH + 1):
                wload(wstate.get("loaded", 0))
                wstate["loaded"] = wstate.get("loaded", 0) + 1
            wstate["i"] = i + 1
            return wstate["slots"].pop(i)

        PS = []
        for i in range(7):
            t = st.enter_context(nc.psum_tensor("ps%d" % i, [128, 512], F32))
            PS.append((t, Buf(psum=True)))
        psT = (st.enter_context(nc.psum_tensor("psT", [128, 1024], BF16)), Buf(psum=True))
        pj = {"i": 0}

        def pproj():
            it = PS[pj["i"] % 2]
            pj["i"] += 1
            return it
        pStat, pRot, pSc0, pSc1, pO = PS[2], PS[3], PS[4], PS[5], PS[6]
        sc = {"i": 0}

        def pscore():
            it = (pSc0, pSc1)[sc["i"] % 2]
            sc["i"] += 1
            return it

        HT, B_HT = sb("HT", [128, 8, 512], BF16)
        TB, B_TB = sb("TB", [128, 4, 512], F32)
        QT, B_QT = sb("QT", [128, 4, 512], BF16)
        QR, B_QR = sb("QR", [128, 4, 512], BF16)
        KTw = Ring(nc, st, "ktw", [128, 2560], BF16, 2)
        VAr = Ring(nc, st, "va", [128, NREL, 132], BF16, 2)
        YAG, B_YAG = sb("YAG", [64, 8, 512], BF16)
        YRG, B_YRG = sb("YRG", [128, 8, 512], BF16)
        VRb, B_VRb = sb("VRb", [128, 4, 1024], BF16)
        MG, B_MG = sb("MG", [128, 8, 512], BF16)
        KRb, B_KRb = sb("KRb", [128, 4, 512], BF16)
        SBlR = Ring(nc, st, "sbl", [128, 4, 256], BF16, 2)
        KD, B_KD = sb("KD", [128, 4, 512], BF16)
        rstd, B_rstd = sb("rstd", [128, 512], F32)
        R32 = Ring(nc, st, "r32", [128, 512], F32, 6)
        R16 = Ring(nc, st, "r16", [128, 512], BF16, 6)
        PR = Ring(nc, st, "pr", [128, 512], BF16, 3)
        SM = Ring(nc, st, "sm", [128, 128], BF16, 3)
        QF = Ring(nc, st, "qf", [128, 256], BF16, 3)
        SBst = Ring(nc, st, "sbst", [128, 1024], BF16, 2)
        VAst = Ring(nc, st, "vast", [128, 528], BF16, 2)
        SZ = Ring(nc, st, "sz", [64, 512], F32, 2)
        XO = Ring(nc, st, "xo", [128, 512], F32, 2)
        XC = Ring(nc, st, "xc", [128, 512], F32, 4)
        for (t, B) in VAst.items:
            op("pool", I("memset", t[:], 1.0), writes=[B])

        def mm(out, lhsT, rhs, start, stop, reads, writes, **kw):
            op("pe", I("matmul", out, lhsT=lhsT, rhs=rhs, start=start, stop=stop, **kw), reads=reads, writes=writes)

        def rsqrt_ps(dst, B_dst, ps, B_ps, scale):
            op("act", I("activation", out=dst, in_=ps, func=AF.Sqrt, bias=epscol, scale=scale), reads=[B_ps, B_cc], writes=[B_dst])
            op("dve", I("reciprocal", out=dst, in_=dst), reads=[B_dst], writes=[B_dst])

        def proj_fm(Wt, B_W, c0, ncols, rhs_of_kc, B_rhs, nk=8, prow=128):
            pt, B_p = pproj()
            for kc in range(nk):
                mm(pt[0:ncols, :], Wt[0:prow, kc, c0:c0 + ncols], rhs_of_kc(kc), kc == 0, kc == nk - 1, [B_W, B_rhs], [B_p])
            return pt, B_p

        def rope(dst, B_dst, src, B_src, RT, tC, tS, mul=None, src_is_psum=True):
            sbf, B_sbf = R16.get()
            op("act", I("activation", out=sbf[:], in_=src, func=AF.Copy), reads=[B_src], writes=[B_sbf])
            mm(pRot[0][:], RT, sbf[:], True, True, [B_cm, B_sbf], [pRot[1]])
            t1, B_t1 = R32.get()
            t2, B_t2 = R32.get()
            if mul is None:
                op("dve", I("tensor_tensor", out=t1[:], in0=src, in1=tC, op=ALU.mult), reads=[B_src, B_TB], writes=[B_t1])
                op("dve", I("tensor_tensor", out=t2[:], in0=pRot[0][:], in1=tS, op=ALU.mult), reads=[pRot[1], B_TB], writes=[B_t2])
            else:
                op("dve", I("scalar_tensor_tensor", out=t1[:], in0=src, scalar=mul, in1=tC, op0=ALU.mult, op1=ALU.mult), reads=[B_src, B_TB, B_cc], writes=[B_t1])
                op("dve", I("scalar_tensor_tensor", out=t2[:], in0=pRot[0][:], scalar=mul, in1=tS, op0=ALU.mult, op1=ALU.mult), reads=[pRot[1], B_TB, B_cc], writes=[B_t2])
            op("pool", I("tensor_tensor", out=dst, in0=t1[:], in1=t2[:], op=ALU.add), reads=[B_t1, B_t2], writes=[B_dst])

        def qknorm_rope(dst, B_dst, pt, B_p, gcol, Rg):
            sq, B_sq = R16.get()
            op("act", I("activation", out=sq[:], in_=pt[:], func=AF.Square), reads=[B_p], writes=[B_sq])
            mm(pStat[0][:], ones_blk, sq[:], True, True, [B_cm, B_sq], [pStat[1]])
            sbf, B_sbf = R16.get()
            op("act", I("activation", out=sbf[:], in_=pt[:], func=AF.Copy), reads=[B_p], writes=[B_sbf])
            mm(pRot[0][:], Rg, sbf[:], True, True, [B_RgT, B_sbf], [pRot[1]])
            t1, B_t1 = R32.get()
            op("dve", I("scalar_tensor_tensor", out=t1[:], in0=pt[:], scalar=gcol, in1=TB[:, 0, :], op0=ALU.mult, op1=ALU.mult), reads=[B_p, B_TB, B_qkg], writes=[B_t1])
            rs, B_rs = R32.get()
            op("act", I("activation", out=rs[:], in_=pStat[0][:], func=AF.Sqrt, bias=epscol, scale=1.0 / 64), reads=[pStat[1], B_cc], writes=[B_rs])
            t2, B_t2 = R32.get()
            op("dve", I("tensor_tensor", out=t2[:], in0=pRot[0][:], in1=TB[:, 1, :], op=ALU.mult), reads=[pRot[1], B_TB], writes=[B_t2])
            op("dve", I("reciprocal", out=rs[:], in_=rs[:]), reads=[B_rs], writes=[B_rs])
            op("pool", I("tensor_tensor", out=t1[:], in0=t1[:], in1=t2[:], op=ALU.add), reads=[B_t1, B_t2], writes=[B_t1])
            op("dve", I("tensor_tensor", out=dst, in0=t1[:], in1=rs[:], op=ALU.mult), reads=[B_t1, B_rs], writes=[B_dst])

        def chunk_pipeline(n, proj_fn, fin_fn):
            cur = proj_fn(0)
            for c in range(n):
                nxt = proj_fn(c + 1) if c + 1 < n else None
                fin_fn(c, cur)
                cur = nxt

        def ktranspose(src_of_h, B_src, n, colsel):
            for h in range(4):
                op("pe", I("transpose", psT[0][:, h * 128:(h + 1) * 128], src_of_h(h), ident), reads=[B_src, B_cm], writes=[psT[1]])
            for h in range(4):
                op("act", I("activation", out=KD[:, n, h * 128:(h + 1) * 128], in_=psT[0][:, h * 128:(h + 1) * 128], func=AF.Copy, scale=kcol[:, colsel + h:colsel + h + 1]), reads=[psT[1], B_kcol], writes=[B_KD])

        import os
        KSTOP = int(os.environ.get("KSTOP", "0"))

        class _Stop(Exception):
            pass
        try:
          for l in range(depth):
              dma("sp", I("dma_start", out=bada[:], in_=b_adaT[l]), writes=[B_bada])
              dma("sp", I("dma_start", out=ngT[:], in_=norm_gT[l]), writes=[B_ngT])
              dma("sp", I("dma_start", out=rgT[:], in_=retgT[l]), writes=[B_rgT])
              for fc in range(24):
                  if True:
                      dma("sp", I("dma_start", out=wada[:], in_=w_ada[l].rearrange("(kc p) n -> p kc n", p=128)[:, :, fc * 128:(fc + 1) * 128]), writes=[B_wada])
                      pt, B_p = pproj()
                      for kc in range(8):
                          mm(pt[:, 0:nseq], wada[:, kc, :], csil[:, kc, :], kc == 0, kc == 7, [B_wada, B_csil], [B_p])
                      op("dve", I("tensor_scalar", out=modT[:, fc, :], in0=pt[:, 0:nseq], scalar1=bada[:, fc:fc + 1], scalar2=None, op0=ALU.add), reads=[B_p, B_bada], writes=[B_mod])
              for c in range(8):
                  op("dve", I("tensor_scalar", out=gmod[:, c, :], in0=modT[:, 8 + c, :], scalar1=1.0, scalar2=ngT[:, c:c + 1], op0=ALU.add, op1=ALU.mult), reads=[B_mod, B_ngT], writes=[B_gmod])
              lo = 8 * l
              for h in range(4):
                  op("dve", I("tensor_scalar", out=tmpd[:], in0=cr[:, 0, :], scalar1=lg[:, lo + h:lo + h + 1], scalar2=None, op0=ALU.mult), reads=[B_cr, B_lg], writes=[B_tmpd])
                  op("dve", I("scalar_tensor_tensor", out=tmpd2[:], in0=cr[:, 1, :], scalar=lg[:, lo + 4 + h:lo + 5 + h], in1=tmpd[:], op0=ALU.mult, op1=ALU.add), reads=[B_cr, B_lg, B_tmpd], writes=[B_tmpd2])
                  op("act", I("activation", out=tmpd2[:], in_=tmpd2[:], func=AF.Exp), reads=[B_tmpd2], writes=[B_tmpd2])
                  op("dve", I("tensor_tensor", out=MT[:, h, :], in0=tmpd2[:], in1=cr[:, 2, :], op=ALU.mult), reads=[B_tmpd2, B_cr], writes=[B_MT])
                  op("act", I("activation", out=qdec[:, h, :], in_=cr[:, 3, :], func=AF.Exp, scale=lg[:, lo + h:lo + h + 1]), reads=[B_cr, B_lg], writes=[B_qdec])
                  op("act", I("activation", out=qdec[:, 4 + h, :], in_=cr[:, 4, :], func=AF.Exp, scale=lg[:, lo + 4 + h:lo + 5 + h]), reads=[B_cr, B_lg], writes=[B_qdec])
                  op("act", I("activation", out=kcol[:, h:h + 1], in_=cc_[:, 0:1], func=AF.Exp, scale=lg[:, lo + h:lo + h + 1]), reads=[B_cc, B_lg], writes=[B_kcol])
                  op("act", I("activation", out=kcol[:, 4 + h:5 + h], in_=cc_[:, 1:2], func=AF.Exp, scale=lg[:, lo + 4 + h:lo + 5 + h]), reads=[B_cc, B_lg], writes=[B_kcol])
              op("act", I("activation", out=gC[:], in_=lg[:, lo:lo + 8], func=AF.Exp, scale=128.0), reads=[B_lg], writes=[B_gC])
              if KSTOP == 1:
                  raise _Stop()
              for j_ in range(2):
                  op("dve", I("tensor_scalar", out=RgT[:, j_, :], in0=RaT, scalar1=qkg_s[:, 2 * l + j_:2 * l + j_ + 1], scalar2=None, op0=ALU.mult), reads=[B_cm, B_qkg], writes=[B_RgT])
              qg = qkg_s[:, 2 * l:2 * l + 1]
              kg = qkg_s[:, 2 * l + 1:2 * l + 2]

              for si, S in enumerate(seq_lens):
                  nb = S // BLK
                  xsrc = xin[si] if l == 0 else XS[si]
                  B_xsrc = None if l == 0 else B_XS[si]
                  xdst = yout[si] if l == depth - 1 else XS[si]
                  B_xdst = None if l == depth - 1 else B_XS[si]
                  tab = tabs[S]
                  rd_x = [B_xsrc] if B_xsrc is not None else []

                  op("dve", I("memset", Sb[:], 0.0), writes=B_Sfh)
                  for b in reversed(range(nb)):
                      t0 = b * BLK
                      dma("sp", I("dma_start", out=TB[:], in_=tab.rearrange("k p t -> p k t")[:, :, t0:t0 + BLK]), writes=[B_TB])
                      for c in range(8):
                          sq, B_sq = R16.get()
                          xc, B_xc = XC.get()
                          dma("sp", I("dma_start", out=xc[:], in_=xsrc[c * 128:(c + 1) * 128, t0:t0 + BLK]), reads=rd_x, writes=[B_xc])
                          op("act", I("activation", out=sq[:], in_=xc[:], func=AF.Square), reads=[B_xc], writes=[B_sq])
                          mm(pStat[0][:], ones_full, sq[:], c == 0, c == 7, [B_cm, B_sq], [pStat[1]])
                      rsqrt_ps(rstd[:], B_rstd, pStat[0][:], pStat[1], 1.0 / D)
                      for c in range(8):
                          t1, B_t1 = R32.get()
                          xc, B_xc = XC.get()
                          dma("sp", I("dma_start", out=xc[:], in_=xsrc[c * 128:(c + 1) * 128, t0:t0 + BLK]), reads=rd_x, writes=[B_xc])
                          op("dve", I("scalar_tensor_tensor", out=t1[:], in0=xc[:], scalar=gmod[:, c, si:si + 1], in1=rstd[:], op0=ALU.mult, op1=ALU.mult), reads=[B_xc, B_gmod, B_rstd], writes=[B_t1])
                          op("act", I("activation", out=HT[:, c, :], in_=t1[:], func=AF.Identity, bias=modT[:, c, si:si + 1], scale=1.0), reads=[B_t1, B_mod], writes=[B_HT])
                      dma("sp", I("dma_start", out=HTd.rearrange("c p t -> p c t")[:, :, t0:t0 + BLK], in_=HT[:]), reads=[B_HT], writes=[B_HTd])
                      hrhs = lambda kc: HT[:, kc, :]
                      Wt, B_W = wnext("in", l, 512)
                      def ka_fin(cc, pp):
                          pt, B_p = pp
                          ks, B_ks = R16.get()
                          qknorm_rope(ks[:], B_ks, pt, B_p, kg, RgT[:, 1, :])
                          dma("sp", I("dma_start", out=KTd[cc][:, t0:t0 + BLK], in_=ks[:]), reads=[B_ks], writes=[B_KTd])
                      chunk_pipeline(4, lambda cc: proj_fm(Wt, B_W, cc * 128, 128, hrhs, B_HT), ka_fin)
                      if KSTOP == 5:
                          raise _Stop()
                      Wt, B_W = wnext("in", l, 1024)
                      for tt in range(4):
                          pt, B_p = pproj()
                          for kc in range(8):
                              mm(pt[:], HT[:, kc, tt * 128:(tt + 1) * 128], Wt[:, kc, :], kc == 0, kc == 7, [B_HT, B_W], [B_p])
                          vs, B_vs = VAst.get()
                          op("act", I("activation", out=vs[:].rearrange("p (h e) -> p h e", e=66)[:, :, 0:64], in_=pt[:].rearrange("p (h e) -> p h e", e=64), func=AF.Copy), reads=[B_p], writes=[B_vs])
                          dma("sp", I("dma_start", out=VAd[4 * b + tt], in_=vs[:]), reads=[B_vs], writes=[B_VAd])
                      if KSTOP == 7:
                          raise _Stop()
                      Wt, B_W = wnext("in", l, 2560)
                      chunk_pipeline(4, lambda h: proj_fm(Wt, B_W, h * 128, 128, hrhs, B_HT),
                                     lambda h, pp: rope(KRb[:, h, :], B_KRb, pp[0][:], pp[1], RrT, TB[:, 2, :], TB[:, 3, :], mul=cc_[:, 4:5]))
                      dma("sp", I("dma_start", out=KRd.rearrange("h p t -> p h t")[:, :, t0:t0 + BLK], in_=KRb[:]), reads=[B_KRb], writes=[B_KRd])
                      if KSTOP == 8:
                          raise _Stop()
                      for g in range(2):
                          Wt, B_W = wnext("in", l, 3072 + 512 * g)
                          for tt in range(4):
                              pt, B_p = pproj()
                              for kc in range(8):
                                  mm(pt[:], HT[:, kc, tt * 128:(tt + 1) * 128], Wt[:, kc, :], kc == 0, kc == 7, [B_HT, B_W], [B_p])
                              op("act", I("activation", out=VRb[:, tt, g * 512:(g + 1) * 512], in_=pt[:], func=AF.Copy), reads=[B_p], writes=[B_VRb])
                      dma("sp", I("dma_start", out=VRd[4 * b:4 * b + 4].rearrange("t p n -> p t n"), in_=VRb[:]), reads=[B_VRb], writes=[B_VRd])
                      if KSTOP == 6:
                          raise _Stop()
                      for n in reversed(range(4)):
                          ktranspose(lambda h, n=n: KRb[:, h, n * 128:(n + 1) * 128], B_KRb, n, 4)
                          stg, B_stg = SBst.get()
                          op("pool", I("tensor_copy", out=stg[:], in_=Sb[:].rearrange("p h n -> p (h n)")), reads=B_Sfh, writes=[B_stg])
                          dma("sp", I("dma_start", out=SBd[4 * b + n], in_=stg[:]), reads=[B_stg], writes=[B_SBd])
                          for h in range(4):
                              pt, B_p = pproj()
                              mm(pt[:, 0:256], KD[:, n, h * 128:(h + 1) * 128], VRb[:, n, h * 256:(h + 1) * 256], True, True, [B_KD, B_VRb], [B_p])
                              op("dve", I("scalar_tensor_tensor", out=Sb[:, h, :], in0=Sb[:, h, :], scalar=gC[:, 4 + h:5 + h], in1=pt[:, 0:256], op0=ALU.mult, op1=ALU.add), reads=[B_p, B_gC, B_Sfh[h]], writes=[B_Sfh[h]])

                  if KSTOP == 2:
                      raise _Stop()
                  op("dve", I("memset", Sf[:], 0.0), writes=B_Sfh)
                  ntile = S // 128
                  def block_loads(bb):
                      tt0 = bb * BLK
                      dma("sp", I("dma_start", out=HT[:], in_=HTd.rearrange("c p t -> p c t")[:, :, tt0:tt0 + BLK]), reads=[B_HTd], writes=[B_HT])
                      dma("sp", I("dma_start", out=TB[:], in_=tab.rearrange("k p t -> p k t")[:, :, tt0:tt0 + BLK]), writes=[B_TB])
                      dma("sp", I("dma_start", out=VRb[:], in_=VRd[4 * bb:4 * bb + 4].rearrange("t p n -> p t n")), reads=[B_VRd], writes=[B_VRb])
                      dma("sp", I("dma_start", out=KRb[:], in_=KRd.rearrange("h p t -> p h t")[:, :, tt0:tt0 + BLK]), reads=[B_KRd], writes=[B_KRb])
                  block_loads(0)
                  for b in range(nb):
                      t0 = b * BLK
                      ulo = max(0, 4 * b - 8)
                      uhi = min(ntile, 4 * b + 12)
                      hrhs = lambda kc: HT[:, kc, :]
                      Wt, B_W = wnext("in", l, 0)
                      chunk_pipeline(4, lambda cc: proj_fm(Wt, B_W, cc * 128, 128, hrhs, B_HT),
                                     lambda cc, pp: qknorm_rope(QT[:, cc, :], B_QT, pp[0], pp[1], qg, RgT[:, 0, :]))
                      Wz, B_Wz = wnext("in", l, 1536)
                      po_alt = [pO, pStat]

                      def load_pair(cc):
                          ktw, B_ktw = KTw.get()
                          VA, B_VA = VAr.get()
                          dma("sp", I("dma_start", out=ktw[:, 0:(uhi - ulo) * 128], in_=KTd[cc][:, ulo * 128:uhi * 128]), reads=[B_KTd], writes=[B_ktw])
                          dma("sp", I("dma_start", out=VA[:, 0:uhi - ulo, :], in_=VAd[ulo:uhi].rearrange("t p n -> p t n")[:, :, cc * 132:(cc + 1) * 132]), reads=[B_VAd], writes=[B_VA])
                          return ktw, B_ktw, VA, B_VA

                      def normalise_steps(h, pOh, B_pOh, sz, B_sz):
                          rd, B_rd = R32.get()
                          op("dve", I("reciprocal", out=rd[64:65, :], in_=pOh[64:65, :]), reads=[B_pOh], writes=[B_rd])
                          rh, B_rh = R16.get()
                          rl, B_rl = R16.get()
                          op("dve", I("tensor_copy", out=rh[64:65, :], in_=rd[64:65, :]), reads=[B_rd], writes=[B_rh])
                          op("dve", I("tensor_tensor", out=rl[64:65, :], in0=rd[64:65, :], in1=rh[64:65, :], op=ALU.subtract), reads=[B_rd, B_rh], writes=[B_rl])
                          yield
                          yield
                          mm(pRot[0][0:64, :], ones_full[64:65, 0:64], rh[64:65, :], True, False, [B_cm, B_rh], [pRot[1]])
                          mm(pRot[0][0:64, :], ones_full[64:65, 0:64], rl[64:65, :], False, True, [B_cm, B_rl], [pRot[1]])
                          yield
                          yield
                          bc, B_bc = R32.get()
                          op("act", I("activation", out=bc[0:64, :], in_=pRot[0][0:64, :], func=AF.Copy), reads=[pRot[1]], writes=[B_bc])
                          yield
                          yield
                          ya, B_ya = R32.get()
                          op("dve", I("tensor_tensor", out=ya[0:64, :], in0=pOh[0:64, :], in1=bc[0:64, :], op=ALU.mult), reads=[B_pOh, B_bc], writes=[B_ya])
                          yield
                          op("pool", I("tensor_tensor", out=YAG[:, h, :], in0=ya[0:64, :], in1=sz[0:64, :], op=ALU.mult), reads=[B_ya, B_sz], writes=[B_YAG])
                          yield

                      def run_all(gen):
                          for _ in gen:
                              pass
                      tiles = list(range(ulo, uhi))
                      nt = len(tiles)
                      items = [(h, i) for h in range(8) for i in range(nt)]
                      pairs = {0: load_pair(0)}
                      sbufs = {}
                      LA = 2

                      def qk(k):
                          h, i = items[k]
                          cc, hp = h // 2, h % 2
                          if cc not in pairs:
                              pairs[cc] = load_pair(cc)
                          ktw, B_ktw, VA, B_VA = pairs[cc]
                          rows = slice(hp * 64, hp * 64 + 64)
                          u = tiles[i]
                          ps_s, B_s = pscore()
                          mm(ps_s[:], ktw[rows, (u - ulo) * 128:(u - ulo + 1) * 128], QT[rows, cc, :], True, True, [B_ktw, B_QT], [B_s])
                          sbufs[k] = (ps_s, B_s)
                      for k in range(LA):
                          qk(k)
                      pending = None
                      cur = None
                      ngen = None
                      zpend = None
                      for k, (h, i) in enumerate(items):
                          cc, hp = h // 2, h % 2
                          u = tiles[i]
                          if i == 0:
                              if hp == 0 and cc + 1 < 4 and (cc + 1) not in pairs:
                                  pairs[cc + 1] = load_pair(cc + 1)
                              pOh, B_pOh = po_alt[h % 2]
                              pt, B_p = proj_fm(Wz, B_Wz, h * 64, 64, hrhs, B_HT)
                              sz, B_sz = SZ.get()
                              cur = (h, pOh, B_pOh, sz, B_sz)
                              zpend = (pt, B_p, sz, B_sz)
                          if i == 2:
                              op("act", I("activation", out=zpend[2][0:64, :], in_=zpend[0][0:64, :], func=AF.Silu), reads=[zpend[1]], writes=[zpend[3]])
                          ktw, B_ktw, VA, B_VA = pairs[cc]
                          ps_s, B_s = sbufs.pop(k)
                          pr, B_pr = PR.get()
                          op("act", I("activation", out=pr[:], in_=ps_s[:], func=AF.Exp, scale=0.125), reads=[B_s], writes=[B_pr])
                          rel = u - 4 * b + 8
                          op("dve", I("tensor_tensor", out=pr[:], in0=pr[:], in1=masks[:, 2432 - 128 * rel:2944 - 128 * rel], op=ALU.mult), reads=[B_pr, B_masks], writes=[B_pr])
                          if k + LA < len(items):
                              qk(k + LA)
                          mm(cur[1][0:65, :], VA[:, u - ulo, hp * 66:hp * 66 + 65], pr[:], i == 0, i == nt - 1, [B_VA, B_pr], [cur[2]])
                          if i == 1 and pending is not None:
                              ngen = normalise_steps(*pending)
                              pending = None
                          if ngen is not None and i >= 1:
                              if next(ngen, "end") == "end":
                                  ngen = None
                          if i == nt - 1:
                              if ngen is not None:
                                  run_all(ngen)
                                  ngen = None
                              pending = cur
                      run_all(normalise_steps(*pending))
                      if KSTOP == 3:
                          raise _Stop()
                      Wt, B_W = wnext("in", l, 2048)
                      chunk_pipeline(4, lambda h: proj_fm(Wt, B_W, h * 128, 128, hrhs, B_HT),
                                     lambda h, pp: rope(QR[:, h, :], B_QR, pp[0][:], pp[1], RrT, TB[:, 2, :], TB[:, 3, :]))
                      Wzr = [wnext("in", l, 4096), wnext("in", l, 4608)]
                      for n in range(4):
                          ktranspose(lambda h, n=n: KRb[:, h, n * 128:(n + 1) * 128], B_KRb, n, 0)
                      def stateA(h):
                          SFl, B_SFl = SFlR.get()
                          for n in range(4):
                              op("act", I("activation", out=SFl[:, n, :], in_=Sf[:, h, :], func=AF.Copy), reads=[B_Sfh[h]], writes=[B_SFl])
                              pd, B_pd = pproj()
                              mm(pd[:, 0:256], KD[:, n, h * 128:(h + 1) * 128], VRb[:, n, h * 256:(h + 1) * 256], True, True, [B_KD, B_VRb], [B_pd])
                              op("dve", I("scalar_tensor_tensor", out=Sf[:, h, :], in0=Sf[:, h, :], scalar=gC[:, h:h + 1], in1=pd[:, 0:256], op0=ALU.mult, op1=ALU.add), reads=[B_pd, B_gC, B_Sfh[h]], writes=[B_Sfh[h]])
                          return SFl, B_SFl
                      sf_next = stateA(0)
                      for h in range(4):
                          SFl, B_SFl = sf_next
                          if h + 1 < 4:
                              sf_next = stateA(h + 1)
                          po = [pscore(), pscore()]
                          SBl, B_SBl = SBlR.get()
                          dma("sp", I("dma_start", out=SBl[:], in_=SBd[4 * b:4 * b + 4].rearrange("t p n -> p t n")[:, :, h * 256:(h + 1) * 256]), reads=[B_SBd], writes=[B_SBl])
                          scd = {}

                          def scores(n):
                              cs = slice(n * 128, (n + 1) * 128)
                              ps_s, B_s = pproj()
                              mm(ps_s[:, 0:128], KRb[:, h, cs], QR[:, h, cs], True, True, [B_KRb, B_QR], [B_s])
                              smt, B_sm = SM.get()
                              op("dve", I("tensor_tensor", out=smt[:], in0=ps_s[:, 0:128], in1=MT[:, h, :], op=ALU.mult), reads=[B_s, B_MT], writes=[B_sm])
                              qf, B_qf = QF.get()
                              op("pool", I("tensor_tensor", out=qf[:, 0:128], in0=QR[:, h, cs], in1=qdec[:, h, :], op=ALU.mult), reads=[B_QR, B_qdec], writes=[B_qf])
                              op("pool", I("tensor_tensor", out=qf[:, 128:256], in0=QR[:, h, cs], in1=qdec[:, 4 + h, :], op=ALU.mult), reads=[B_QR, B_qdec], writes=[B_qf])
                              scd[n] = (smt, B_sm, qf, B_qf)
                          scores(0)
                          for n in range(4):
                              cs = slice(n * 128, (n + 1) * 128)
                              if n + 1 < 4:
                                  scores(n + 1)
                              smt, B_sm, qf, B_qf = scd.pop(n)
                              for ev in range(2):
                                  vs_ = slice(h * 256 + ev * 128, h * 256 + ev * 128 + 128)
                                  pt_o, B_o = po[ev]
                                  mm(pt_o[:, cs], VRb[:, n, vs_], smt[:], True, False, [B_VRb, B_sm], [B_o])
                                  mm(pt_o[:, cs], SFl[:, n, ev * 128:(ev + 1) * 128], qf[:, 0:128], False, False, [B_SFl, B_qf], [B_o])
                                  mm(pt_o[:, cs], SBl[:, n, ev * 128:(ev + 1) * 128], qf[:, 128:256], False, True, [B_SBl, B_qf], [B_o])
                          yb = [R16.get(), R16.get()]
                          ysq = [R16.get(), R16.get()]
                          for ev in range(2):
                              op("act", I("activation", out=yb[ev][0][:], in_=po[ev][0][:], func=AF.Copy), reads=[po[ev][1]], writes=[yb[ev][1]])
                              op("act", I("activation", out=ysq[ev][0][:], in_=po[ev][0][:], func=AF.Square), reads=[po[ev][1]], writes=[ysq[ev][1]])
                          for ev in range(2):
                              mm(pStat[0][:], ones_full, yb[ev][0][:], ev == 0, ev == 1, [B_cm, yb[ev][1]], [pStat[1]])
                          for ev in range(2):
                              mm(pRot[0][:], ones_full, ysq[ev][0][:], ev == 0, ev == 1, [B_cm, ysq[ev][1]], [pRot[1]])
                          mean, B_mean = R32.get()
                          op("dve", I("tensor_scalar", out=mean[:], in0=pStat[0][:], scalar1=1.0 / 256, scalar2=None, op0=ALU.mult), reads=[pStat[1]], writes=[B_mean])
                          msq, B_msq = R32.get()
                          op("pool", I("tensor_tensor", out=msq[:], in0=mean[:], in1=mean[:], op=ALU.mult), reads=[B_mean], writes=[B_msq])
                          var, B_var = R32.get()
                          op("dve", I("scalar_tensor_tensor", out=var[:], in0=pRot[0][:], scalar=1.0 / 256, in1=msq[:], op0=ALU.mult, op1=ALU.subtract), reads=[pRot[1], B_msq], writes=[B_var])
                          op("act", I("activation", out=var[:], in_=var[:], func=AF.Sqrt, bias=epscol, scale=1.0), reads=[B_var, B_cc], writes=[B_var])
                          op("dve", I("reciprocal", out=var[:], in_=var[:]), reads=[B_var], writes=[B_var])
                          for ev in range(2):
                              c8 = 2 * h + ev
                              Wzt, B_Wzt = Wzr[c8 // 4]
                              pt, B_p = proj_fm(Wzt, B_Wzt, (c8 % 4) * 128, 128, hrhs, B_HT)
                              sz, B_sz = R32.get()
                              op("act", I("activation", out=sz[:], in_=pt[:], func=AF.Silu), reads=[B_p], writes=[B_sz])
                              d1, B_d1 = R32.get()
                              op("dve", I("tensor_tensor", out=d1[:], in0=po[ev][0][:], in1=mean[:], op=ALU.subtract), reads=[po[ev][1], B_mean], writes=[B_d1])
                              op("dve", I("scalar_tensor_tensor", out=d1[:], in0=d1[:], scalar=rgT[:, c8:c8 + 1], in1=var[:], op0=ALU.mult, op1=ALU.mult), reads=[B_d1, B_var, B_rgT], writes=[B_d1])
                              op("pool", I("tensor_tensor", out=YRG[:, c8, :], in0=d1[:], in1=sz[:], op=ALU.mult), reads=[B_d1, B_sz], writes=[B_YRG])
                      if KSTOP == 4:
                          raise _Stop()
                      xcs = {}

                      def xload(o_):
                          xc_, B_xc_ = XC.get()
                          dma("sp", I("dma_start", out=xc_[:], in_=xsrc[o_ * 128:(o_ + 1) * 128, t0:t0 + BLK]), reads=rd_x, writes=[B_xc_])
                          xcs[o_] = (xc_, B_xc_)
                      for o_ in range(3):
                          xload(o_)
                      for oc in range(8):
                          Wm, B_Wm = wnext("mg", l, oc)
                          pa_, B_pa = proj_fm(Wm, B_Wm, 256, 128, lambda hh: YAG[:, hh, :], B_YAG, nk=8, prow=64)
                          pga, B_pga = pStat
                          for kc in range(8):
                              mm(pga[:], Wm[:, kc, 0:128], HT[:, kc, :], kc == 0, kc == 7, [B_Wm, B_HT], [B_pga])
                          sa, B_sa = R32.get()
                          op("act", I("activation", out=sa[:], in_=pga[:], func=AF.Sigmoid), reads=[B_pga], writes=[B_sa])
                          m1, B_m1 = R32.get()
                          op("dve", I("tensor_tensor", out=m1[:], in0=pa_[:], in1=sa[:], op=ALU.mult), reads=[B_pa, B_sa], writes=[B_m1])
                          pb_, B_pb = proj_fm(Wm, B_Wm, 384, 128, lambda c: YRG[:, c, :], B_YRG)
                          pgr, B_pgr = pRot
                          for kc in range(8):
                              mm(pgr[:], Wm[:, kc, 128:256], HT[:, kc, :], kc == 0, kc == 7, [B_Wm, B_HT], [B_pgr])
                          sr, B_sr = R32.get()
                          op("act", I("activation", out=sr[:], in_=pgr[:], func=AF.Sigmoid), reads=[B_pgr], writes=[B_sr])
                          m2, B_m2 = R32.get()
                          op("dve", I("tensor_tensor", out=m2[:], in0=pb_[:], in1=sr[:], op=ALU.mult), reads=[B_pb, B_sr], writes=[B_m2])
                          op("pool", I("tensor_tensor", out=MG[:, oc, :], in0=m1[:], in1=m2[:], op=ALU.add), reads=[B_m1, B_m2], writes=[B_MG])
                      if b + 1 < nb:
                          block_loads(b + 1)
                      for q in range(2):
                          Wo, B_Wo = wnext("wo", l, 512 * q)
                          for j in range(4):
                              oc = 4 * q + j
                              if oc + 3 < 8:
                                  xload(oc + 3)
                              xc, B_xc = xcs.pop(oc)
                              po_, B_po = proj_fm(Wo, B_Wo, j * 128, 128, lambda c: MG[:, c, :], B_MG)
                              xo, B_xo = XO.get()
                              op("dve", I("scalar_tensor_tensor", out=xo[:], in0=po_[:], scalar=modT[:, 16 + oc, si:si + 1], in1=xc[:], op0=ALU.mult, op1=ALU.add), reads=[B_po, B_xc, B_mod], writes=[B_xo])
                              wr_x = [B_xdst] if B_xdst is not None else []
                              dma("act", I("dma_start", out=xdst[oc * 128:(oc + 1) * 128, t0:t0 + BLK], in_=xo[:]), reads=[B_xo], writes=wr_x, sbuf=B_xo)
        except _Stop:
            pass
        if not KSTOP:
            assert wstate["i"] == len(worder)
        print('sbuf_left', nc.sbuf_bytes_remaining, 'nsem', S_.nsem, {e: S_.cnt[e] for e in S_.names})
        S_.finish()
    return nc


def _tables(S):
    pos = np.arange(S, dtype=np.float32)
    p = np.arange(128)
    tab = np.zeros((4, 128, S), np.float32)
    fa = np.exp(np.float32(-math.log(500000.0)) * np.arange(8, dtype=np.float32) / np.float32(8)).astype(np.float32)
    e = p % 64
    tab[0] = 1.0
    for pp in range(128):
        ee = e[pp]
        if ee < 16:
            ang = (pos * fa[ee % 8]).astype(np.float32).astype(np.float64)
            tab[0, pp] = np.cos(ang)
            tab[1, pp] = (-np.sin(ang)) if ee < 8 else np.sin(ang)
    fr = np.exp(np.float32(-math.log(10000.0)) * np.arange(64, dtype=np.float32) / np.float32(64)).astype(np.float32)
    for pp in range(128):
        ang = (pos * fr[pp % 64]).astype(np.float32).astype(np.float64)
        tab[2, pp] = np.cos(ang)
        tab[3, pp] = (-np.sin(ang)) if pp < 64 else np.sin(ang)
    return tab


def _consts():
    p = np.arange(128)
    cmat = np.zeros((6, 128, 128), np.float32)
    cmat[0] = np.eye(128)
    cmat[1] = (p[:, None] // 64 == p[None, :] // 64)
    cmat[2] = 1.0
    for m in range(128):
        e = m % 64
        if e < 8:
            cmat[3, m + 8, m] = 1.0
        elif e < 16:
            cmat[3, m - 8, m] = 1.0
        cmat[4, (m + 64) % 128, m] = 1.0
    d = p[:, None] - np.arange(2944)[None, :] + 1408
    ad = np.abs(d)
    cmask = ((ad <= 64).astype(np.float32) + ((d % 4 == 0) & (ad <= 256)) + ((d % 16 == 0) & (ad <= 1024))).astype(np.float32)
    cret = np.zeros((5, 128, 128), np.float32)
    dd = (np.arange(128)[None, :] - p[:, None]).astype(np.float32)
    cret[0] = np.maximum(dd, 0)
    cret[1] = np.maximum(-dd, 0)
    cret[2] = 1.0 + (dd == 0)
    cret[3] = np.arange(128)[None, :] + 1.0
    cret[4] = 128.0 - np.arange(128)[None, :]
    ccol = np.zeros((128, 8), np.float32)
    ccol[:, 4] = 128.0 ** -0.5
    ccol[:, 0] = 127 - p
    ccol[:, 1] = p
    ccol[:, 2] = 1.0
    ccol[:, 3] = EPS
    return cmat, cmask, cret, ccol


_CACHE = {}


def kernel(x_prompt, x_sample, c_prompt, c_sample, norm_g, w_ada, b_ada, w_in, q_norm_g, k_norm_g,
           ret_decay_logit, ret_norm_g, w_proj_a, w_proj_b, w_out):
    n_cores = 8
    x_prompt = np.asarray(x_prompt, np.float32)
    x_sample = np.asarray(x_sample, np.float32)
    depth = int(np.asarray(norm_g).shape[0])
    seq_lens = [x_prompt.shape[1]] * NP + [x_sample.shape[1]]
    key = (tuple(seq_lens), depth)
    if key not in _CACHE:
        _CACHE[key] = build_nc(seq_lens, depth)
    nc = _CACHE[key]
    cmat, cmask, cret, ccol = _consts()
    tabs = {S: _tables(S) for S in sorted(set(seq_lens))}
    f = lambda a: np.ascontiguousarray(np.asarray(a, np.float32))

    def colT(a, n):
        a = f(a)
        return np.ascontiguousarray(a.reshape(a.shape[0], n, 128).transpose(0, 2, 1))
    qkg = np.zeros((128, 2 * depth), np.float32)
    for l in range(depth):
        qkg[:, 2 * l] = np.tile(f(q_norm_g)[l], 2)
        qkg[:, 2 * l + 1] = np.tile(f(k_norm_g)[l], 2)
    dlog = np.ascontiguousarray(np.broadcast_to(f(ret_decay_logit).reshape(1, -1), (128, 8 * depth)))
    xsT = np.ascontiguousarray(x_sample[0].T)
    common = {
        "w_ada": f(w_ada), "b_adaT": colT(b_ada, 24), "norm_gT": colT(norm_g, 8), "w_in": f(w_in),
        "qkg": qkg, "dlog": dlog, "retgT": colT(ret_norm_g, 8), "w_pa": f(w_proj_a), "w_pb": f(w_proj_b),
        "w_o": f(w_out), "cmat": cmat, "cmask": cmask, "cret": cret, "ccol": ccol,
    }
    for S, t in tabs.items():
        common["tab%d" % S] = t
    in_maps = []
    for c in range(n_cores):
        m = dict(common)
        cs = np.concatenate([f(c_prompt)[c * NP:(c + 1) * NP], f(c_sample)], axis=0)
        m["cT"] = np.ascontiguousarray(cs.reshape(NP + 1, 8, 128).transpose(2, 1, 0))
        for i in range(NP):
            m["x%d" % i] = np.ascontiguousarray(x_prompt[c * NP + i].T)
        m["x%d" % NP] = xsT
        in_maps.append(m)
    res = run_bass_kernel_spmd(nc, in_maps, core_ids=list(range(n_cores)))
    y_prompt = np.empty_like(x_prompt)
    for c in range(n_cores):
        for i in range(NP):
            y_prompt[c * NP + i] = res.results[c]["y%d" % i].T
    y_sample = np.ascontiguousarray(res.results[0]["y%d" % NP].T)[None]
    return (y_prompt, y_sample.astype(np.float32))
```

```python
import contextlib
import math
import numpy as np
import concourse.bass as bass
import concourse.mybir as mybir
from concourse.bass_utils import run_bass_kernel_spmd

F32 = mybir.dt.float32
BF16 = mybir.dt.bfloat16
ALU = mybir.AluOpType
AF = mybir.ActivationFunctionType

D = 1024
DEPTH = 4
SEQ = 2048
DEC_SEQ = 16384
NP = 4
BLK = 512
IN_W = 7168
EPS = 1e-6
EPOCH = 30000
NREL = 20


class Buf:
    __slots__ = ("w", "r", "dsem", "dram", "psum")

    def __init__(self, dram=False, psum=False):
        self.dram = dram
        self.psum = psum
        self.w = {}
        self.r = {}
        self.dsem = None


class Sched:
    def __init__(self, nc, stack):
        self.nc = nc
        self.stack = stack
        self.names = ["pe", "act", "dve", "pool", "sp"]
        self.ops = {e: [] for e in self.names}
        self.cnt = {e: 0 for e in self.names}
        self.seen = {e: {} for e in self.names}
        self.sems = {}
        self.nsem = 0
        self.dcount = {}
        self.waited = {e: set() for e in self.names}

    def _sem(self, key):
        if key not in self.sems:
            self.nsem += 1
            self.sems[key] = self.stack.enter_context(self.nc.semaphore("s%d" % self.nsem))
        return self.sems[key]

    def _deps(self, e, reads, writes):
        deps = {}
        for b in reads:
            for k, v in b.w.items():
                if deps.get(k, 0) < v:
                    deps[k] = v
            if b.psum:
                for k, v in b.r.items():
                    if k[0] == "c" and k[1] != e and deps.get(k, 0) < v:
                        deps[k] = v
        for b in writes:
            for d in (b.w, b.r):
                for k, v in d.items():
                    if deps.get(k, 0) < v:
                        deps[k] = v
        waits = []
        seen = self.seen[e]
        for k, v in deps.items():
            if k[0] == "c" and k[1] == "pe" and e == "pe":
                continue
            if k[0] == "d":
                v = self.dcount[k]
            if seen.get(k, 0) >= v:
                continue
            seen[k] = v
            if k[0] == "c":
                self.waited[k[1]].add(v)
            waits.append((k, v))
        return waits

    def op(self, e, fn, reads=(), writes=()):
        waits = self._deps(e, reads, writes)
        self.cnt[e] += 1
        n = self.cnt[e]
        key = ("c", e)
        for b in writes:
            b.w = {key: n}
            b.r = {}
        for b in reads:
            if b not in writes:
                b.r[key] = n
        self.ops[e].append((waits, fn, ("c", n)))

    def dma(self, e, fn, reads=(), writes=(), sbuf=None):
        waits = self._deps(e, reads, writes)
        tb = sbuf if sbuf is not None else (writes[0] if writes else reads[0])
        if tb.dsem is None:
            tb.dsem = ("d", id(tb))
            self.dcount[tb.dsem] = 0
        key = tb.dsem
        self.dcount[key] += 16
        val = self.dcount[key]
        for b in writes:
            if b.dram:
                b.w[key] = val
            else:
                b.w = {key: val}
            b.r = {}
        for b in reads:
            if b not in writes:
                b.r[key] = val
        self.ops[e].append((waits, fn, ("d", key)))

    def finish(self):
        nc = self.nc
        import bisect
        for e in self.names:
            if self.cnt[e]:
                self.waited[e].add(self.cnt[e])
        wl = {e: sorted(self.waited[e]) for e in self.names}

        def csem(e, n):
            r = bisect.bisect_left(wl[e], n)
            assert wl[e][r] == n
            return self._sem(("c", e, r // EPOCH)), r % EPOCH + 1
        final = [(self._sem(k), v) for k, v in self.dcount.items()]
        last = {e: csem(e, self.cnt[e]) for e in self.names if self.cnt[e]}
        ops = self.ops
        waited = self.waited

        def run(engine, name):
            for waits, fn, evt in ops[name]:
                for (k, v) in waits:
                    if k[0] == "c":
                        s_, v_ = csem(k[1], v)
                    else:
                        s_, v_ = self._sem(k), v
                    engine.wait_ge(s_, v_)
                ins = fn(engine)
                if evt[0] == "d":
                    ins.then_inc(self._sem(evt[1]), 16)
                elif evt[1] in waited[name]:
                    s_, _ = csem(name, evt[1])
                    ins.then_inc(s_, 1)
            if name == "sp":
                for (s_, v_) in final:
                    engine.wait_ge(s_, v_)
                for e2, (s_, v_) in last.items():
                    if e2 != "sp":
                        engine.wait_ge(s_, v_)

        with nc.Block() as block:
            @block.sync
            def _(eng):
                run(eng, "sp")

            @block.scalar
            def _(eng):
                run(eng, "act")

            @block.vector
            def _(eng):
                run(eng, "dve")

            @block.gpsimd
            def _(eng):
                run(eng, "pool")

            @block.tensor
            def _(eng):
                run(eng, "pe")


class Ring:
    def __init__(self, nc, st, name, shape, dtype, n):
        self.items = []
        for i in range(n):
            t = st.enter_context(nc.sbuf_tensor("rg_%s%d" % (name, i), shape, dtype))
            self.items.append((t, Buf()))
        self.i = 0

    def get(self):
        it = self.items[self.i % len(self.items)]
        self.i += 1
        return it


def weight_order(seq_lens, depth):
    order = []
    for l in range(depth):
        for S in seq_lens:
            nb = S // BLK
            for _ in range(nb):
                order += [("in", l, 512), ("in", l, 1024), ("in", l, 2560), ("in", l, 3072), ("in", l, 3584)]
            for _ in range(nb):
                order += [("in", l, 0), ("in", l, 1536), ("in", l, 2048), ("in", l, 4096), ("in", l, 4608)]
                for oc in range(8):
                    order += [("mg", l, oc)]
                for q in range(2):
                    order += [("wo", l, 512 * q)]
    return order


def build_nc(seq_lens, depth):
    nseq = len(seq_lens)
    SMAX = max(seq_lens)
    nc = bass.Bass("TRN2", target_bir_lowering=False)

    def din(name, shape, dt=F32):
        return nc.dram_tensor(name, list(shape), dt, kind="ExternalInput").ap()

    xin = [din("x%d" % i, [D, S]) for i, S in enumerate(seq_lens)]
    yout = [nc.dram_tensor("y%d" % i, [D, S], F32, kind="ExternalOutput").ap() for i, S in enumerate(seq_lens)]
    cT = din("cT", [128, 8, nseq])
    w_ada = din("w_ada", [depth, D, 3 * D])
    b_adaT = din("b_adaT", [depth, 128, 24])
    norm_gT = din("norm_gT", [depth, 128, 8])
    w_in = din("w_in", [depth, D, IN_W])
    qkg = din("qkg", [128, 2 * depth])
    dlog = din("dlog", [128, 8 * depth])
    retgT = din("retgT", [depth, 128, 8])
    w_pa = din("w_pa", [depth, 512, D])
    w_pb = din("w_pb", [depth, D, D])
    w_o = din("w_o", [depth, D, D])
    cmat = din("cmat", [6, 128, 128])
    cmask = din("cmask", [128, 2944])
    cret = din("cret", [5, 128, 128])
    ccol = din("ccol", [128, 8])
    tabs = {}
    for S in sorted(set(seq_lens)):
        tabs[S] = din("tab%d" % S, [4, 128, S])

    def dscr(name, shape, dt):
        return nc.dram_tensor(name, list(shape), dt).ap()

    XS = [dscr("xs%d" % i, [D, S], F32) for i, S in enumerate(seq_lens)]
    HTd = dscr("HTd", [8, 128, SMAX], BF16)
    KTd = dscr("KTd", [4, 128, SMAX], BF16)
    VAd = dscr("VAd", [SMAX // 128, 128, 528], BF16)
    KRd = dscr("KRd", [4, 128, SMAX], BF16)
    VRd = dscr("VRd", [SMAX // 128, 128, 1024], BF16)
    SBd = dscr("SBd", [SMAX // 128, 128, 1024], BF16)
    B_XS = [Buf(dram=True) for _ in seq_lens]
    B_HTd, B_KTd, B_VAd, B_KRd, B_VRd, B_SBd = (Buf(dram=True) for _ in range(6))

    worder = weight_order(seq_lens, depth)

    with contextlib.ExitStack() as st:
        S_ = Sched(nc, st)
        op, dma = S_.op, S_.dma

        def I(name, *a, **kw):
            return lambda e: getattr(e, name)(*a, **kw)

        def sb(name, shape, dt):
            return st.enter_context(nc.sbuf_tensor("sb_" + name, list(shape), dt)), Buf()

        cm, B_cm = sb("cm", [128, 6, 128], BF16)
        for i in range(5):
            dma("pool", I("dma_start", out=cm[:, i, :], in_=cmat[i]), writes=[B_cm])
        ident, ones_blk, ones_full, RaT, RrT = (cm[:, i, :] for i in range(5))
        masks, B_masks = sb("masks", [128, 2944], BF16)
        dma("pool", I("dma_start", out=masks[:], in_=cmask), writes=[B_masks])
        cr, B_cr = sb("cr", [128, 5, 128], F32)
        dma("sp", I("dma_start", out=cr[:], in_=cret.rearrange("c p n -> p c n")), writes=[B_cr])
        cc_, B_cc = sb("ccol", [128, 8], F32)
        dma("sp", I("dma_start", out=cc_[:], in_=ccol), writes=[B_cc])
        onecol, epscol = cc_[:, 2:3], cc_[:, 3:4]
        qkg_s, B_qkg = sb("qkg", [128, 2 * depth], F32)
        dma("sp", I("dma_start", out=qkg_s[:], in_=qkg), writes=[B_qkg])
        lg, B_lg = sb("lg", [128, 8 * depth], F32)
        dma("sp", I("dma_start", out=lg[:], in_=dlog), writes=[B_lg])
        op("act", I("activation", out=lg[:], in_=lg[:], func=AF.Exp, scale=-1.0), reads=[B_lg], writes=[B_lg])
        op("act", I("activation", out=lg[:], in_=lg[:], func=AF.Ln, bias=onecol, scale=1.0), reads=[B_lg, B_cc], writes=[B_lg])
        op("dve", I("tensor_scalar", out=lg[:], in0=lg[:], scalar1=-1.0, scalar2=None, op0=ALU.mult), reads=[B_lg], writes=[B_lg])
        csil, B_csil = sb("csil", [128, 8, nseq], F32)
        dma("sp", I("dma_start", out=csil[:], in_=cT), writes=[B_csil])
        op("act", I("activation", out=csil[:], in_=csil[:], func=AF.Silu), reads=[B_csil], writes=[B_csil])

        modT, B_mod = sb("modT", [128, 24, nseq], F32)
        gmod, B_gmod = sb("gmod", [128, 8, nseq], F32)
        bada, B_bada = sb("bada", [128, 24], F32)
        ngT, B_ngT = sb("ngT", [128, 8], F32)
        rgT, B_rgT = sb("rgT", [128, 8], F32)
        MT, B_MT = sb("MT", [128, 4, 128], BF16)
        qdec, B_qdec = sb("qdec", [128, 8, 128], F32)
        kcol, B_kcol = sb("kcol", [128, 8], F32)
        gC, B_gC = sb("gC", [128, 8], F32)
        wada, B_wada = sb("wada", [128, 8, 128], F32)
        RgT, B_RgT = sb("RgT", [128, 2, 128], BF16)
        tmpd, B_tmpd = sb("tmpd", [128, 128], F32)
        tmpd2, B_tmpd2 = sb("tmpd2", [128, 128], F32)

        Sf, B_Sf = sb("Sf", [128, 4, 256], F32)
        B_Sfh = [Buf() for _ in range(4)]
        Sb = Sf
        SFlR = Ring(nc, st, "sfl", [128, 4, 256], BF16, 2)

        WR = Ring(nc, st, "wr", [128, 8, 512], BF16, 5)
        wstate = {"i": 0, "slots": {}}
        PREFETCH = 3

        GK = [("in", 512), ("in", 1024), ("in", 2560), ("in", 3072), ("in", 3584),
              ("in", 0), ("in", 1536), ("in", 2048), ("in", 4096), ("in", 4608)] + \
             [("mg", oc) for oc in range(8)] + [("wo", 0), ("wo", 512)]
        GIDX = {k: i for i, k in enumerate(GK)}
        WB = nc.dram_tensor("WB", [depth, len(GK), 128, 4096], BF16).ap()
        B_WB = Buf(dram=True)
        BP = {}
        for (t_, B_) in WR.items:
            BP[id(B_)] = Buf()
            op("pool", I("memset", t_[:], 0.0), writes=[B_, BP[id(B_)]])
        for l_ in range(depth):
            for g_, (kind, c0) in enumerate(GK):
                t, B0 = WR.get()
                B = BP[id(B0)]
                if kind == "in":
                    src = w_in[l_].rearrange("(kc p) n -> p kc n", p=128)[:, :, c0:c0 + 512]
                    dma("pool", I("dma_start", out=t[:], in_=src), writes=[B])
                elif kind == "wo":
                    src = w_o[l_].rearrange("(kc p) n -> p kc n", p=128)[:, :, c0:c0 + 512]
                    dma("pool", I("dma_start", out=t[:], in_=src), writes=[B])
                else:
                    oc = c0
                    win = w_in[l_].rearrange("(kc p) n -> p kc n", p=128)
                    dma("pool", I("dma_start", out=t[:, :, 0:128], in_=win[:, :, 5120 + 128 * oc:5248 + 128 * oc]), writes=[B])
                    dma("pool", I("dma_start", out=t[:, :, 128:256], in_=win[:, :, 6144 + 128 * oc:6272 + 128 * oc]), writes=[B])
                    dma("pool", I("dma_start", out=t[0:64, :, 256:384], in_=w_pa[l_].rearrange("(h p) n -> p h n", p=64)[:, :, 128 * oc:128 * oc + 128]), writes=[B])
                    dma("pool", I("dma_start", out=t[:, :, 384:512], in_=w_pb[l_].rearrange("(kc p) n -> p kc n", p=128)[:, :, 128 * oc:128 * oc + 128]), writes=[B])
                dma("sp", I("dma_start", out=WB[l_, g_], in_=t[:].rearrange("p k n -> p (k n)")), reads=[B], writes=[B_WB], sbuf=B_WB)

        def wload(idx):
            kind, l, c0 = worder[idx]
            t, B = WR.get()
            dma("sp", I("dma_start", out=t[:].rearrange("p k n -> p (k n)"), in_=WB[l, GIDX[(kind, c0)]]), reads=[B_WB], writes=[B])
            wstate["slots"][idx] = (t, B)

        def wnext(kind, l, c0):
            i = wstate["i"]
            assert worder[i] == (kind, l, c0), (worder[i], kind, l, c0)
            while wstate.get("loaded", 0) < min(len(worder), i + PREFETCH + 1):
                wload(wstate.get("loaded", 0))
                wstate["loaded"] = wstate.get("loaded", 0) + 1
            wstate["i"] = i + 1
            return wstate["slots"].pop(i)

        PS = []
        for i in range(7):
            t = st.enter_context(nc.psum_tensor("ps%d" % i, [128, 512], F32))
            PS.append((t, Buf(psum=True)))
        psT = (st.enter_context(nc.psum_tensor("psT", [128, 1024], BF16)), Buf(psum=True))
        pj = {"i": 0}

        def pproj():
            it = PS[pj["i"] % 2]
            pj["i"] += 1
            return it
        pStat, pRot, pSc0, pSc1, pO = PS[2], PS[3], PS[4], PS[5], PS[6]
        sc = {"i": 0}

        def pscore():
            it = (pSc0, pSc1)[sc["i"] % 2]
            sc["i"] += 1
            return it

        HT, B_HT = sb("HT", [128, 8, 512], BF16)
        TB, B_TB = sb("TB", [128, 4, 512], F32)
        QT, B_QT = sb("QT", [128, 8, 512], BF16)
        op("pool", I("memset", QT[:], 0.0), writes=[B_QT])
        QR, B_QR = sb("QR", [128, 4, 512], BF16)
        KTw = Ring(nc, st, "ktw", [128, 2560], BF16, 2)
        VAr = Ring(nc, st, "va", [128, NREL + 1, 132], BF16, 2)
        for (t_, B_) in VAr.items:
            op("pool", I("memset", t_[:], 0.0), writes=[B_])
        YAG, B_YAG = sb("YAG", [64, 8, 512], BF16)
        YRG, B_YRG = sb("YRG", [128, 8, 512], BF16)
        VRb, B_VRb = sb("VRb", [128, 4, 1024], BF16)
        MG, B_MG = sb("MG", [128, 8, 512], BF16)
        KRb, B_KRb = sb("KRb", [128, 4, 512], BF16)
        SBlR = Ring(nc, st, "sbl", [128, 4, 256], BF16, 2)
        KD, B_KD = sb("KD", [128, 4, 512], BF16)
        rstd, B_rstd = sb("rstd", [128, 512], F32)
        R32 = Ring(nc, st, "r32", [128, 512], F32, 6)
        R16 = Ring(nc, st, "r16", [128, 512], BF16, 6)
        PR = Ring(nc, st, "pr", [128, 512], BF16, 3)
        SM = Ring(nc, st, "sm", [128, 128], BF16, 3)
        QF = Ring(nc, st, "qf", [128, 256], BF16, 3)
        SBst = Ring(nc, st, "sbst", [128, 1024], BF16, 2)
        VAst = Ring(nc, st, "vast", [128, 528], BF16, 2)
        SZ = Ring(nc, st, "sz", [64, 512], F32, 2)
        XO = Ring(nc, st, "xo", [128, 512], F32, 2)
        XC = Ring(nc, st, "xc", [128, 512], F32, 2)
        for (t, B) in VAst.items:
            op("pool", I("memset", t[:], 1.0), writes=[B])

        def mm(out, lhsT, rhs, start, stop, reads, writes, **kw):
            op("pe", I("matmul", out, lhsT=lhsT, rhs=rhs, start=start, stop=stop, **kw), reads=reads, writes=writes)

        def rsqrt_ps(dst, B_dst, ps, B_ps, scale):
            op("act", I("activation", out=dst, in_=ps, func=AF.Sqrt, bias=epscol, scale=scale), reads=[B_ps, B_cc], writes=[B_dst])
            op("dve", I("reciprocal", out=dst, in_=dst), reads=[B_dst], writes=[B_dst])

        def proj_fm(Wt, B_W, c0, ncols, rhs_of_kc, B_rhs, nk=8, prow=128):
            pt, B_p = pproj()
            for kc in range(nk):
                mm(pt[0:ncols, :], Wt[0:prow, kc, c0:c0 + ncols], rhs_of_kc(kc), kc == 0, kc == nk - 1, [B_W, B_rhs], [B_p])
            return pt, B_p

        def rope(dst, B_dst, src, B_src, RT, tC, tS, mul=None, src_is_psum=True):
            sbf, B_sbf = R16.get()
            op("act", I("activation", out=sbf[:], in_=src, func=AF.Copy), reads=[B_src], writes=[B_sbf])
            mm(pRot[0][:], RT, sbf[:], True, True, [B_cm, B_sbf], [pRot[1]])
            t1, B_t1 = R32.get()
            t2, B_t2 = R32.get()
            if mul is None:
                op("dve", I("tensor_tensor", out=t1[:], in0=src, in1=tC, op=ALU.mult), reads=[B_src, B_TB], writes=[B_t1])
                op("dve", I("tensor_tensor", out=t2[:], in0=pRot[0][:], in1=tS, op=ALU.mult), reads=[pRot[1], B_TB], writes=[B_t2])
            else:
                op("dve", I("scalar_tensor_tensor", out=t1[:], in0=src, scalar=mul, in1=tC, op0=ALU.mult, op1=ALU.mult), reads=[B_src, B_TB, B_cc], writes=[B_t1])
                op("dve", I("scalar_tensor_tensor", out=t2[:], in0=pRot[0][:], scalar=mul, in1=tS, op0=ALU.mult, op1=ALU.mult), reads=[pRot[1], B_TB, B_cc], writes=[B_t2])
            op("pool", I("tensor_tensor", out=dst, in0=t1[:], in1=t2[:], op=ALU.add), reads=[B_t1, B_t2], writes=[B_dst])

        def qknorm_rope(dst, B_dst, pt, B_p, gcol, Rg):
            sq, B_sq = R16.get()
            op("act", I("activation", out=sq[:], in_=pt[:], func=AF.Square), reads=[B_p], writes=[B_sq])
            mm(pStat[0][:], ones_blk, sq[:], True, True, [B_cm, B_sq], [pStat[1]])
            sbf, B_sbf = R16.get()
            op("act", I("activation", out=sbf[:], in_=pt[:], func=AF.Copy), reads=[B_p], writes=[B_sbf])
            mm(pRot[0][:], Rg, sbf[:], True, True, [B_RgT, B_sbf], [pRot[1]])
            t1, B_t1 = R32.get()
            op("dve", I("scalar_tensor_tensor", out=t1[:], in0=pt[:], scalar=gcol, in1=TB[:, 0, :], op0=ALU.mult, op1=ALU.mult), reads=[B_p, B_TB, B_qkg], writes=[B_t1])
            rs, B_rs = R32.get()
            op("act", I("activation", out=rs[:], in_=pStat[0][:], func=AF.Sqrt, bias=epscol, scale=1.0 / 64), reads=[pStat[1], B_cc], writes=[B_rs])
            t2, B_t2 = R32.get()
            op("dve", I("tensor_tensor", out=t2[:], in0=pRot[0][:], in1=TB[:, 1, :], op=ALU.mult), reads=[pRot[1], B_TB], writes=[B_t2])
            op("dve", I("reciprocal", out=rs[:], in_=rs[:]), reads=[B_rs], writes=[B_rs])
            op("pool", I("tensor_tensor", out=t1[:], in0=t1[:], in1=t2[:], op=ALU.add), reads=[B_t1, B_t2], writes=[B_t1])
            if isinstance(dst, list):
                for (d_ap, psl) in dst:
                    op("dve", I("tensor_tensor", out=d_ap, in0=t1[psl, :], in1=rs[psl, :], op=ALU.mult), reads=[B_t1, B_rs], writes=[B_dst])
            else:
                op("dve", I("tensor_tensor", out=dst, in0=t1[:], in1=rs[:], op=ALU.mult), reads=[B_t1, B_rs], writes=[B_dst])

        def chunk_pipeline(n, proj_fn, fin_fn):
            cur = proj_fn(0)
            for c in range(n):
                nxt = proj_fn(c + 1) if c + 1 < n else None
                fin_fn(c, cur)
                cur = nxt

        def ktranspose(src_of_h, B_src, n, colsel):
            for h in range(4):
                op("pe", I("transpose", psT[0][:, h * 128:(h + 1) * 128], src_of_h(h), ident), reads=[B_src, B_cm], writes=[psT[1]])
            for h in range(4):
                op("act", I("activation", out=KD[:, n, h * 128:(h + 1) * 128], in_=psT[0][:, h * 128:(h + 1) * 128], func=AF.Copy, scale=kcol[:, colsel + h:colsel + h + 1]), reads=[psT[1], B_kcol], writes=[B_KD])

        import os
        KSTOP = int(os.environ.get("KSTOP", "0"))

        class _Stop(Exception):
            pass
        try:
          for l in range(depth):
              dma("sp", I("dma_start", out=bada[:], in_=b_adaT[l]), writes=[B_bada])
              dma("sp", I("dma_start", out=ngT[:], in_=norm_gT[l]), writes=[B_ngT])
              dma("sp", I("dma_start", out=rgT[:], in_=retgT[l]), writes=[B_rgT])
              for fc in range(24):
                  if True:
                      dma("sp", I("dma_start", out=wada[:], in_=w_ada[l].rearrange("(kc p) n -> p kc n", p=128)[:, :, fc * 128:(fc + 1) * 128]), writes=[B_wada])
                      pt, B_p = pproj()
                      for kc in range(8):
                          mm(pt[:, 0:nseq], wada[:, kc, :], csil[:, kc, :], kc == 0, kc == 7, [B_wada, B_csil], [B_p])
                      op("dve", I("tensor_scalar", out=modT[:, fc, :], in0=pt[:, 0:nseq], scalar1=bada[:, fc:fc + 1], scalar2=None, op0=ALU.add), reads=[B_p, B_bada], writes=[B_mod])
              for c in range(8):
                  op("dve", I("tensor_scalar", out=gmod[:, c, :], in0=modT[:, 8 + c, :], scalar1=1.0, scalar2=ngT[:, c:c + 1], op0=ALU.add, op1=ALU.mult), reads=[B_mod, B_ngT], writes=[B_gmod])
              lo = 8 * l
              for h in range(4):
                  op("dve", I("tensor_scalar", out=tmpd[:], in0=cr[:, 0, :], scalar1=lg[:, lo + h:lo + h + 1], scalar2=None, op0=ALU.mult), reads=[B_cr, B_lg], writes=[B_tmpd])
                  op("dve", I("scalar_tensor_tensor", out=tmpd2[:], in0=cr[:, 1, :], scalar=lg[:, lo + 4 + h:lo + 5 + h], in1=tmpd[:], op0=ALU.mult, op1=ALU.add), reads=[B_cr, B_lg, B_tmpd], writes=[B_tmpd2])
                  op("act", I("activation", out=tmpd2[:], in_=tmpd2[:], func=AF.Exp), reads=[B_tmpd2], writes=[B_tmpd2])
                  op("dve", I("tensor_tensor", out=MT[:, h, :], in0=tmpd2[:], in1=cr[:, 2, :], op=ALU.mult), reads=[B_tmpd2, B_cr], writes=[B_MT])
                  op("act", I("activation", out=qdec[:, h, :], in_=cr[:, 3, :], func=AF.Exp, scale=lg[:, lo + h:lo + h + 1]), reads=[B_cr, B_lg], writes=[B_qdec])
                  op("act", I("activation", out=qdec[:, 4 + h, :], in_=cr[:, 4, :], func=AF.Exp, scale=lg[:, lo + 4 + h:lo + 5 + h]), reads=[B_cr, B_lg], writes=[B_qdec])
                  op("act", I("activation", out=kcol[:, h:h + 1], in_=cc_[:, 0:1], func=AF.Exp, scale=lg[:, lo + h:lo + h + 1]), reads=[B_cc, B_lg], writes=[B_kcol])
                  op("act", I("activation", out=kcol[:, 4 + h:5 + h], in_=cc_[:, 1:2], func=AF.Exp, scale=lg[:, lo + 4 + h:lo + 5 + h]), reads=[B_cc, B_lg], writes=[B_kcol])
              op("act", I("activation", out=gC[:], in_=lg[:, lo:lo + 8], func=AF.Exp, scale=128.0), reads=[B_lg], writes=[B_gC])
              if KSTOP == 1:
                  raise _Stop()
              for j_ in range(2):
                  op("dve", I("tensor_scalar", out=RgT[:, j_, :], in0=RaT, scalar1=qkg_s[:, 2 * l + j_:2 * l + j_ + 1], scalar2=None, op0=ALU.mult), reads=[B_cm, B_qkg], writes=[B_RgT])
              qg = qkg_s[:, 2 * l:2 * l + 1]
              kg = qkg_s[:, 2 * l + 1:2 * l + 2]

              for si, S in enumerate(seq_lens):
                  nb = S // BLK
                  xsrc = xin[si] if l == 0 else XS[si]
                  B_xsrc = None if l == 0 else B_XS[si]
                  xdst = yout[si] if l == depth - 1 else XS[si]
                  B_xdst = None if l == depth - 1 else B_XS[si]
                  tab = tabs[S]
                  rd_x = [B_xsrc] if B_xsrc is not None else []

                  op("dve", I("memset", Sb[:], 0.0), writes=B_Sfh)
                  for b in reversed(range(nb)):
                      t0 = b * BLK
                      dma("sp", I("dma_start", out=TB[:], in_=tab.rearrange("k p t -> p k t")[:, :, t0:t0 + BLK]), writes=[B_TB])
                      for c in range(8):
                          sq, B_sq = R16.get()
                          xc, B_xc = XC.get()
                          dma("sp", I("dma_start", out=xc[:], in_=xsrc[c * 128:(c + 1) * 128, t0:t0 + BLK]), reads=rd_x, writes=[B_xc])
                          op("act", I("activation", out=sq[:], in_=xc[:], func=AF.Square), reads=[B_xc], writes=[B_sq])
                          mm(pStat[0][:], ones_full, sq[:], c == 0, c == 7, [B_cm, B_sq], [pStat[1]])
                      rsqrt_ps(rstd[:], B_rstd, pStat[0][:], pStat[1], 1.0 / D)
                      for c in range(8):
                          t1, B_t1 = R32.get()
                          xc, B_xc = XC.get()
                          dma("sp", I("dma_start", out=xc[:], in_=xsrc[c * 128:(c + 1) * 128, t0:t0 + BLK]), reads=rd_x, writes=[B_xc])
                          op("dve", I("scalar_tensor_tensor", out=t1[:], in0=xc[:], scalar=gmod[:, c, si:si + 1], in1=rstd[:], op0=ALU.mult, op1=ALU.mult), reads=[B_xc, B_gmod, B_rstd], writes=[B_t1])
                          op("act", I("activation", out=HT[:, c, :], in_=t1[:], func=AF.Identity, bias=modT[:, c, si:si + 1], scale=1.0), reads=[B_t1, B_mod], writes=[B_HT])
                      dma("sp", I("dma_start", out=HTd.rearrange("c p t -> p c t")[:, :, t0:t0 + BLK], in_=HT[:]), reads=[B_HT], writes=[B_HTd])
                      hrhs = lambda kc: HT[:, kc, :]
                      Wt, B_W = wnext("in", l, 512)
                      def ka_fin(cc, pp):
                          pt, B_p = pp
                          ks, B_ks = R16.get()
                          qknorm_rope(ks[:], B_ks, pt, B_p, kg, RgT[:, 1, :])
                          dma("sp", I("dma_start", out=KTd[cc][:, t0:t0 + BLK], in_=ks[:]), reads=[B_ks], writes=[B_KTd])
                      chunk_pipeline(4, lambda cc: proj_fm(Wt, B_W, cc * 128, 128, hrhs, B_HT), ka_fin)
                      if KSTOP == 5:
                          raise _Stop()
                      Wt, B_W = wnext("in", l, 1024)
                      for tt in range(4):
                          pt, B_p = pproj()
                          for kc in range(8):
                              mm(pt[:], HT[:, kc, tt * 128:(tt + 1) * 128], Wt[:, kc, :], kc == 0, kc == 7, [B_HT, B_W], [B_p])
                          vs, B_vs = VAst.get()
                          op("act", I("activation", out=vs[:].rearrange("p (h e) -> p h e", e=66)[:, :, 0:64], in_=pt[:].rearrange("p (h e) -> p h e", e=64), func=AF.Copy), reads=[B_p], writes=[B_vs])
                          dma("sp", I("dma_start", out=VAd[4 * b + tt], in_=vs[:]), reads=[B_vs], writes=[B_VAd])
                      if KSTOP == 7:
                          raise _Stop()
                      Wt, B_W = wnext("in", l, 2560)
                      chunk_pipeline(4, lambda h: proj_fm(Wt, B_W, h * 128, 128, hrhs, B_HT),
                                     lambda h, pp: rope(KRb[:, h, :], B_KRb, pp[0][:], pp[1], RrT, TB[:, 2, :], TB[:, 3, :], mul=cc_[:, 4:5]))
                      dma("sp", I("dma_start", out=KRd.rearrange("h p t -> p h t")[:, :, t0:t0 + BLK], in_=KRb[:]), reads=[B_KRb], writes=[B_KRd])
                      if KSTOP == 8:
                          raise _Stop()
                      for g in range(2):
                          Wt, B_W = wnext("in", l, 3072 + 512 * g)
                          for tt in range(4):
                              pt, B_p = pproj()
                              for kc in range(8):
                                  mm(pt[:], HT[:, kc, tt * 128:(tt + 1) * 128], Wt[:, kc, :], kc == 0, kc == 7, [B_HT, B_W], [B_p])
                              op("act", I("activation", out=VRb[:, tt, g * 512:(g + 1) * 512], in_=pt[:], func=AF.Copy), reads=[B_p], writes=[B_VRb])
                      dma("sp", I("dma_start", out=VRd[4 * b:4 * b + 4].rearrange("t p n -> p t n"), in_=VRb[:]), reads=[B_VRb], writes=[B_VRd])
                      if KSTOP == 6:
                          raise _Stop()
                      for n in reversed(range(4)):
                          ktranspose(lambda h, n=n: KRb[:, h, n * 128:(n + 1) * 128], B_KRb, n, 4)
                          stg, B_stg = SBst.get()
                          op("pool", I("tensor_copy", out=stg[:], in_=Sb[:].rearrange("p h n -> p (h n)")), reads=B_Sfh, writes=[B_stg])
                          dma("sp", I("dma_start", out=SBd[4 * b + n], in_=stg[:]), reads=[B_stg], writes=[B_SBd])
                          for h in range(4):
                              pt, B_p = pproj()
                              mm(pt[:, 0:256], KD[:, n, h * 128:(h + 1) * 128], VRb[:, n, h * 256:(h + 1) * 256], True, True, [B_KD, B_VRb], [B_p])
                              op("dve", I("scalar_tensor_tensor", out=Sb[:, h, :], in0=Sb[:, h, :], scalar=gC[:, 4 + h:5 + h], in1=pt[:, 0:256], op0=ALU.mult, op1=ALU.add), reads=[B_p, B_gC, B_Sfh[h]], writes=[B_Sfh[h]])

                  if KSTOP == 2:
                      raise _Stop()
                  op("dve", I("memset", Sf[:], 0.0), writes=B_Sfh)
                  ntile = S // 128
                  def block_loads(bb):
                      tt0 = bb * BLK
                      dma("sp", I("dma_start", out=HT[:], in_=HTd.rearrange("c p t -> p c t")[:, :, tt0:tt0 + BLK]), reads=[B_HTd], writes=[B_HT])
                      dma("sp", I("dma_start", out=TB[:], in_=tab.rearrange("k p t -> p k t")[:, :, tt0:tt0 + BLK]), writes=[B_TB])
                      dma("sp", I("dma_start", out=VRb[:], in_=VRd[4 * bb:4 * bb + 4].rearrange("t p n -> p t n")), reads=[B_VRd], writes=[B_VRb])
                      dma("sp", I("dma_start", out=KRb[:], in_=KRd.rearrange("h p t -> p h t")[:, :, tt0:tt0 + BLK]), reads=[B_KRd], writes=[B_KRb])
                  block_loads(0)
                  for b in range(nb):
                      t0 = b * BLK
                      ulo = max(0, 4 * b - 8)
                      uhi = min(ntile, 4 * b + 12)
                      hrhs = lambda kc: HT[:, kc, :]
                      Wt, B_W = wnext("in", l, 0)
                      chunk_pipeline(4, lambda cc: proj_fm(Wt, B_W, cc * 128, 128, hrhs, B_HT),
                                     lambda cc, pp: qknorm_rope([(QT[0:64, 2 * cc, :], slice(0, 64)), (QT[64:128, 2 * cc + 1, :], slice(64, 128))],
                                                               B_QT, pp[0], pp[1], qg, RgT[:, 0, :]))
                      Wz, B_Wz = wnext("in", l, 1536)
                      po_alt = [pO, pStat]

                      def load_pair(cc):
                          ktw, B_ktw = KTw.get()
                          VA, B_VA = VAr.get()
                          dma("sp", I("dma_start", out=ktw[:, 0:(uhi - ulo) * 128], in_=KTd[cc][:, ulo * 128:uhi * 128]), reads=[B_KTd], writes=[B_ktw])
                          dma("sp", I("dma_start", out=VA[:, 0:uhi - ulo, :], in_=VAd[ulo:uhi].rearrange("t p n -> p t n")[:, :, cc * 132:(cc + 1) * 132]), reads=[B_VAd], writes=[B_VA])
                          return ktw, B_ktw, VA, B_VA

                      def normalise_steps(h, pOh, B_pOh, sz, B_sz):
                          rd, B_rd = R32.get()
                          op("dve", I("reciprocal", out=rd[64:65, :], in_=pOh[64:65, :]), reads=[B_pOh], writes=[B_rd])
                          rh, B_rh = R16.get()
                          rl, B_rl = R16.get()
                          op("dve", I("tensor_copy", out=rh[64:65, :], in_=rd[64:65, :]), reads=[B_rd], writes=[B_rh])
                          op("dve", I("tensor_tensor", out=rl[64:65, :], in0=rd[64:65, :], in1=rh[64:65, :], op=ALU.subtract), reads=[B_rd, B_rh], writes=[B_rl])
                          yield
                          yield
                          mm(pRot[0][0:64, :], ones_full[64:65, 0:64], rh[64:65, :], True, False, [B_cm, B_rh], [pRot[1]])
                          mm(pRot[0][0:64, :], ones_full[64:65, 0:64], rl[64:65, :], False, True, [B_cm, B_rl], [pRot[1]])
                          yield
                          yield
                          bc, B_bc = R32.get()
                          op("act", I("activation", out=bc[0:64, :], in_=pRot[0][0:64, :], func=AF.Copy), reads=[pRot[1]], writes=[B_bc])
                          yield
                          yield
                          ya, B_ya = R32.get()
                          op("dve", I("tensor_tensor", out=ya[0:64, :], in0=pOh[0:64, :], in1=bc[0:64, :], op=ALU.mult), reads=[B_pOh, B_bc], writes=[B_ya])
                          yield
                          op("pool", I("tensor_tensor", out=YAG[:, h, :], in0=ya[0:64, :], in1=sz[0:64, :], op=ALU.mult), reads=[B_ya, B_sz], writes=[B_YAG])
                          yield

                      def run_all(gen):
                          for _ in gen:
                              pass
                      tiles = list(range(ulo, uhi))
                      nt = len(tiles)
                      items = [(h, i) for h in range(8) for i in range(nt)]
                      pairs = {0: load_pair(0)}
                      sbufs = {}
                      LA = 3
                      sc3 = [pSc0, pSc1, PS[1]]
                      sc3i = {"i": 0}

                      def qk(k):
                          h, i = items[k]
                          cc, hp = h // 2, h % 2
                          if cc not in pairs:
                              pairs[cc] = load_pair(cc)
                          ktw, B_ktw, VA, B_VA = pairs[cc]
                          rows = slice(hp * 64, hp * 64 + 64)
                          u = tiles[i]
                          ps_s, B_s = sc3[sc3i["i"] % 3]
                          sc3i["i"] += 1
                          mm(ps_s[:], ktw[:, (u - ulo) * 128:(u - ulo + 1) * 128], QT[:, h, :], True, True, [B_ktw, B_QT], [B_s])
                          sbufs[k] = (ps_s, B_s)
                      for k in range(LA):
                          qk(k)
                      pending = None
                      cur = None
                      ngen = None
                      zpend = None
                      for k, (h, i) in enumerate(items):
                          cc, hp = h // 2, h % 2
                          u = tiles[i]
                          if i == 0:
                              if hp == 0 and cc + 1 < 4 and (cc + 1) not in pairs:
                                  pairs[cc + 1] = load_pair(cc + 1)
                              pOh, B_pOh = po_alt[h % 2]
                              pt, B_p = PS[0]
                              for kc in range(8):
                                  mm(pt[0:64, :], Wz[:, kc, h * 64:h * 64 + 64], HT[:, kc, :], kc == 0, kc == 7, [B_Wz, B_HT], [B_p])
                              sz, B_sz = SZ.get()
                              cur = (h, pOh, B_pOh, sz, B_sz)
                              zpend = (pt, B_p, sz, B_sz)
                          if i == 2:
                              op("act", I("activation", out=zpend[2][0:64, :], in_=zpend[0][0:64, :], func=AF.Silu), reads=[zpend[1]], writes=[zpend[3]])
                          ktw, B_ktw, VA, B_VA = pairs[cc]
                          ps_s, B_s = sbufs.pop(k)
                          pr, B_pr = PR.get()
                          op("act", I("activation", out=pr[:], in_=ps_s[:], func=AF.Exp, scale=0.125), reads=[B_s], writes=[B_pr])
                          rel = u - 4 * b + 8
                          op("dve", I("tensor_tensor", out=pr[:], in0=pr[:], in1=masks[:, 2432 - 128 * rel:2944 - 128 * rel], op=ALU.mult), reads=[B_pr, B_masks], writes=[B_pr])
                          if k + LA < len(items):
                              qk(k + LA)
                          vo_ = (u - ulo) * 132 + hp * 66
                          mm(cur[1][:, :], VA[:].rearrange("p t n -> p (t n)")[:, vo_:vo_ + 128], pr[:], i == 0, i == nt - 1, [B_VA, B_pr], [cur[2]])
                          if i == 1 and pending is not None:
                              ngen = normalise_steps(*pending)
                              pending = None
                          if ngen is not None and i >= 1:
                              if next(ngen, "end") == "end":
                                  ngen = None
                          if i == nt - 1:
                              if ngen is not None:
                                  run_all(ngen)
                                  ngen = None
                              pending = cur
                      run_all(normalise_steps(*pending))
                      if KSTOP == 3:
                          raise _Stop()
                      Wt, B_W = wnext("in", l, 2048)
                      chunk_pipeline(4, lambda h: proj_fm(Wt, B_W, h * 128, 128, hrhs, B_HT),
                                     lambda h, pp: rope(QR[:, h, :], B_QR, pp[0][:], pp[1], RrT, TB[:, 2, :], TB[:, 3, :]))
                      Wzr = [wnext("in", l, 4096), wnext("in", l, 4608)]
                      for n in range(4):
                          ktranspose(lambda h, n=n: KRb[:, h, n * 128:(n + 1) * 128], B_KRb, n, 0)
                      def stateA(h):
                          SFl, B_SFl = SFlR.get()
                          for n in range(4):
                              op("act", I("activation", out=SFl[:, n, :], in_=Sf[:, h, :], func=AF.Copy), reads=[B_Sfh[h]], writes=[B_SFl])
                              pd, B_pd = pproj()
                              mm(pd[:, 0:256], KD[:, n, h * 128:(h + 1) * 128], VRb[:, n, h * 256:(h + 1) * 256], True, True, [B_KD, B_VRb], [B_pd])
                              op("dve", I("scalar_tensor_tensor", out=Sf[:, h, :], in0=Sf[:, h, :], scalar=gC[:, h:h + 1], in1=pd[:, 0:256], op0=ALU.mult, op1=ALU.add), reads=[B_pd, B_gC, B_Sfh[h]], writes=[B_Sfh[h]])
                          return SFl, B_SFl
                      sf_next = stateA(0)
                      for h in range(4):
                          SFl, B_SFl = sf_next
                          if h + 1 < 4:
                              sf_next = stateA(h + 1)
                          po = [pscore(), pscore()]
                          SBl, B_SBl = SBlR.get()
                          dma("sp", I("dma_start", out=SBl[:], in_=SBd[4 * b:4 * b + 4].rearrange("t p n -> p t n")[:, :, h * 256:(h + 1) * 256]), reads=[B_SBd], writes=[B_SBl])
                          scd = {}

                          def scores(n):
                              cs = slice(n * 128, (n + 1) * 128)
                              ps_s, B_s = pproj()
                              mm(ps_s[:, 0:128], KRb[:, h, cs], QR[:, h, cs], True, True, [B_KRb, B_QR], [B_s])
                              smt, B_sm = SM.get()
                              op("dve", I("tensor_tensor", out=smt[:], in0=ps_s[:, 0:128], in1=MT[:, h, :], op=ALU.mult), reads=[B_s, B_MT], writes=[B_sm])
                              qf, B_qf = QF.get()
                              op("pool", I("tensor_tensor", out=qf[:, 0:128], in0=QR[:, h, cs], in1=qdec[:, h, :], op=ALU.mult), reads=[B_QR, B_qdec], writes=[B_qf])
                              op("pool", I("tensor_tensor", out=qf[:, 128:256], in0=QR[:, h, cs], in1=qdec[:, 4 + h, :], op=ALU.mult), reads=[B_QR, B_qdec], writes=[B_qf])
                              scd[n] = (smt, B_sm, qf, B_qf)
                          scores(0)
                          for n in range(4):
                              cs = slice(n * 128, (n + 1) * 128)
                              if n + 1 < 4:
                                  scores(n + 1)
                              smt, B_sm, qf, B_qf = scd.pop(n)
                              for ev in range(2):
                                  vs_ = slice(h * 256 + ev * 128, h * 256 + ev * 128 + 128)
                                  pt_o, B_o = po[ev]
                                  mm(pt_o[:, cs], VRb[:, n, vs_], smt[:], True, False, [B_VRb, B_sm], [B_o])
                                  mm(pt_o[:, cs], SFl[:, n, ev * 128:(ev + 1) * 128], qf[:, 0:128], False, False, [B_SFl, B_qf], [B_o])
                                  mm(pt_o[:, cs], SBl[:, n, ev * 128:(ev + 1) * 128], qf[:, 128:256], False, True, [B_SBl, B_qf], [B_o])
                          yb = [R16.get(), R16.get()]
                          ysq = [R16.get(), R16.get()]
                          for ev in range(2):
                              op("act", I("activation", out=yb[ev][0][:], in_=po[ev][0][:], func=AF.Copy), reads=[po[ev][1]], writes=[yb[ev][1]])
                              op("act", I("activation", out=ysq[ev][0][:], in_=po[ev][0][:], func=AF.Square), reads=[po[ev][1]], writes=[ysq[ev][1]])
                          for ev in range(2):
                              mm(pStat[0][:], ones_full, yb[ev][0][:], ev == 0, ev == 1, [B_cm, yb[ev][1]], [pStat[1]])
                          for ev in range(2):
                              mm(pRot[0][:], ones_full, ysq[ev][0][:], ev == 0, ev == 1, [B_cm, ysq[ev][1]], [pRot[1]])
                          mean, B_mean = R32.get()
                          op("dve", I("tensor_scalar", out=mean[:], in0=pStat[0][:], scalar1=1.0 / 256, scalar2=None, op0=ALU.mult), reads=[pStat[1]], writes=[B_mean])
                          msq, B_msq = R32.get()
                          op("pool", I("tensor_tensor", out=msq[:], in0=mean[:], in1=mean[:], op=ALU.mult), reads=[B_mean], writes=[B_msq])
                          var, B_var = R32.get()
                          op("dve", I("scalar_tensor_tensor", out=var[:], in0=pRot[0][:], scalar=1.0 / 256, in1=msq[:], op0=ALU.mult, op1=ALU.subtract), reads=[pRot[1], B_msq], writes=[B_var])
                          op("act", I("activation", out=var[:], in_=var[:], func=AF.Sqrt, bias=epscol, scale=1.0), reads=[B_var, B_cc], writes=[B_var])
                          op("dve", I("reciprocal", out=var[:], in_=var[:]), reads=[B_var], writes=[B_var])
                          for ev in range(2):
                              c8 = 2 * h + ev
                              Wzt, B_Wzt = Wzr[c8 // 4]
                              pt, B_p = proj_fm(Wzt, B_Wzt, (c8 % 4) * 128, 128, hrhs, B_HT)
                              sz, B_sz = R32.get()
                              op("act", I("activation", out=sz[:], in_=pt[:], func=AF.Silu), reads=[B_p], writes=[B_sz])
                              d1, B_d1 = R32.get()
                              op("dve", I("tensor_tensor", out=d1[:], in0=po[ev][0][:], in1=mean[:], op=ALU.subtract), reads=[po[ev][1], B_mean], writes=[B_d1])
                              op("dve", I("scalar_tensor_tensor", out=d1[:], in0=d1[:], scalar=rgT[:, c8:c8 + 1], in1=var[:], op0=ALU.mult, op1=ALU.mult), reads=[B_d1, B_var, B_rgT], writes=[B_d1])
                              op("pool", I("tensor_tensor", out=YRG[:, c8, :], in0=d1[:], in1=sz[:], op=ALU.mult), reads=[B_d1, B_sz], writes=[B_YRG])
                      if KSTOP == 4:
                          raise _Stop()
                      xcs = {}

                      def xload(o_):
                          xc_, B_xc_ = XC.get()
                          dma("sp", I("dma_start", out=xc_[:], in_=xsrc[o_ * 128:(o_ + 1) * 128, t0:t0 + BLK]), reads=rd_x, writes=[B_xc_])
                          xcs[o_] = (xc_, B_xc_)
                      for o_ in range(1):
                          xload(o_)
                      for oc in range(8):
                          Wm, B_Wm = wnext("mg", l, oc)
                          pa_, B_pa = proj_fm(Wm, B_Wm, 256, 128, lambda hh: YAG[:, hh, :], B_YAG, nk=8, prow=64)
                          pga, B_pga = pStat
                          for kc in range(8):
                              mm(pga[:], Wm[:, kc, 0:128], HT[:, kc, :], kc == 0, kc == 7, [B_Wm, B_HT], [B_pga])
                          sa, B_sa = R32.get()
                          op("act", I("activation", out=sa[:], in_=pga[:], func=AF.Sigmoid), reads=[B_pga], writes=[B_sa])
                          m1, B_m1 = R32.get()
                          op("dve", I("tensor_tensor", out=m1[:], in0=pa_[:], in1=sa[:], op=ALU.mult), reads=[B_pa, B_sa], writes=[B_m1])
                          pb_, B_pb = proj_fm(Wm, B_Wm, 384, 128, lambda c: YRG[:, c, :], B_YRG)
                          pgr, B_pgr = pRot
                          for kc in range(8):
                              mm(pgr[:], Wm[:, kc, 128:256], HT[:, kc, :], kc == 0, kc == 7, [B_Wm, B_HT], [B_pgr])
                          sr, B_sr = R32.get()
                          op("act", I("activation", out=sr[:], in_=pgr[:], func=AF.Sigmoid), reads=[B_pgr], writes=[B_sr])
                          m2, B_m2 = R32.get()
                          op("dve", I("tensor_tensor", out=m2[:], in0=pb_[:], in1=sr[:], op=ALU.mult), reads=[B_pb, B_sr], writes=[B_m2])
                          op("pool", I("tensor_tensor", out=MG[:, oc, :], in0=m1[:], in1=m2[:], op=ALU.add), reads=[B_m1, B_m2], writes=[B_MG])
                      if b + 1 < nb:
                          block_loads(b + 1)
                      for q in range(2):
                          Wo, B_Wo = wnext("wo", l, 512 * q)
                          for j in range(4):
                              oc = 4 * q + j
                              if oc + 1 < 8:
                                  xload(oc + 1)
                              xc, B_xc = xcs.pop(oc)
                              po_, B_po = proj_fm(Wo, B_Wo, j * 128, 128, lambda c: MG[:, c, :], B_MG)
                              xo, B_xo = XO.get()
                              op("dve", I("scalar_tensor_tensor", out=xo[:], in0=po_[:], scalar=modT[:, 16 + oc, si:si + 1], in1=xc[:], op0=ALU.mult, op1=ALU.add), reads=[B_po, B_xc, B_mod], writes=[B_xo])
                              wr_x = [B_xdst] if B_xdst is not None else []
                              dma("act", I("dma_start", out=xdst[oc * 128:(oc + 1) * 128, t0:t0 + BLK], in_=xo[:]), reads=[B_xo], writes=wr_x, sbuf=B_xo)
        except _Stop:
            pass
        if not KSTOP:
            assert wstate["i"] == len(worder)
        print('sbuf_left', nc.sbuf_bytes_remaining, 'nsem', S_.nsem, {e: S_.cnt[e] for e in S_.names})
        S_.finish()
    return nc


def _tables(S):
    pos = np.arange(S, dtype=np.float32)
    p = np.arange(128)
    tab = np.zeros((4, 128, S), np.float32)
    fa = np.exp(np.float32(-math.log(500000.0)) * np.arange(8, dtype=np.float32) / np.float32(8)).astype(np.float32)
    e = p % 64
    tab[0] = 1.0
    for pp in range(128):
        ee = e[pp]
        if ee < 16:
            ang = (pos * fa[ee % 8]).astype(np.float32).astype(np.float64)
            tab[0, pp] = np.cos(ang)
            tab[1, pp] = (-np.sin(ang)) if ee < 8 else np.sin(ang)
    fr = np.exp(np.float32(-math.log(10000.0)) * np.arange(64, dtype=np.float32) / np.float32(64)).astype(np.float32)
    for pp in range(128):
        ang = (pos * fr[pp % 64]).astype(np.float32).astype(np.float64)
        tab[2, pp] = np.cos(ang)
        tab[3, pp] = (-np.sin(ang)) if pp < 64 else np.sin(ang)
    return tab


def _consts():
    p = np.arange(128)
    cmat = np.zeros((6, 128, 128), np.float32)
    cmat[0] = np.eye(128)
    cmat[1] = (p[:, None] // 64 == p[None, :] // 64)
    cmat[2] = 1.0
    for m in range(128):
        e = m % 64
        if e < 8:
            cmat[3, m + 8, m] = 1.0
        elif e < 16:
            cmat[3, m - 8, m] = 1.0
        cmat[4, (m + 64) % 128, m] = 1.0
    d = p[:, None] - np.arange(2944)[None, :] + 1408
    ad = np.abs(d)
    cmask = ((ad <= 64).astype(np.float32) + ((d % 4 == 0) & (ad <= 256)) + ((d % 16 == 0) & (ad <= 1024))).astype(np.float32)
    cret = np.zeros((5, 128, 128), np.float32)
    dd = (np.arange(128)[None, :] - p[:, None]).astype(np.float32)
    cret[0] = np.maximum(dd, 0)
    cret[1] = np.maximum(-dd, 0)
    cret[2] = 1.0 + (dd == 0)
    cret[3] = np.arange(128)[None, :] + 1.0
    cret[4] = 128.0 - np.arange(128)[None, :]
    ccol = np.zeros((128, 8), np.float32)
    ccol[:, 4] = 128.0 ** -0.5
    ccol[:, 0] = 127 - p
    ccol[:, 1] = p
    ccol[:, 2] = 1.0
    ccol[:, 3] = EPS
    return cmat, cmask, cret, ccol


_CACHE = {}


def kernel(x_prompt, x_sample, c_prompt, c_sample, norm_g, w_ada, b_ada, w_in, q_norm_g, k_norm_g,
           ret_decay_logit, ret_norm_g, w_proj_a, w_proj_b, w_out):
    n_cores = 8
    x_prompt = np.asarray(x_prompt, np.float32)
    x_sample = np.asarray(x_sample, np.float32)
    depth = int(np.asarray(norm_g).shape[0])
    seq_lens = [x_prompt.shape[1]] * NP + [x_sample.shape[1]]
    key = (tuple(seq_lens), depth)
    if key not in _CACHE:
        _CACHE[key] = build_nc(seq_lens, depth)
    nc = _CACHE[key]
    cmat, cmask, cret, ccol = _consts()
    tabs = {S: _tables(S) for S in sorted(set(seq_lens))}
    f = lambda a: np.ascontiguousarray(np.asarray(a, np.float32))

    def colT(a, n):
        a = f(a)
        return np.ascontiguousarray(a.reshape(a.shape[0], n, 128).transpose(0, 2, 1))
    qkg = np.zeros((128, 2 * depth), np.float32)
    for l in range(depth):
        qkg[:, 2 * l] = np.tile(f(q_norm_g)[l], 2)
        qkg[:, 2 * l + 1] = np.tile(f(k_norm_g)[l], 2)
    dlog = np.ascontiguousarray(np.broadcast_to(f(ret_decay_logit).reshape(1, -1), (128, 8 * depth)))
    xsT = np.ascontiguousarray(x_sample[0].T)
    common = {
        "w_ada": f(w_ada), "b_adaT": colT(b_ada, 24), "norm_gT": colT(norm_g, 8), "w_in": f(w_in),
        "qkg": qkg, "dlog": dlog, "retgT": colT(ret_norm_g, 8), "w_pa": f(w_proj_a), "w_pb": f(w_proj_b),
        "w_o": f(w_out), "cmat": cmat, "cmask": cmask, "cret": cret, "ccol": ccol,
    }
    for S, t in tabs.items():
        common["tab%d" % S] = t
    in_maps = []
    for c in range(n_cores):
        m = dict(common)
        cs = np.concatenate([f(c_prompt)[c * NP:(c + 1) * NP], f(c_sample)], axis=0)
        m["cT"] = np.ascontiguousarray(cs.reshape(NP + 1, 8, 128).transpose(2, 1, 0))
        for i in range(NP):
            m["x%d" % i] = np.ascontiguousarray(x_prompt[c * NP + i].T)
        m["x%d" % NP] = xsT
        in_maps.append(m)
    res = run_bass_kernel_spmd(nc, in_maps, core_ids=list(range(n_cores)))
    y_prompt = np.empty_like(x_prompt)
    for c in range(n_cores):
        for i in range(NP):
            y_prompt[c * NP + i] = res.results[c]["y%d" % i].T
    y_sample = np.ascontiguousarray(res.results[0]["y%d" % NP].T)[None]
    return (y_prompt, y_sample.astype(np.float32))
```
